# Optimizing a Trainium2 kernel written in Bass

```python
import jax, jax.numpy as jnp
from jax import lax
import numpy as np

D_MODEL = 1024
BATCH = 8
SEQ = 4096
DEPTH = 1

HG_HEADS = 8
HG_DK = 128
HG_DV = D_MODEL // HG_HEADS
HG_CHUNK = 64
RET_HEADS = 8
RET_DV = D_MODEL // RET_HEADS
RET_DK = RET_DV // 2
RET_CHUNK = 128
ROPE_BASE = 10000.0
EPS = 1e-6

HG_Q = HG_HEADS * HG_DK
HG_V = HG_HEADS * HG_DV
RET_Q = RET_HEADS * RET_DK
RET_V = RET_HEADS * RET_DV
SPLITS = [HG_Q, HG_Q, HG_V, HG_V, RET_Q, RET_Q, RET_V, RET_V, D_MODEL, D_MODEL]
D_IN = sum(SPLITS)

kernel_name = "hybrid_hgrn2_retention_gated_block"


def rms_norm(x, g):
    xf = x.astype(jnp.float32)
    y = xf * lax.rsqrt(jnp.mean(xf * xf, axis=-1, keepdims=True) + EPS)
    return (y * g.astype(jnp.float32)).astype(x.dtype)


def head_rms_norm(o, g):
    H, d = o.shape[-2], o.shape[-1]
    y = o * lax.rsqrt(jnp.mean(o * o, axis=-1, keepdims=True) + EPS)
    y = y * g.astype(jnp.float32).reshape(H, d)
    return y.reshape(o.shape[0], o.shape[1], H * d)


def rotary(t, pos):
    dk = t.shape[-1]
    inv_freq = 1.0 / (ROPE_BASE ** jnp.linspace(0.0, 1.0, dk // 2, dtype=jnp.float32))
    ang = pos.astype(jnp.float32)[:, None] * inv_freq[None, :]
    cos = jnp.cos(ang)[None, :, None, :]
    sin = jnp.sin(ang)[None, :, None, :]
    t1, t2 = t[..., : dk // 2], t[..., dk // 2:]
    return jnp.concatenate([t1 * cos - t2 * sin, t1 * sin + t2 * cos], axis=-1)


def hgrn2_chunkwise(q, fpre, v, lb):
    B, L, H, dk = q.shape
    dv = v.shape[-1]
    C = HG_CHUNK
    N = L // C
    q = jax.nn.silu(q)
    lbh = lb.astype(jnp.float32).reshape(H, dk)
    f = lbh + (1.0 - lbh) * jax.nn.sigmoid(fpre)
    k = 1.0 - f
    logf = jnp.log(f)

    def to_chunks(t):
        return t.reshape(B, N, C, H, t.shape[-1]).transpose(1, 0, 3, 2, 4)

    mask = jnp.tril(jnp.ones((C, C), dtype=bool))[:, :, None]

    def step(S, inp):
        qc, kc, lfc, vc = inp
        b = jnp.cumsum(lfc, axis=2)
        inter = jnp.einsum('bhtd,bhde->bhte', qc * jnp.exp(b), S)
        diff = b[:, :, :, None, :] - b[:, :, None, :, :]
        decay = jnp.exp(jnp.where(mask, diff, -jnp.inf))
        A = jnp.einsum('bhtsd,bhsd->bhts', qc[:, :, :, None, :] * decay, kc)
        intra = jnp.einsum('bhts,bhse->bhte', A, vc)
        b_last = b[:, :, -1:, :]
        S_new = jnp.exp(b_last[:, :, 0, :])[..., None] * S + jnp.einsum(
            'bhsd,bhse->bhde', kc * jnp.exp(b_last - b), vc)
        return S_new, inter + intra

    S0 = jnp.zeros((B, H, dk, dv), jnp.float32)
    _, o = lax.scan(step, S0, (to_chunks(q), to_chunks(k), to_chunks(logf), to_chunks(v)))
    return o.transpose(1, 0, 3, 2, 4).reshape(B, L, H, dv)


def retention_chunkwise(q, k, v):
    B, L, H, dk = q.shape
    dv = v.shape[-1]
    C = RET_CHUNK
    N = L // C
    log_gamma = jnp.log(1.0 - jnp.exp2(-5.0 - jnp.arange(H, dtype=jnp.float32)))
    k = k * (dk ** -0.5)
    qc = q.reshape(B, N, C, H, dk).transpose(0, 3, 1, 2, 4)
    kc = k.reshape(B, N, C, H, dk).transpose(0, 3, 1, 2, 4)
    vc = v.reshape(B, N, C, H, dv).transpose(0, 3, 1, 2, 4)
    idx = jnp.arange(C, dtype=jnp.float32)
    rel = idx[:, None] - idx[None, :]
    Dm = jnp.where(rel >= 0, jnp.exp(log_gamma[:, None, None] * jnp.maximum(rel, 0.0)), 0.0)
    scores = jnp.einsum('bhnid,bhnjd->bhnij', qc, kc) * Dm[:, None]
    intra = jnp.einsum('bhnij,bhnje->bhnie', scores, vc)
    zeta = jnp.exp(log_gamma[:, None] * (C - 1.0 - idx))
    chunk_state = jnp.einsum('bhnjd,bhnje->bhnde', kc * zeta[:, None, :, None], vc)
    chunk_decay = jnp.exp(log_gamma * C)[None, :, None, None]

    def step(R, s):
        return chunk_decay * R + s, R

    R0 = jnp.zeros((B, H, dk, dv), jnp.float32)
    _, R_prev = lax.scan(step, R0, chunk_state.transpose(2, 0, 1, 3, 4))
    R_prev = R_prev.transpose(1, 2, 0, 3, 4)
    xi = jnp.exp(log_gamma[:, None] * (idx + 1.0))
    cross = jnp.einsum('bhnid,bhnde->bhnie', qc * xi[:, None, :, None], R_prev)
    return (intra + cross).transpose(0, 2, 3, 1, 4).reshape(B, L, H, dv)


def setup_inputs(seed: int = 0) -> dict:
    key = jax.random.key(seed)
    ks = jax.random.split(key, 12)
    f32 = jnp.float32
    x = jax.random.normal(ks[0], (BATCH, SEQ, D_MODEL), f32)
    c = jax.random.normal(ks[1], (BATCH, D_MODEL), f32)
    norm_g = 1.0 + 0.05 * jax.random.normal(ks[2], (DEPTH, D_MODEL), f32)
    w_ada = 0.5 * D_MODEL ** -0.5 * jax.random.normal(ks[3], (DEPTH, D_MODEL, 3 * D_MODEL), f32)
    b_ada = 0.02 * jax.random.normal(ks[4], (DEPTH, 3 * D_MODEL), f32)
    w_in = D_MODEL ** -0.5 * jax.random.normal(ks[5], (DEPTH, D_MODEL, D_IN), f32)
    hg_lb_logits = 0.5 * jax.random.normal(ks[6], (DEPTH + 1, HG_Q), f32)
    hg_norm_g = 1.0 + 0.05 * jax.random.normal(ks[7], (DEPTH, HG_V), f32)
    ret_norm_g = 1.0 + 0.05 * jax.random.normal(ks[8], (DEPTH, RET_V), f32)
    w_out = D_MODEL ** -0.5 * jax.random.normal(ks[9], (DEPTH, D_MODEL, D_MODEL), f32)
    final_g = 1.0 + 0.05 * jax.random.normal(ks[10], (D_MODEL,), f32)
    return {"x": x, "c": c, "norm_g": norm_g, "w_ada": w_ada, "b_ada": b_ada,
            "w_in": w_in, "hg_lb_logits": hg_lb_logits, "hg_norm_g": hg_norm_g,
            "ret_norm_g": ret_norm_g, "w_out": w_out, "final_g": final_g}


def reference(x, c, norm_g, w_ada, b_ada, w_in, hg_lb_logits, hg_norm_g, ret_norm_g, w_out, final_g):
    B, L, D = x.shape
    pos = jnp.arange(L, dtype=jnp.int32)
    lower_bounds = jnp.cumsum(jax.nn.softmax(hg_lb_logits.astype(jnp.float32), axis=0), axis=0)
    offsets = np.cumsum([0] + SPLITS)[1:-1].tolist()
    for layer in range(DEPTH):
        mod = jax.nn.silu(c) @ w_ada[layer] + b_ada[layer]
        shift, scale, gate = jnp.split(mod[:, None, :], 3, axis=-1)
        h = rms_norm(x, norm_g[layer]) * (1.0 + scale) + shift
        proj = h @ w_in[layer]
        (hq, hf, hi, hz, rq, rk, rv, rz, ga, gb) = jnp.split(proj, offsets, axis=-1)
        f32 = jnp.float32
        oA = hgrn2_chunkwise(hq.astype(f32).reshape(B, L, HG_HEADS, HG_DK),
                             hf.astype(f32).reshape(B, L, HG_HEADS, HG_DK),
                             hi.astype(f32).reshape(B, L, HG_HEADS, HG_DV),
                             lower_bounds[layer])
        uA = head_rms_norm(oA, hg_norm_g[layer]) * jax.nn.silu(hz.astype(f32))
        qB = rotary(rq.astype(f32).reshape(B, L, RET_HEADS, RET_DK), pos)
        kB = rotary(rk.astype(f32).reshape(B, L, RET_HEADS, RET_DK), pos)
        oB = retention_chunkwise(qB, kB, rv.astype(f32).reshape(B, L, RET_HEADS, RET_DV))
        uB = head_rms_norm(oB, ret_norm_g[layer]) * jax.nn.silu(rz.astype(f32))
        m = (jax.nn.sigmoid(ga.astype(f32)) * uA + jax.nn.sigmoid(gb.astype(f32)) * uB).astype(x.dtype)
        x = x + gate * (m @ w_out[layer])
    return rms_norm(x, final_g)
```

```python
import numpy as np
import concourse.bass as bass
import concourse.mybir as mybir
from concourse.bass_utils import run_bass_kernel_spmd

F32 = mybir.dt.float32
BF16 = mybir.dt.bfloat16
AF = mybir.ActivationFunctionType
ALU = mybir.AluOpType

D = 1024
L = 4096
NH = 8
TW = 256
NT = L // TW
NSUB = TW // 128
EPS = 1e-6
WCOLS = 1152
ENGS = ("pe", "act", "dve", "pool", "sp")


class _Op:
    __slots__ = ("eng", "fn", "deps", "signal", "tok", "dma", "name")

    def __init__(self, eng, fn, dma, name):
        self.eng, self.fn, self.dma, self.name = eng, fn, dma, name
        self.deps = []
        self.signal = False
        self.tok = None


class Sched:
    def __init__(self, n_dma_sems=12):
        self.ops = {e: [] for e in ENGS}
        self.last_w = {}
        self.readers = {}
        self.n_dma_sems = n_dma_sems
        self.all_ops = []

    def add(self, eng, fn, reads=(), writes=(), dma=False, name=""):
        op = _Op(eng, fn, dma, name)
        deps = []
        for k in reads:
            w = self.last_w.get(k)
            if w is not None:
                deps.append(w)
        for k in writes:
            w = self.last_w.get(k)
            if w is not None:
                deps.append(w)
            for r in self.readers.get(k, ()):
                deps.append(r)
        seen = set()
        for d in deps:
            if d is op or id(d) in seen:
                continue
            if eng == "pe" and d.eng == "pe" and not d.dma and not dma:
                continue
            seen.add(id(d))
            op.deps.append(d)
            d.signal = True
        for k in reads:
            self.readers.setdefault(k, []).append(op)
        for k in writes:
            self.last_w[k] = op
            self.readers[k] = []
        self.ops[eng].append(op)
        self.all_ops.append(op)
        return op

    def fence(self):
        lasts = [self.ops[e][-1] for e in ENGS if self.ops[e]]
        dmas = [o for o in self.all_ops if o.dma]
        for e in ENGS:
            op = _Op(e, None, False, "fence")
            for d in lasts + dmas:
                if d.eng == e and not d.dma:
                    continue
                op.deps.append(d)
                d.signal = True
            self.ops[e].append(op)
            self.all_ops.append(op)
        self.last_w = {}
        self.readers = {}

    def emit(self, nc, block, sems, dma_sems, sw_sems):
        cnt = {e: 0 for e in ENGS}
        dcnt = [0] * len(dma_sems)
        dnext = 0
        swnext = 0
        dma_prev = {}
        for e in ENGS:
            pass
        for op in self.all_ops:
            if op.fn is None:
                continue
            if op.dma and op.eng == "pool":
                op.tok = ("w", swnext, 16)
                swnext += 1
                op.signal = True
            elif op.dma:
                i = dnext % len(dma_sems)
                dnext += 1
                dcnt[i] += 16
                op.tok = ("d", i, dcnt[i])
                prev = dma_prev.get(i)
                if prev is not None:
                    op.deps.append(prev)
                dma_prev[i] = op
                op.signal = True
            elif op.signal:
                cnt[op.eng] += 1
                op.tok = ("e", op.eng, cnt[op.eng])

        def run(engname, eng):
            waited = {}
            for op in self.ops[engname]:
                for d in op.deps:
                    t = d.tok
                    if t is None:
                        continue
                    key = (t[0], t[1])
                    if waited.get(key, 0) >= t[2]:
                        continue
                    waited[key] = t[2]
                    sem = dma_sems[t[1]] if t[0] == "d" else (sw_sems[t[1]] if t[0] == "w" else sems[t[1]])
                    eng.wait_ge(sem, t[2])
                if op.fn is None:
                    continue
                ins = op.fn(eng)
                if op.tok is not None:
                    if op.tok[0] == "d":
                        ins.then_inc(dma_sems[op.tok[1]], 16)
                    elif op.tok[0] == "w":
                        ins.then_inc(sw_sems[op.tok[1]], 16)
                    else:
                        ins.then_inc(sems[engname], 1)

        @block.tensor
        def _(eng):
            run("pe", eng)

        @block.scalar
        def _(eng):
            run("act", eng)

        @block.vector
        def _(eng):
            run("dve", eng)

        @block.gpsimd
        def _(eng):
            run("pool", eng)

        @block.sync
        def _(eng):
            run("sp", eng)


def _tables():
    try:
        import jax
        import jax.numpy as jnp
        with jax.default_device(jax.devices("cpu")[0]):
            inv_freq = 1.0 / (10000.0 ** jnp.linspace(0.0, 1.0, 32, dtype=jnp.float32))
            ang = jnp.arange(L, dtype=jnp.int32).astype(jnp.float32)[:, None] * inv_freq[None, :]
            cos = np.asarray(jnp.cos(ang), dtype=np.float32)
            sin = np.asarray(jnp.sin(ang), dtype=np.float32)
            log_gamma = np.asarray(jnp.log(1.0 - jnp.exp2(-5.0 - jnp.arange(NH, dtype=jnp.float32))), dtype=np.float32)
    except Exception:
        inv_freq = (1.0 / (np.float32(10000.0) ** np.linspace(0.0, 1.0, 32, dtype=np.float32))).astype(np.float32)
        ang = (np.arange(L, dtype=np.float32)[:, None] * inv_freq[None, :]).astype(np.float32).astype(np.float64)
        cos = np.cos(ang).astype(np.float32)
        sin = np.sin(ang).astype(np.float32)
        log_gamma = np.log(np.float32(1.0) - np.exp2(np.float32(-5.0) - np.arange(NH, dtype=np.float32))).astype(np.float32)
    cos = cos.reshape(L // 128, 128, 32).transpose(1, 0, 2)
    sin = sin.reshape(L // 128, 128, 32).transpose(1, 0, 2)
    lg = log_gamma.astype(np.float64)
    i = np.arange(128, dtype=np.float64)
    xi = np.exp(lg[None, :] * (i[:, None] + 1.0))
    zeta = np.exp(lg[None, :] * (127.0 - i[:, None])) * (64.0 ** -0.5)
    g128 = np.exp(lg * 128.0)
    ginv = np.exp(-lg * 128.0)
    s = np.arange(128)[:, None]
    t = np.arange(128)[None, :]
    mask_r = (t >= s).astype(np.float32)
    mask_h = ((t >= s) & ((t // 64) == (s // 64))).astype(np.float32)
    ident = np.eye(128, dtype=np.float32)
    return dict(cos=np.ascontiguousarray(cos), sin=np.ascontiguousarray(sin),
                xi=xi.astype(np.float32), zeta=zeta.astype(np.float32),
                g128=g128, ginv=ginv, mask_r=mask_r, mask_h=mask_h, ident=ident)


_TAB = _tables()


def build_program(dbg=None, n_heads_run=NH, n_tiles_run=NT, dbg_T=0):
    nc = bass.Bass("TRN2", target_bir_lowering=False)
    dt = lambda name, shape, dtype=F32, kind="ExternalInput": nc.dram_tensor(name, shape, dtype, kind=kind).ap()
    x_d = dt("x", [L, D])
    ccol_d = dt("c_col", [128, 8])
    wada_d = dt("w_ada", [D, 3 * D])
    bada_d = dt("b_ada", [1, 3 * D])
    ng_d = dt("norm_g", [1, D])
    fg_d = dt("final_g", [1, D])
    wp_d = dt("w_in_p", [NH, D, WCOLS])
    lbl_d = dt("lbl", [128, 16])
    ga_d = dt("hg_g", [128, 8])
    gb_d = dt("ret_g", [128, 8])
    wo_d = dt("w_out", [D, D])
    cos_d = dt("cos_t", [128, 32 * 32])
    sin_d = dt("sin_t", [128, 32 * 32])
    xi_d = dt("xi_t", [128, 8])
    zeta_d = dt("zeta_t", [128, 8])
    maskr_d = dt("mask_r", [128, 128])
    maskh_d = dt("mask_h", [128, 128])
    ident_d = dt("ident", [128, 128])
    out_d = dt("out", [L, D], kind="ExternalOutput")
    dbg_d = {}
    if dbg:
        for name, shape, dtype in dbg:
            dbg_d[name] = dt("dbg_" + name, shape, dtype, kind="ExternalOutput")

    S = Sched()
    g128 = [float(v) for v in _TAB["g128"]]
    ginv = [float(v) for v in _TAB["ginv"]]

    def bcast_rows(ap2d, n):
        return bass.AP(ap2d.tensor, ap2d.offset, [[0, 128], [1, n]])

    from contextlib import ExitStack
    with nc.cleanup_on_exit(), ExitStack() as es:
        sb = lambda name, shape, dtype=F32: es.enter_context(nc.sbuf_tensor("sb_" + name, shape, dtype))
        ps = lambda name, shape, dtype=F32: es.enter_context(nc.psum_tensor("ps_" + name, shape, dtype))
        hT = sb("hT", [128, 8, L], BF16)
        mT = sb("mT", [128, 8, L], BF16)
        gate_b = sb("gate_b", [128, D])
        identb = sb("identb", [128, 128], BF16)
        onesb = sb("onesb", [128, 128], BF16)
        maskh = sb("maskh", [128, 128])
        maskr = sb("maskr", [128, 128])
        xi_t = sb("xi_t", [128, 8])
        zeta_t = sb("zeta_t", [128, 8])
        c0 = sb("c0", [128, 8])
        c1 = sb("c1", [128, 8])
        nc1 = sb("nc1", [128, 8])
        gA = sb("gA", [128, 8])
        gB = sb("gB", [128, 8])
        zer = sb("zer", [128, 64])
        mh4 = sb("mh4", [128, 4])
        identf = sb("identf", [128, 128])
        onesf = sb("onesf", [128, 128])
        onescol = sb("onescol", [128, 2], BF16)
        PB = [ps("pb%d" % i, [128, 512]) for i in range(3)]
        PTf = ps("ptr", [128, 512])
        PT = PTf[:].bitcast(BF16)
        P4 = ps("p4", [128, 512])
        P5 = ps("p5", [128, 512])
        P6 = ps("p6", [128, 512])
        P7 = ps("p7", [128, 512])

        sems = {e: nc.alloc_semaphore("s_" + e) for e in ENGS}
        dma_sems = [nc.alloc_semaphore("d%d" % i) for i in range(S.n_dma_sems)]
        sw_sems = [nc.alloc_semaphore("w%d" % i) for i in range(2 * NH + 2)]

        def dma(eng, out, in_, reads=(), writes=(), name=""):
            return S.add(eng, lambda e: e.dma_start(out=out, in_=in_), reads, writes, dma=True, name=name)

        def dbg_out(name, src_ap, key, dst=None):
            if name in dbg_d:
                dma("sp", dbg_d[name] if dst is None else dst, src_ap, reads=[key], writes=["dbg_" + name])

        with ExitStack() as es0:
            sb0 = lambda name, shape, dtype=F32: es0.enter_context(nc.sbuf_tensor("s0_" + name, shape, dtype))
            ccol = sb0("ccol", [128, 8])
            scol = sb0("scol", [128, 8])
            screp = sb0("screp", [128, 8, 128])
            wa = [sb0("wa%d" % i, [128, 8, 256]) for i in range(2)]
            ng_b = sb0("ng_b", [128, D])
            mod_b = sb0("mod_b", [128, 2 * D])
            g1_b = sb0("g1_b", [128, D])
            lbl = sb0("lbl", [128, 16])
            dl = sb0("dl", [128, 8])
            thl = sb0("thl", [128, 8])
            graw = sb0("graw", [128, 16])
            xt = [sb0("xt%d" % i, [128, D]) for i in range(2)]
            junk = sb0("junk", [128, D], BF16)
            t1 = [sb0("t1_%d" % i, [128, D]) for i in range(2)]
            hb = [sb0("hb%d" % i, [128, D], BF16) for i in range(2)]
            ssq = sb0("ssq", [128, 2])
            rstd = sb0("rstd", [128, 2])
            mh1 = sb0("mh1", [128, 1])

            dma("sp", identf[:], ident_d, writes=["identf"])
            dma("sp", maskh[:], maskh_d, writes=["maskh"])
            dma("sp", maskr[:], maskr_d, writes=["maskr"])
            dma("sp", xi_t[:], xi_d, writes=["xi"])
            dma("sp", zeta_t[:], zeta_d, writes=["zeta"])
            dma("sp", ccol[:], ccol_d, writes=["ccol"])
            dma("sp", lbl[:], lbl_d, writes=["lbl"])
            dma("sp", graw[:, 0:8], ga_d, writes=["graw0"])
            dma("sp", graw[:, 8:16], gb_d, writes=["graw1"])
            dma("sp", mod_b[:], bcast_rows(bada_d[:, 0:2 * D], 2 * D), writes=["bias_b"])
            dma("sp", gate_b[:], bcast_rows(bada_d[:, 2 * D:3 * D], D), writes=["bias_b"])
            dma("sp", ng_b[:], bcast_rows(ng_d, D), writes=["ng_b"])

            S.add("dve", lambda e: e.tensor_copy(out=identb[:], in_=identf[:]), ["identf"], ["identb"])
            S.add("pool", lambda e: e.memset(onesb[:], 1.0), [], ["onesb"])
            S.add("pool", lambda e: e.memset(zer[:], 0.0), [], ["zer"])
            S.add("pool", lambda e: e.memset(mh4[:], -0.5), [], ["mh4"])
            S.add("pool", lambda e: e.memset(onesf[:], 1.0), [], ["onesf"])
            S.add("pool", lambda e: e.memset(onescol[:], 1.0), [], ["onescol"])
            S.add("pool", lambda e: e.memset(mh1[:], -0.5), [], ["mh1"])
            S.add("dve", lambda e: e.tensor_tensor(out=dl[:], in0=lbl[:, 0:8], in1=lbl[:, 8:16], op=ALU.subtract),
                  ["lbl"], ["dl"])
            S.add("act", lambda e: e.activation(out=thl[:], in_=dl[:], func=AF.Tanh, scale=0.5), ["dl"], ["thl"])
            S.add("dve", lambda e: e.tensor_scalar(out=c0[:], in0=thl[:], scalar1=0.25, scalar2=0.75,
                                                   op0=ALU.mult, op1=ALU.add), ["thl"], ["c0"])
            S.add("dve", lambda e: e.tensor_scalar(out=c1[:], in0=thl[:], scalar1=-0.25, scalar2=0.25,
                                                   op0=ALU.mult, op1=ALU.add), ["thl"], ["c1"])
            S.add("dve", lambda e: e.tensor_scalar(out=nc1[:], in0=thl[:], scalar1=0.25, scalar2=-0.25,
                                                   op0=ALU.mult, op1=ALU.add), ["thl"], ["nc1"])
            S.add("dve", lambda e: e.tensor_scalar(out=gA[:], in0=graw[:, 0:8], scalar1=0.5, scalar2=None,
                                                   op0=ALU.mult), ["graw0"], ["gA"])
            S.add("dve", lambda e: e.tensor_scalar(out=gB[:], in0=graw[:, 8:16], scalar1=0.5, scalar2=None,
                                                   op0=ALU.mult), ["graw1"], ["gB"])
            S.add("act", lambda e: e.activation(out=scol[:], in_=ccol[:], func=AF.Silu), ["ccol"], ["scol"])
            S.add("dve", lambda e: e.tensor_copy(out=screp[:], in_=scol[:].unsqueeze(2).to_broadcast([128, 8, 128])),
                  ["scol"], ["screp"])
            for g in range(12):
                wb = wa[g % 2]
                wk = "wa%d" % (g % 2)
                dma("sp", wb[:], wada_d[:, g * 256:(g + 1) * 256].rearrange("(j p) c -> p j c", p=128), writes=[wk])
                pb = PB[g % 3]
                pk = "pb%d" % (g % 3)
                for j in range(8):
                    S.add("pe", lambda e, j=j, wb=wb, pb=pb: e.matmul(pb[:, 0:256], lhsT=screp[:, j, :], rhs=wb[:, j, :],
                                                                     start=(j == 0), stop=(j == 7)),
                          ["screp", wk], [pk])
                if g < 8:
                    dst = mod_b[:, g * 256:(g + 1) * 256]
                else:
                    dst = gate_b[:, (g - 8) * 256:(g - 7) * 256]
                S.add("dve", lambda e, dst=dst, pb=pb: e.tensor_tensor(out=dst, in0=pb[:, 0:256], in1=dst, op=ALU.add),
                      [pk, "bias_b"], ["modg%d" % g])
            S.add("dve", lambda e: e.scalar_tensor_tensor(out=g1_b[:], in0=mod_b[:, D:2 * D], scalar=1.0, in1=ng_b[:],
                                                          op0=ALU.add, op1=ALU.mult),
                  ["modg%d" % g for g in range(4, 8)] + ["ng_b"], ["g1_b"])
            dbg_out("mod", mod_b[0:1, :], "g1_b")

            for ti in range(L // 128):
                b = ti % 2
                xk, hk = "xt%d" % b, "hb%d" % b
                dma("sp", xt[b][:], x_d[ti * 128:(ti + 1) * 128, :], writes=[xk])
                S.add("act", lambda e, b=b: e.activation(out=junk[:], in_=xt[b][:], func=AF.Square,
                                                         accum_out=ssq[:, b:b + 1]), [xk], ["junk", "ssq%d" % b])
                S.add("dve", lambda e, b=b: e.tensor_scalar(out=ssq[:, b:b + 1], in0=ssq[:, b:b + 1], scalar1=1.0 / D,
                                                            scalar2=EPS, op0=ALU.mult, op1=ALU.add),
                      ["ssq%d" % b], ["ssq%d" % b])
                S.add("pool", lambda e, b=b: e.tensor_tensor(out=rstd[:, b:b + 1], in0=ssq[:, b:b + 1], in1=mh1[:],
                                                             op=ALU.pow), ["ssq%d" % b, "mh1"], ["rstd%d" % b])
                S.add("dve", lambda e, b=b: e.scalar_tensor_tensor(out=t1[b][:], in0=xt[b][:], scalar=rstd[:, b:b + 1],
                                                                   in1=g1_b[:], op0=ALU.mult, op1=ALU.mult),
                      [xk, "rstd%d" % b, "g1_b"], ["t1_%d" % b])
                S.add("pool", lambda e, b=b: e.tensor_tensor(out=hb[b][:], in0=t1[b][:], in1=mod_b[:, 0:D], op=ALU.add),
                      ["t1_%d" % b] + ["modg%d" % g for g in range(4)], [hk])
                ptb1 = (PTf if b == 0 else PB[0])[:].bitcast(BF16)
                ptk1 = "PT1_%d" % b
                for j in range(8):
                    S.add("pe", lambda e, b=b, j=j, ptb1=ptb1: e.transpose(out=ptb1[:, j * 128:(j + 1) * 128],
                                                                           in_=hb[b][:, j * 128:(j + 1) * 128], identity=identb[:]),
                          [hk, "identb"], [ptk1])
                S.add("act", lambda e, ti=ti, ptb1=ptb1: e.activation(out=hT[:, :, ti * 128:(ti + 1) * 128],
                                                                      in_=ptb1[:].rearrange("p (j t) -> p j t", j=8), func=AF.Copy),
                      [ptk1], ["hT%d" % (ti * 128 // TW)])
            if "hT" in dbg_d:
                for j in range(8):
                    dma("sp", dbg_d["hT"][j * 128:(j + 1) * 128, :], hT[:, j, :],
                        reads=["hT%d" % t for t in range(NT)], writes=["dbg_hT"])
            S.fence()


        with ExitStack() as es2:
            sb2 = lambda name, shape, dtype=F32: es2.enter_context(nc.sbuf_tensor("s2_" + name, shape, dtype))
            D2 = lambda name, shape, dtype=F32: [sb2("%s_%d" % (name, i), shape, dtype) for i in range(2)]
            Wh = D2("wh", [128, 8, WCOLS], BF16)
            cs = D2("cs", [128, NSUB, 32]); sn = D2("sn", [128, NSUB, 32])
            th = sb2("th", [128, TW]); sq = th
            kk = sb2("kk", [128, TW]); ff = sb2("ff", [128, TW]); RR = ff
            sz = sb2("sz", [128, TW]); tga = sb2("tga", [128, TW]); srz = sb2("srz", [128, TW]); tgb = sb2("tgb", [128, TW])
            PP = D2("PP", [128, TW])
            Qt = D2("Qt", [128, TW], BF16); Kt = D2("Kt", [128, TW], BF16)
            Ktm = D2("Ktm", [128, NSUB, 2, 128], BF16)
            vAB = D2("vAB", [128, NSUB, 256], BF16)
            qkr = D2("qkr", [128, NSUB, 256], BF16)
            qkT = D2("qkT", [128, 2, TW], BF16)
            At = sb2("At", [128, NSUB, 128], BF16)
            Sc = sb2("Sc", [128, NSUB, 128], BF16)
            Sst = D2("Sst", [128, 128])
            Vst = [sb2("Vst%d" % i, [128, 128]) for i in range(4)]
            NSB = 4
            Sbf = [sb2("Sbf%d" % i, [128, 128], BF16) for i in range(NSB)]
            NRB = 4
            Rst = D2("Rst", [128, 128])
            Rbf = [sb2("Rbf%d" % i, [128, 128], BF16) for i in range(NRB)]
            qk = sb2("qk", [128, NSUB, 128])
            ra = sb2("ra", [128, NSUB, 2, 32]); rb = sb2("rb", [128, NSUB, 2, 32])
            rc = sb2("rc", [128, NSUB, 2, 32]); rd = sb2("rd", [128, NSUB, 2, 32])
            sqoA = sb2("sqoA", [128, TW], BF16); sqoB = sb2("sqoB", [128, TW], BF16)
            uA = sb2("uA", [128, TW]); uB = sb2("uB", [128, TW])
            dgh = sb2("dgh", [128, 2 * NSUB, 128], BF16); dgl = sb2("dgl", [128, 2 * NSUB, 128], BF16)
            rsv = sb2("rsv", [128, 4])

            def load_w(h):
                hb_ = h % 2
                for half in range(2):
                    dma("pool", Wh[hb_][:, half * 4:(half + 1) * 4, :],
                        wp_d[h, half * 512:(half + 1) * 512, :].rearrange("(j p) c -> p j c", p=128),
                        writes=["wh%d_%d" % (hb_, half)])

            load_w(0)
            pbrr = [0]
            for p_ in range(2):
                S.add("pool", lambda e, p_=p_: e.memset(Ktm[p_][:], 0.0), [], ["Ktm0_%d" % p_, "Ktm1_%d" % p_])
                S.add("pool", lambda e, p_=p_: e.memset(qkT[p_][:], 0.0), [], ["qkT_%d" % p_])
                S.add("pool", lambda e, p_=p_: e.memset(qkr[p_][:], 0.0), [], ["qkr_a_%d" % p_, "qkr_b_%d" % p_])
            for i_ in range(NRB):
                S.add("pool", lambda e, i_=i_: e.memset(Rbf[i_][:], 0.0), [], ["Rbf%d" % i_])

            PB4 = PB + [PTf]
            PB4b = [b_[:].bitcast(BF16) for b_ in PB4]

            def next_pb(bf=False):
                i = pbrr[0] % 4
                pbrr[0] += 1
                return (PB4b[i] if bf else PB4[i]), "pb%d" % i

            def front(g):
                h, T = divmod(g, n_tiles_run)
                p = g % 2
                K = lambda nm: "%s_%d" % (nm, p)
                hb_ = h % 2
                W = Wh[hb_]
                whk = ["wh%d_0" % hb_, "wh%d_1" % hb_]
                hs = slice(h, h + 1)
                tok0 = T * TW
                hTk = "hT%d" % T
                dma("sp", cs[p][:], cos_d[:, T * NSUB * 32:(T + 1) * NSUB * 32].rearrange("p (s f) -> p s f", s=NSUB), writes=[K("cs")])
                dma("sp", sn[p][:], sin_d[:, T * NSUB * 32:(T + 1) * NSUB * 32].rearrange("p (s f) -> p s f", s=NSUB), writes=[K("sn")])

                def proj_fm(cblk):
                    pb, pk = next_pb()
                    for j in range(8):
                        S.add("pe", lambda e, j=j, pb=pb: e.matmul(pb[:, 0:TW], lhsT=W[:, j, cblk * 128:(cblk + 1) * 128],
                                                                   rhs=hT[:, j, tok0:tok0 + TW], start=(j == 0), stop=(j == 7)),
                              whk + [hTk], [pk])
                    return pb, pk

                pb, pk = proj_fm(1)
                S.add("act", lambda e, pb=pb: e.activation(out=th[:], in_=pb[:, 0:TW], func=AF.Tanh, scale=0.5), [pk], ["th"])
                S.add("act", lambda e: e.activation(out=kk[:], in_=th[:], func=AF.Identity, scale=nc1[:, hs], bias=c1[:, hs]), ["th"], ["kk"])
                S.add("act", lambda e: e.activation(out=ff[:], in_=th[:], func=AF.Identity, scale=c1[:, hs], bias=c0[:, hs]), ["th"], ["ff"])
                yield
                for c in range(TW // 64):
                    S.add("dve", lambda e, c=c: e.tensor_tensor_scan(out=PP[p][:, c * 64:(c + 1) * 64], data0=ff[:, c * 64:(c + 1) * 64],
                                                                    data1=zer[:, 0:64], initial=1.0, op0=ALU.mult, op1=ALU.add),
                          ["ff"], [K("PP%d" % c)])
                PPk = [K("PP%d" % c) for c in range(TW // 64)]
                S.add("dve", lambda e: e.reciprocal(out=RR[:], in_=PP[p][:]), PPk, ["ff"])
                pb, pk = proj_fm(0)
                S.add("act", lambda e, pb=pb: e.activation(out=sq[:], in_=pb[:, 0:TW], func=AF.Silu), [pk], ["th"])
                yield
                S.add("dve", lambda e: e.tensor_tensor(out=Qt[p][:], in0=sq[:], in1=PP[p][:], op=ALU.mult), ["th"] + PPk, [K("Qt")])
                S.add("pool", lambda e: e.tensor_tensor(out=Kt[p][:], in0=kk[:], in1=RR[:], op=ALU.mult), ["kk", "ff"], [K("Kt")])
                for s_ in range(NSUB):
                    pb, pk = next_pb()
                    for j in range(8):
                        S.add("pe", lambda e, j=j, pb=pb, s_=s_: e.matmul(pb[:, 0:384], lhsT=hT[:, j, tok0 + s_ * 128:tok0 + (s_ + 1) * 128],
                                                                          rhs=W[:, j, 768:1152], start=(j == 0), stop=(j == 7)),
                              whk + [hTk], [pk])
                    S.add("act", lambda e, pb=pb, s_=s_: e.activation(out=vAB[p][:, s_, :], in_=pb[:, 0:256], func=AF.Copy), [pk], [K("vAB%d" % s_)])
                    S.add("act", lambda e, pb=pb, s_=s_: e.activation(out=qk[:, s_, 0:64], in_=pb[:, 256:320], func=AF.Identity,
                                                                      scale=xi_t[:, hs]), [pk], ["qk%d" % s_])
                    S.add("act", lambda e, pb=pb, s_=s_: e.activation(out=qk[:, s_, 64:128], in_=pb[:, 320:384], func=AF.Identity,
                                                                      scale=zeta_t[:, hs]), [pk], ["qk%d" % s_])
                    yield
                ptb, ptk = next_pb(bf=True)
                for s_ in range(NSUB):
                    S.add("pe", lambda e, s_=s_, ptb=ptb: e.transpose(out=ptb[:, s_ * 128:(s_ + 1) * 128], in_=Kt[p][:, s_ * 128:(s_ + 1) * 128],
                                                                      identity=identb[:]), [K("Kt")], [ptk])
                for ci in range(2):
                    S.add("act", lambda e, ci=ci, ptb=ptb: e.activation(out=Ktm[p][ci * 64:(ci + 1) * 64, :, ci, :],
                                                                        in_=ptb[ci * 64:(ci + 1) * 64, 0:256].rearrange("p (s d) -> p s d", s=NSUB),
                                                                        func=AF.Copy), [ptk], [K("Ktm%d" % ci)])
                qk5 = qk[:].rearrange("p s (a b f) -> p s a b f", a=2, b=2)
                qa, qb = qk5[:, :, :, 0, :], qk5[:, :, :, 1, :]
                qr6 = qkr[p][:].rearrange("p s (a z b f) -> p s a z b f", a=2, z=2, b=2)
                cosb = cs[p][:].unsqueeze(2).to_broadcast([128, NSUB, 2, 32])
                sinb = sn[p][:].unsqueeze(2).to_broadcast([128, NSUB, 2, 32])
                qkk = ["qk%d" % s_ for s_ in range(NSUB)]
                S.add("dve", lambda e: e.tensor_tensor(out=ra[:], in0=qa, in1=cosb, op=ALU.mult), qkk + [K("cs")], ["ra"])
                S.add("dve", lambda e: e.tensor_tensor(out=rb[:], in0=qb, in1=sinb, op=ALU.mult), qkk + [K("sn")], ["rb"])
                S.add("dve", lambda e: e.tensor_tensor(out=qr6[:, :, :, 0, 0, :], in0=ra[:], in1=rb[:], op=ALU.subtract), ["ra", "rb"], [K("qkr_a")])
                S.add("pool", lambda e: e.tensor_tensor(out=rc[:], in0=qa, in1=sinb, op=ALU.mult), qkk + [K("sn")], ["rc"])
                S.add("pool", lambda e: e.tensor_tensor(out=rd[:], in0=qb, in1=cosb, op=ALU.mult), qkk + [K("cs")], ["rd"])
                S.add("pool", lambda e: e.tensor_tensor(out=qr6[:, :, :, 0, 1, :], in0=rc[:], in1=rd[:], op=ALU.add), ["rc", "rd"], [K("qkr_b")])
                yield
                qkrk = [K("qkr_a"), K("qkr_b")]
                ptb, ptk = next_pb(bf=True)
                for s_ in range(NSUB):
                    S.add("pe", lambda e, s_=s_, ptb=ptb: e.transpose(out=ptb[:, s_ * 128:(s_ + 1) * 128], in_=qkr[p][:, s_, 0:128],
                                                                      identity=identb[:]), qkrk, [ptk])
                    S.add("pe", lambda e, s_=s_, ptb=ptb: e.transpose(out=ptb[:, 256 + s_ * 128:256 + (s_ + 1) * 128], in_=qkr[p][:, s_, 128:256],
                                                                      identity=identb[:]), qkrk, [ptk])
                S.add("act", lambda e, ptb=ptb: e.activation(out=qkT[p][:].rearrange("p a t -> p (a t)"), in_=ptb[:, 0:512], func=AF.Copy),
                      [ptk], [K("qkT")])
                yield

            def back(g):
                h, T = divmod(g, n_tiles_run)
                p = g % 2
                K = lambda nm: "%s_%d" % (nm, p)
                hs = slice(h, h + 1)
                tok0 = T * TW
                gc0 = g * (TW // 64)
                rn0 = g * NSUB
                PPk = [K("PP%d" % c) for c in range(TW // 64)]
                qkrk = [K("qkr_a"), K("qkr_b")]
                POb, pok = (P6, "P6") if p == 0 else (P7, "P7")

                def tap(name, ap, keys):
                    if name in dbg_d and h == 0 and T == dbg_T:
                        dma("sp", dbg_d[name], ap, reads=keys, writes=["dbg_" + name])

                import os
                if int(os.environ.get("KCUT", "99")) <= -1:
                    return
                for s_ in range(NSUB if "h" not in os.environ.get("KSKIP", "") else 0):
                    S.add("pe", lambda e, s_=s_: e.matmul(P4[:, s_ * 128:(s_ + 1) * 128], lhsT=Kt[p][:, s_ * 128:(s_ + 1) * 128],
                                                          rhs=Qt[p][:, s_ * 128:(s_ + 1) * 128], start=True, stop=True),
                          [K("Kt"), K("Qt")], ["P4"])
                    S.add("dve", lambda e, s_=s_: e.tensor_tensor(out=At[:, s_, :], in0=P4[:, s_ * 128:(s_ + 1) * 128], in1=maskh[:],
                                                                  op=ALU.mult), ["P4"], ["At%d" % s_])
                KS = os.environ.get("KSKIP", "")
                for s_ in range(NSUB if "r" not in KS else 0):
                    S.add("pe", lambda e, s_=s_: e.matmul(P4[:, 256 + s_ * 128:256 + (s_ + 1) * 128], lhsT=qkT[p][:, 1, s_ * 128:(s_ + 1) * 128],
                                                          rhs=qkT[p][:, 0, s_ * 128:(s_ + 1) * 128], start=True, stop=True),
                          [K("qkT")], ["P4"])
                    S.add("dve", lambda e, s_=s_: e.scalar_tensor_tensor(out=Sc[:, s_, :], in0=P4[:, 256 + s_ * 128:256 + (s_ + 1) * 128],
                                                                         scalar=ginv[h], in1=maskr[:], op0=ALU.mult, op1=ALU.mult),
                          ["P4"], ["Sc%d" % s_])
                    S.add("pe", lambda e, s_=s_: e.matmul(P5[:, 256 + s_ * 128:256 + (s_ + 1) * 128], lhsT=qkr[p][:, s_, 128:256],
                                                          rhs=vAB[p][:, s_, 128:256], start=True, stop=True),
                          qkrk + [K("vAB%d" % s_)], ["P5"])
                yield

                def emit_U(cc):
                    s_, ci = divmod(cc, 2)
                    pu = cc % 2
                    plast = PP[p][:, cc * 64 + 63:cc * 64 + 64]
                    S.add("pe", lambda e: e.matmul(P5[:, pu * 128:(pu + 1) * 128], lhsT=Ktm[p][:, s_, ci, :],
                                                   rhs=vAB[p][:, s_, 0:128], start=True, stop=True),
                          [K("Ktm%d" % ci), K("vAB%d" % s_)], ["P5"])
                    S.add("act", lambda e: e.activation(out=Vst[cc][:], in_=P5[:, pu * 128:(pu + 1) * 128], func=AF.Identity, scale=plast),
                          ["P5", K("PP%d" % cc)], ["Vst%d" % cc])

                def emit_state(cc):
                    gcn = gc0 + cc
                    first = (T == 0 and cc == 0)
                    so, sn_ = gcn % 2, (gcn + 1) % 2
                    plast = PP[p][:, cc * 64 + 63:cc * 64 + 64]
                    if first:
                        S.add("dve", lambda e: e.tensor_copy(out=Sst[sn_][:], in_=Vst[cc][:]), ["Vst%d" % cc], ["Sst%d" % sn_])
                    else:
                        S.add("dve", lambda e: e.scalar_tensor_tensor(out=Sst[sn_][:], in0=Sst[so][:], scalar=plast, in1=Vst[cc][:],
                                                                      op0=ALU.mult, op1=ALU.add),
                              ["Sst%d" % so, "Vst%d" % cc, K("PP%d" % cc)], ["Sst%d" % sn_])
                    sl = (gcn + 1) % NSB
                    S.add("act", lambda e: e.activation(out=Sbf[sl][:], in_=Sst[sn_][:], func=AF.Copy), ["Sst%d" % sn_], ["Sbf%d" % sl])

                def emit_o(cc):
                    gcn = gc0 + cc
                    s_, ci = divmod(cc, 2)
                    first = (T == 0 and cc == 0)
                    S.add("pe", lambda e: e.matmul(POb[:, cc * 64:(cc + 1) * 64], lhsT=vAB[p][:, s_, 0:128], rhs=At[:, s_, ci * 64:(ci + 1) * 64],
                                                   start=True, stop=first), [K("vAB%d" % s_), "At%d" % s_], [pok])
                    if not first:
                        sl = gcn % NSB
                        S.add("pe", lambda e: e.matmul(POb[:, cc * 64:(cc + 1) * 64], lhsT=Sbf[sl][:], rhs=Qt[p][:, cc * 64:(cc + 1) * 64],
                                                       start=False, stop=True), ["Sbf%d" % sl, K("Qt")], [pok])

                def emit_ret(s_):
                    rnn = rn0 + s_
                    first = (T == 0 and s_ == 0)
                    ro, rnw = rnn % 2, (rnn + 1) % 2
                    S.add("pe", lambda e: e.matmul(POb[:, 256 + s_ * 128:256 + (s_ + 1) * 128], lhsT=vAB[p][:, s_, 128:256], rhs=Sc[:, s_, :],
                                                   start=True, stop=first), [K("vAB%d" % s_), "Sc%d" % s_], [pok])
                    if not first:
                        sl = rnn % NRB
                        S.add("pe", lambda e: e.matmul(POb[:, 256 + s_ * 128:256 + (s_ + 1) * 128], lhsT=Rbf[sl][:], rhs=qkT[p][:, 0, s_ * 128:(s_ + 1) * 128],
                                                       start=False, stop=True), ["Rbf%d" % sl, K("qkT")], [pok])
                        S.add("dve", lambda e: e.scalar_tensor_tensor(out=Rst[rnw][0:64, :], in0=Rst[ro][0:64, :], scalar=g128[h],
                                                                      in1=P5[0:64, 256 + s_ * 128:256 + (s_ + 1) * 128],
                                                                      op0=ALU.mult, op1=ALU.add), ["Rst%d" % ro, "P5"], ["Rst%d" % rnw])
                    else:
                        S.add("dve", lambda e: e.tensor_copy(out=Rst[rnw][0:64, :], in_=P5[0:64, 256 + s_ * 128:256 + (s_ + 1) * 128]),
                              ["P5"], ["Rst%d" % rnw])
                    sl2 = (rnn + 1) % NRB
                    S.add("act", lambda e: e.activation(out=Rbf[sl2][0:64, :], in_=Rst[rnw][0:64, :], func=AF.Copy), ["Rst%d" % rnw], ["Rbf%d" % sl2])

                import os
                CUT = int(os.environ.get("KCUT", "99"))
                if CUT <= 0:
                    return
                emit_U(0); emit_U(1)
                emit_state(0)
                emit_o(0)
                yield
                emit_U(2)
                emit_state(1)
                emit_ret(0)
                yield
                emit_U(3)
                emit_o(1)
                emit_state(2)
                yield
                emit_ret(1)
                emit_o(2)
                emit_state(3)
                yield
                emit_o(3)
                yield

            def back2(g):
                h, T = divmod(g, n_tiles_run)
                p = g % 2
                hb_ = h % 2
                W = Wh[hb_]
                whk = ["wh%d_0" % hb_, "wh%d_1" % hb_]
                hs = slice(h, h + 1)
                tok0 = T * TW
                hTk = "hT%d" % T
                POb, pok = (P6, "P6") if p == 0 else (P7, "P7")

                def tap(name, ap, keys):
                    if name in dbg_d and h == 0 and T == dbg_T:
                        dma("sp", dbg_d[name], ap, reads=keys, writes=["dbg_" + name])

                def proj_gate(cblk, dst, func, key, **kw):
                    pb, pk = next_pb()
                    for j in range(8):
                        S.add("pe", lambda e, j=j, pb=pb: e.matmul(pb[:, 0:TW], lhsT=W[:, j, cblk * 128:(cblk + 1) * 128],
                                                                   rhs=hT[:, j, tok0:tok0 + TW], start=(j == 0), stop=(j == 7)),
                              whk + [hTk], [pk])
                    S.add("act", lambda e, pb=pb: e.activation(out=dst[:], in_=pb[:, 0:TW], func=func, **kw), [pk], [key])

                proj_gate(2, sz, AF.Silu, "sz")
                S.add("act", lambda e: e.activation(out=sqoA[:], in_=POb[:, 0:TW], func=AF.Square), [pok], ["sqoA"])
                S.add("act", lambda e: e.activation(out=sqoB[:], in_=POb[:, 256:256 + TW], func=AF.Square), [pok], ["sqoB"])
                yield
                ptf, ptk = next_pb()
                for bi, sqo in ((0, sqoA), (1, sqoB)):
                    for s_ in range(NSUB):
                        S.add("pe", lambda e, sqo=sqo, s_=s_, bi=bi, ptf=ptf: e.matmul(ptf[:, bi * 2 + s_:bi * 2 + s_ + 1],
                                                                                      lhsT=sqo[:, s_ * 128:(s_ + 1) * 128], rhs=onescol[:, 0:1],
                                                                                      start=True, stop=True), ["sqoA", "sqoB"], [ptk])
                S.add("dve", lambda e, ptf=ptf: e.tensor_scalar(out=rsv[:], in0=ptf[:, 0:4], scalar1=1.0 / 128.0, scalar2=EPS,
                                                                op0=ALU.mult, op1=ALU.add), [ptk], ["rsv"])
                S.add("pool", lambda e: e.tensor_tensor(out=rsv[:], in0=rsv[:], in1=mh4[:], op=ALU.pow), ["rsv"], ["rsv"])
                proj_gate(4, srz, AF.Silu, "srz")
                yield
                S.add("dve", lambda e: e.scalar_tensor_tensor(out=uA[:], in0=POb[:, 0:TW], scalar=gA[:, hs], in1=sz[:],
                                                              op0=ALU.mult, op1=ALU.mult), [pok, "sz"], ["uA"])
                S.add("dve", lambda e: e.scalar_tensor_tensor(out=uB[:], in0=POb[:, 256:256 + TW], scalar=gB[:, hs], in1=srz[:],
                                                              op0=ALU.mult, op1=ALU.mult), [pok, "srz"], ["uB"])
                for k_ in range(2 * NSUB):
                    S.add("act", lambda e, k_=k_: e.activation(out=dgh[:, k_, :], in_=identf[:], func=AF.Identity, scale=rsv[:, k_:k_ + 1]),
                          ["rsv"], ["dgh%d" % k_])
                    S.add("dve", lambda e, k_=k_: e.scalar_tensor_tensor(out=dgl[:, k_, :], in0=identf[:], scalar=rsv[:, k_:k_ + 1], in1=dgh[:, k_, :],
                                                                         op0=ALU.mult, op1=ALU.subtract), ["rsv", "dgh%d" % k_], ["dgl%d" % k_])
                proj_gate(3, tga, AF.Tanh, "tga", scale=0.5)
                yield
                pbc, pbk = next_pb()
                for k_ in range(2 * NSUB):
                    S.add("pe", lambda e, k_=k_, pbc=pbc: e.matmul(pbc[:, k_ * 128:(k_ + 1) * 128], lhsT=onesb[:], rhs=dgh[:, k_, :],
                                                                   start=True, stop=False), ["dgh%d" % k_], [pbk])
                    S.add("pe", lambda e, k_=k_, pbc=pbc: e.matmul(pbc[:, k_ * 128:(k_ + 1) * 128], lhsT=onesb[:], rhs=dgl[:, k_, :],
                                                                   start=False, stop=True), ["dgl%d" % k_], [pbk])
                proj_gate(5, tgb, AF.Tanh, "tgb", scale=0.5)
                yield
                S.add("dve", lambda e, pbc=pbc: e.tensor_tensor(out=uA[:], in0=uA[:], in1=pbc[:, 0:TW], op=ALU.mult), ["uA", pbk], ["uA"])
                S.add("dve", lambda e, pbc=pbc: e.tensor_tensor(out=uB[:], in0=uB[:], in1=pbc[:, 256:256 + TW], op=ALU.mult), ["uB", pbk], ["uB"])
                yield
                S.add("dve", lambda e: e.scalar_tensor_tensor(out=uA[:], in0=tga[:], scalar=1.0, in1=uA[:], op0=ALU.add, op1=ALU.mult),
                      ["uA", "tga"], ["uA"])
                S.add("dve", lambda e: e.scalar_tensor_tensor(out=uB[:], in0=tgb[:], scalar=1.0, in1=uB[:], op0=ALU.add, op1=ALU.mult),
                      ["uB", "tgb"], ["uB"])
                tap("uA", uA[:], ["uA"]); tap("uB", uB[:], ["uB"])
                S.add("dve", lambda e: e.tensor_tensor(out=mT[:, h, tok0:tok0 + TW], in0=uA[:], in1=uB[:], op=ALU.add), ["uA", "uB"], ["mT%d" % T])
                yield

            from itertools import zip_longest
            import os
            G = n_heads_run * n_tiles_run
            for _ in front(0):
                pass
            TLOAD = min(2, n_tiles_run - 1)
            for i in range(G + 1):
                if i < G:
                    h_i, T_i = divmod(i, n_tiles_run)
                    if T_i == TLOAD and h_i + 1 < n_heads_run:
                        load_w(h_i + 1)
                gens = []
                if i + 1 < G:
                    gens.append(front(i + 1))
                if i < G:
                    gens.append(back(i))
                if i >= 1:
                    gens.append(back2(i - 1))
                for _ in zip_longest(*gens):
                    pass
            if "mT" in dbg_d:
                for j in range(8):
                    dma("sp", dbg_d["mT"][j * 128:(j + 1) * 128, :], mT[:, j, :],
                        reads=["mT%d" % t for t in range(NT)], writes=["dbg_mT"])
            S.fence()

        with ExitStack() as es3:
            sb3 = lambda name, shape, dtype=F32: es3.enter_context(nc.sbuf_tensor("s3_" + name, shape, dtype))
            Wo = sb3("wo", [128, 8, D], BF16)
            fg_b = sb3("fg_b", [128, D])
            xt3 = [sb3("xt%d" % i, [128, D]) for i in range(2)]
            xn = [sb3("xn%d" % i, [128, D]) for i in range(2)]
            ot = [sb3("ot%d" % i, [128, D]) for i in range(2)]
            junk3 = sb3("junk", [128, D], BF16)
            ss3 = sb3("ss3", [128, 2]); rs3 = sb3("rs3", [128, 2]); mh3 = sb3("mh3", [128, 1])
            for half in range(2):
                dma("pool", Wo[:, half * 4:(half + 1) * 4, :], wo_d[half * 512:(half + 1) * 512, :].rearrange("(j p) c -> p j c", p=128),
                    writes=["wo%d" % half])
            dma("sp", fg_b[:], bcast_rows(fg_d, D), writes=["fg_b"])
            S.add("pool", lambda e: e.memset(mh3[:], -0.5), [], ["mh3"])
            for ti in range(L // 128):
                b = ti % 2
                xk = "x3_%d" % b
                dma("sp", xt3[b][:], x_d[ti * 128:(ti + 1) * 128, :], writes=[xk])
                PB3 = [PB[0], PB[1], PB[2], PTf]
                for g in range(2):
                    pb, pk = PB3[b * 2 + g], "pb3_%d" % (b * 2 + g)
                    for j in range(8):
                        S.add("pe", lambda e, j=j, g=g, pb=pb, ti=ti: e.matmul(pb[:], lhsT=mT[:, j, ti * 128:(ti + 1) * 128],
                                                                              rhs=Wo[:, j, g * 512:(g + 1) * 512], start=(j == 0), stop=(j == 7)),
                              ["wo0", "wo1", "mT"], [pk])
                    S.add("dve", lambda e, g=g, pb=pb, b=b: e.tensor_tensor(out=xn[b][:, g * 512:(g + 1) * 512], in0=pb[:],
                                                                            in1=gate_b[:, g * 512:(g + 1) * 512], op=ALU.mult),
                          [pk, "gate_b"], ["xn%d_%d" % (b, g)])
                xnk = ["xn%d_0" % b, "xn%d_1" % b]
                S.add("pool", lambda e, b=b: e.tensor_tensor(out=xn[b][:], in0=xn[b][:], in1=xt3[b][:], op=ALU.add), xnk + [xk], xnk)
                S.add("act", lambda e, b=b: e.activation(out=junk3[:], in_=xn[b][:], func=AF.Square, accum_out=ss3[:, b:b + 1]),
                      xnk, ["junk3", "ss3_%d" % b])
                S.add("dve", lambda e, b=b: e.tensor_scalar(out=ss3[:, b:b + 1], in0=ss3[:, b:b + 1], scalar1=1.0 / D, scalar2=EPS,
                                                            op0=ALU.mult, op1=ALU.add), ["ss3_%d" % b], ["ss3_%d" % b])
                S.add("pool", lambda e, b=b: e.tensor_tensor(out=rs3[:, b:b + 1], in0=ss3[:, b:b + 1], in1=mh3[:], op=ALU.pow),
                      ["ss3_%d" % b, "mh3"], ["rs3_%d" % b])
                S.add("dve", lambda e, b=b: e.scalar_tensor_tensor(out=ot[b][:], in0=xn[b][:], scalar=rs3[:, b:b + 1], in1=fg_b[:],
                                                                   op0=ALU.mult, op1=ALU.mult), xnk + ["rs3_%d" % b, "fg_b"], ["ot%d" % b])
                dma("sp", out_d[ti * 128:(ti + 1) * 128, :], ot[b][:], reads=["ot%d" % b], writes=["out"])

        S.fence()
        for sm in list(sems.values()) + dma_sems + sw_sems:
            nc.gpsimd.sem_clear(sm)
        nc.all_engine_barrier()
        with nc.Block() as block:
            S.emit(nc, block, sems, dma_sems, sw_sems)
        nc.all_engine_barrier()
    return nc


def _perm_cols():
    offs = np.cumsum([0, 1024, 1024, 1024, 1024, 512, 512, 1024, 1024, 1024, 1024])
    o_hq, o_hf, o_hi, o_hz, o_rq, o_rk, o_rv, o_rz, o_ga, o_gb = offs[:10]
    cols = []
    for h in range(NH):
        c = []
        for o in (o_hq, o_hf, o_hz, o_ga, o_rz, o_gb, o_hi, o_rv):
            c += list(range(o + h * 128, o + (h + 1) * 128))
        c += list(range(o_rq + h * 64, o_rq + (h + 1) * 64))
        c += list(range(o_rk + h * 64, o_rk + (h + 1) * 64))
        cols.append(c)
    return np.array(cols)


def make_in_maps(x, c, norm_g, w_ada, b_ada, w_in, hg_lb_logits, hg_norm_g, ret_norm_g, w_out, final_g):
    f = lambda a: np.ascontiguousarray(np.asarray(a, dtype=np.float32))
    x, c = f(x), f(c)
    cols = _perm_cols()
    w_in0 = f(w_in)[0]
    wp = np.ascontiguousarray(np.stack([w_in0[:, cols[h]] for h in range(NH)], 0))
    lb = f(hg_lb_logits)
    lbl = np.ascontiguousarray(np.concatenate([lb[0].reshape(8, 128).T, lb[1].reshape(8, 128).T], 1))
    shared = {
        "w_ada": f(w_ada)[0], "b_ada": f(b_ada)[0].reshape(1, -1), "norm_g": f(norm_g)[0].reshape(1, -1),
        "final_g": f(final_g).reshape(1, -1), "w_in_p": wp, "lbl": lbl,
        "hg_g": np.ascontiguousarray(f(hg_norm_g)[0].reshape(8, 128).T),
        "ret_g": np.ascontiguousarray(f(ret_norm_g)[0].reshape(8, 128).T),
        "w_out": f(w_out)[0],
        "cos_t": _TAB["cos"].reshape(128, -1), "sin_t": _TAB["sin"].reshape(128, -1),
        "xi_t": _TAB["xi"], "zeta_t": _TAB["zeta"], "mask_r": _TAB["mask_r"], "mask_h": _TAB["mask_h"],
        "ident": _TAB["ident"],
    }
    maps = []
    for b in range(8):
        m = dict(shared)
        m["x"] = x[b]
        m["c_col"] = np.ascontiguousarray(c[b].reshape(8, 128).T)
        maps.append(m)
    return maps


def kernel(x, c, norm_g, w_ada, b_ada, w_in, hg_lb_logits, hg_norm_g, ret_norm_g, w_out, final_g):
    maps = make_in_maps(x, c, norm_g, w_ada, b_ada, w_in, hg_lb_logits, hg_norm_g, ret_norm_g, w_out, final_g)
    nc = build_program()
    res = run_bass_kernel_spmd(nc, maps, core_ids=list(range(8)))
    return np.stack([np.asarray(r["out"], dtype=np.float32) for r in res.results], 0)
```

```python
import numpy as np
import concourse.bass as bass
import concourse.mybir as mybir
from concourse.bass_utils import run_bass_kernel_spmd

F32 = mybir.dt.float32
BF16 = mybir.dt.bfloat16
AF = mybir.ActivationFunctionType
ALU = mybir.AluOpType

D = 1024
L = 4096
NH = 8
TW = 256
NT = L // TW
NSUB = TW // 128
EPS = 1e-6
WCOLS = 1152
ENGS = ("pe", "act", "dve", "pool", "sp")


class _Op:
    __slots__ = ("eng", "fn", "deps", "signal", "tok", "dma", "name")

    def __init__(self, eng, fn, dma, name):
        self.eng, self.fn, self.dma, self.name = eng, fn, dma, name
        self.deps = []
        self.signal = False
        self.tok = None


class Sched:
    def __init__(self, n_dma_sems=12):
        self.ops = {e: [] for e in ENGS}
        self.last_w = {}
        self.readers = {}
        self.n_dma_sems = n_dma_sems
        self.all_ops = []

    def add(self, eng, fn, reads=(), writes=(), dma=False, name=""):
        op = _Op(eng, fn, dma, name)
        deps = []
        for k in reads:
            w = self.last_w.get(k)
            if w is not None:
                deps.append(w)
        for k in writes:
            w = self.last_w.get(k)
            if w is not None:
                deps.append(w)
            for r in self.readers.get(k, ()):
                deps.append(r)
        seen = set()
        for d in deps:
            if d is op or id(d) in seen:
                continue
            if eng == "pe" and d.eng == "pe" and not d.dma and not dma:
                continue
            seen.add(id(d))
            op.deps.append(d)
            d.signal = True
        for k in reads:
            self.readers.setdefault(k, []).append(op)
        for k in writes:
            self.last_w[k] = op
            self.readers[k] = []
        self.ops[eng].append(op)
        self.all_ops.append(op)
        return op

    def fence(self):
        lasts = [self.ops[e][-1] for e in ENGS if self.ops[e]]
        dmas = [o for o in self.all_ops if o.dma]
        for e in ENGS:
            op = _Op(e, None, False, "fence")
            for d in lasts + dmas:
                if d.eng == e and not d.dma:
                    continue
                op.deps.append(d)
                d.signal = True
            self.ops[e].append(op)
            self.all_ops.append(op)
        self.last_w = {}
        self.readers = {}

    def emit(self, nc, block, sems, dma_sems, sw_sems):
        cnt = {e: 0 for e in ENGS}
        dcnt = [0] * len(dma_sems)
        dnext = 0
        swnext = 0
        dma_prev = {}
        for e in ENGS:
            pass
        for op in self.all_ops:
            if op.fn is None:
                continue
            if op.dma and op.eng == "pool":
                op.tok = ("w", swnext, 16)
                swnext += 1
                op.signal = True
            elif op.dma:
                i = dnext % len(dma_sems)
                dnext += 1
                dcnt[i] += 16
                op.tok = ("d", i, dcnt[i])
                prev = dma_prev.get(i)
                if prev is not None:
                    op.deps.append(prev)
                dma_prev[i] = op
                op.signal = True
            elif op.signal:
                cnt[op.eng] += 1
                op.tok = ("e", op.eng, cnt[op.eng])

        def run(engname, eng):
            waited = {}
            for op in self.ops[engname]:
                for d in op.deps:
                    t = d.tok
                    if t is None:
                        continue
                    key = (t[0], t[1])
                    if waited.get(key, 0) >= t[2]:
                        continue
                    waited[key] = t[2]
                    sem = dma_sems[t[1]] if t[0] == "d" else (sw_sems[t[1]] if t[0] == "w" else sems[t[1]])
                    eng.wait_ge(sem, t[2])
                if op.fn is None:
                    continue
                ins = op.fn(eng)
                if op.tok is not None:
                    if op.tok[0] == "d":
                        ins.then_inc(dma_sems[op.tok[1]], 16)
                    elif op.tok[0] == "w":
                        ins.then_inc(sw_sems[op.tok[1]], 16)
                    else:
                        ins.then_inc(sems[engname], 1)

        @block.tensor
        def _(eng):
            run("pe", eng)

        @block.scalar
        def _(eng):
            run("act", eng)

        @block.vector
        def _(eng):
            run("dve", eng)

        @block.gpsimd
        def _(eng):
            run("pool", eng)

        @block.sync
        def _(eng):
            run("sp", eng)


def _tables():
    try:
        import jax
        import jax.numpy as jnp
        with jax.default_device(jax.devices("cpu")[0]):
            inv_freq = 1.0 / (10000.0 ** jnp.linspace(0.0, 1.0, 32, dtype=jnp.float32))
            ang = jnp.arange(L, dtype=jnp.int32).astype(jnp.float32)[:, None] * inv_freq[None, :]
            cos = np.asarray(jnp.cos(ang), dtype=np.float32)
            sin = np.asarray(jnp.sin(ang), dtype=np.float32)
            log_gamma = np.asarray(jnp.log(1.0 - jnp.exp2(-5.0 - jnp.arange(NH, dtype=jnp.float32))), dtype=np.float32)
    except Exception:
        inv_freq = (1.0 / (np.float32(10000.0) ** np.linspace(0.0, 1.0, 32, dtype=np.float32))).astype(np.float32)
        ang = (np.arange(L, dtype=np.float32)[:, None] * inv_freq[None, :]).astype(np.float32).astype(np.float64)
        cos = np.cos(ang).astype(np.float32)
        sin = np.sin(ang).astype(np.float32)
        log_gamma = np.log(np.float32(1.0) - np.exp2(np.float32(-5.0) - np.arange(NH, dtype=np.float32))).astype(np.float32)
    cos = cos.reshape(L // 128, 128, 32).transpose(1, 0, 2)
    sin = sin.reshape(L // 128, 128, 32).transpose(1, 0, 2)
    lg = log_gamma.astype(np.float64)
    i = np.arange(128, dtype=np.float64)
    xi = np.exp(lg[None, :] * (i[:, None] + 1.0))
    zeta = np.exp(lg[None, :] * (127.0 - i[:, None])) * (64.0 ** -0.5)
    g128 = np.exp(lg * 128.0)
    ginv = np.exp(-lg * 128.0)
    s = np.arange(128)[:, None]
    t = np.arange(128)[None, :]
    mask_r = (t >= s).astype(np.float32)
    mask_h = ((t >= s) & ((t // 64) == (s // 64))).astype(np.float32)
    ident = np.eye(128, dtype=np.float32)
    return dict(cos=np.ascontiguousarray(cos), sin=np.ascontiguousarray(sin),
                xi=xi.astype(np.float32), zeta=zeta.astype(np.float32),
                g128=g128, ginv=ginv, mask_r=mask_r, mask_h=mask_h, ident=ident)


_TAB = _tables()


def build_program(dbg=None, n_heads_run=NH, n_tiles_run=NT, dbg_T=0):
    nc = bass.Bass("TRN2", target_bir_lowering=False)
    dt = lambda name, shape, dtype=F32, kind="ExternalInput": nc.dram_tensor(name, shape, dtype, kind=kind).ap()
    x_d = dt("x", [L, D])
    ccol_d = dt("c_col", [128, 8])
    wada_d = dt("w_ada", [D, 3 * D])
    bada_d = dt("b_ada", [1, 3 * D])
    ng_d = dt("norm_g", [1, D])
    fg_d = dt("final_g", [1, D])
    wp_d = dt("w_in_p", [NH, D, WCOLS])
    lbl_d = dt("lbl", [128, 16])
    ga_d = dt("hg_g", [128, 8])
    gb_d = dt("ret_g", [128, 8])
    wo_d = dt("w_out", [D, D])
    cos_d = dt("cos_t", [128, 32 * 32])
    sin_d = dt("sin_t", [128, 32 * 32])
    xi_d = dt("xi_t", [128, 8])
    zeta_d = dt("zeta_t", [128, 8])
    maskr_d = dt("mask_r", [128, 128])
    maskh_d = dt("mask_h", [128, 128])
    ident_d = dt("ident", [128, 128])
    out_d = dt("out", [L, D], kind="ExternalOutput")
    dbg_d = {}
    if dbg:
        for name, shape, dtype in dbg:
            dbg_d[name] = dt("dbg_" + name, shape, dtype, kind="ExternalOutput")

    S = Sched()
    g128 = [float(v) for v in _TAB["g128"]]
    ginv = [float(v) for v in _TAB["ginv"]]

    def bcast_rows(ap2d, n):
        return bass.AP(ap2d.tensor, ap2d.offset, [[0, 128], [1, n]])

    from contextlib import ExitStack
    with nc.cleanup_on_exit(), ExitStack() as es:
        sb = lambda name, shape, dtype=F32: es.enter_context(nc.sbuf_tensor("sb_" + name, shape, dtype))
        ps = lambda name, shape, dtype=F32: es.enter_context(nc.psum_tensor("ps_" + name, shape, dtype))
        hT = sb("hT", [128, 8, L], BF16)
        mT = sb("mT", [128, 8, L], BF16)
        gate_b = sb("gate_b", [128, D])
        identb = sb("identb", [128, 128], BF16)
        onesb = sb("onesb", [128, 128], BF16)
        maskh = sb("maskh", [128, 128])
        maskr = sb("maskr", [128, 128])
        xi_t = sb("xi_t", [128, 8])
        zeta_t = sb("zeta_t", [128, 8])
        c0 = sb("c0", [128, 8])
        c1 = sb("c1", [128, 8])
        nc1 = sb("nc1", [128, 8])
        gA = sb("gA", [128, 8])
        gB = sb("gB", [128, 8])
        zer = sb("zer", [128, 64])
        mh4 = sb("mh4", [128, 4])
        identf = sb("identf", [128, 128])
        onesf = sb("onesf", [128, 128])
        onescol = sb("onescol", [128, 2], BF16)
        PB = [ps("pb%d" % i, [128, 512]) for i in range(3)]
        PTf = ps("ptr", [128, 512])
        PT = PTf[:].bitcast(BF16)
        P4 = ps("p4", [128, 512])
        P5 = ps("p5", [128, 512])
        P6 = ps("p6", [128, 512])
        P7 = ps("p7", [128, 512])

        sems = {e: nc.alloc_semaphore("s_" + e) for e in ENGS}
        dma_sems = [nc.alloc_semaphore("d%d" % i) for i in range(S.n_dma_sems)]
        sw_sems = [nc.alloc_semaphore("w%d" % i) for i in range(2 * NH + 2)]

        def dma(eng, out, in_, reads=(), writes=(), name=""):
            return S.add(eng, lambda e: e.dma_start(out=out, in_=in_), reads, writes, dma=True, name=name)

        def dbg_out(name, src_ap, key, dst=None):
            if name in dbg_d:
                dma("sp", dbg_d[name] if dst is None else dst, src_ap, reads=[key], writes=["dbg_" + name])

        with ExitStack() as es0:
            sb0 = lambda name, shape, dtype=F32: es0.enter_context(nc.sbuf_tensor("s0_" + name, shape, dtype))
            ccol = sb0("ccol", [128, 8])
            scol = sb0("scol", [128, 8])
            screp = sb0("screp", [128, 8, 128])
            wa = [sb0("wa%d" % i, [128, 8, 256]) for i in range(2)]
            ng_b = sb0("ng_b", [128, D])
            mod_b = sb0("mod_b", [128, 2 * D])
            g1_b = sb0("g1_b", [128, D])
            lbl = sb0("lbl", [128, 16])
            dl = sb0("dl", [128, 8])
            thl = sb0("thl", [128, 8])
            graw = sb0("graw", [128, 16])
            xt = [sb0("xt%d" % i, [128, D]) for i in range(2)]
            junk = sb0("junk", [128, D], BF16)
            t1 = [sb0("t1_%d" % i, [128, D]) for i in range(2)]
            hb = [sb0("hb%d" % i, [128, D], BF16) for i in range(2)]
            ssq = sb0("ssq", [128, 2])
            rstd = sb0("rstd", [128, 2])
            mh1 = sb0("mh1", [128, 1])

            dma("sp", identf[:], ident_d, writes=["identf"])
            dma("sp", maskh[:], maskh_d, writes=["maskh"])
            dma("sp", maskr[:], maskr_d, writes=["maskr"])
            dma("sp", xi_t[:], xi_d, writes=["xi"])
            dma("sp", zeta_t[:], zeta_d, writes=["zeta"])
            dma("sp", ccol[:], ccol_d, writes=["ccol"])
            dma("sp", lbl[:], lbl_d, writes=["lbl"])
            dma("sp", graw[:, 0:8], ga_d, writes=["graw0"])
            dma("sp", graw[:, 8:16], gb_d, writes=["graw1"])
            dma("sp", mod_b[:], bcast_rows(bada_d[:, 0:2 * D], 2 * D), writes=["bias_b"])
            dma("sp", gate_b[:], bcast_rows(bada_d[:, 2 * D:3 * D], D), writes=["bias_b"])
            dma("sp", ng_b[:], bcast_rows(ng_d, D), writes=["ng_b"])

            S.add("dve", lambda e: e.tensor_copy(out=identb[:], in_=identf[:]), ["identf"], ["identb"])
            S.add("pool", lambda e: e.memset(onesb[:], 1.0), [], ["onesb"])
            S.add("pool", lambda e: e.memset(zer[:], 0.0), [], ["zer"])
            S.add("pool", lambda e: e.memset(mh4[:], -0.5), [], ["mh4"])
            S.add("pool", lambda e: e.memset(onesf[:], 1.0), [], ["onesf"])
            S.add("pool", lambda e: e.memset(onescol[:], 1.0), [], ["onescol"])
            S.add("pool", lambda e: e.memset(mh1[:], -0.5), [], ["mh1"])
            S.add("dve", lambda e: e.tensor_tensor(out=dl[:], in0=lbl[:, 0:8], in1=lbl[:, 8:16], op=ALU.subtract),
                  ["lbl"], ["dl"])
            S.add("act", lambda e: e.activation(out=thl[:], in_=dl[:], func=AF.Tanh, scale=0.5), ["dl"], ["thl"])
            S.add("dve", lambda e: e.tensor_scalar(out=c0[:], in0=thl[:], scalar1=0.25, scalar2=0.75,
                                                   op0=ALU.mult, op1=ALU.add), ["thl"], ["c0"])
            S.add("dve", lambda e: e.tensor_scalar(out=c1[:], in0=thl[:], scalar1=-0.25, scalar2=0.25,
                                                   op0=ALU.mult, op1=ALU.add), ["thl"], ["c1"])
            S.add("dve", lambda e: e.tensor_scalar(out=nc1[:], in0=thl[:], scalar1=0.25, scalar2=-0.25,
                                                   op0=ALU.mult, op1=ALU.add), ["thl"], ["nc1"])
            S.add("dve", lambda e: e.tensor_scalar(out=gA[:], in0=graw[:, 0:8], scalar1=0.5, scalar2=None,
                                                   op0=ALU.mult), ["graw0"], ["gA"])
            S.add("dve", lambda e: e.tensor_scalar(out=gB[:], in0=graw[:, 8:16], scalar1=0.5, scalar2=None,
                                                   op0=ALU.mult), ["graw1"], ["gB"])
            S.add("act", lambda e: e.activation(out=scol[:], in_=ccol[:], func=AF.Silu), ["ccol"], ["scol"])
            S.add("dve", lambda e: e.tensor_copy(out=screp[:], in_=scol[:].unsqueeze(2).to_broadcast([128, 8, 128])),
                  ["scol"], ["screp"])
            for g in range(12):
                wb = wa[g % 2]
                wk = "wa%d" % (g % 2)
                dma("sp", wb[:], wada_d[:, g * 256:(g + 1) * 256].rearrange("(j p) c -> p j c", p=128), writes=[wk])
                pb = PB[g % 3]
                pk = "pb%d" % (g % 3)
                for j in range(8):
                    S.add("pe", lambda e, j=j, wb=wb, pb=pb: e.matmul(pb[:, 0:256], lhsT=screp[:, j, :], rhs=wb[:, j, :],
                                                                     start=(j == 0), stop=(j == 7)),
                          ["screp", wk], [pk])
                if g < 8:
                    dst = mod_b[:, g * 256:(g + 1) * 256]
                else:
                    dst = gate_b[:, (g - 8) * 256:(g - 7) * 256]
                S.add("dve", lambda e, dst=dst, pb=pb: e.tensor_tensor(out=dst, in0=pb[:, 0:256], in1=dst, op=ALU.add),
                      [pk, "bias_b"], ["modg%d" % g])
            S.add("dve", lambda e: e.scalar_tensor_tensor(out=g1_b[:], in0=mod_b[:, D:2 * D], scalar=1.0, in1=ng_b[:],
                                                          op0=ALU.add, op1=ALU.mult),
                  ["modg%d" % g for g in range(4, 8)] + ["ng_b"], ["g1_b"])
            dbg_out("mod", mod_b[0:1, :], "g1_b")

            for ti in range(L // 128):
                b = ti % 2
                xk, hk = "xt%d" % b, "hb%d" % b
                dma("sp", xt[b][:], x_d[ti * 128:(ti + 1) * 128, :], writes=[xk])
                S.add("act", lambda e, b=b: e.activation(out=junk[:], in_=xt[b][:], func=AF.Square,
                                                         accum_out=ssq[:, b:b + 1]), [xk], ["junk", "ssq%d" % b])
                S.add("dve", lambda e, b=b: e.tensor_scalar(out=ssq[:, b:b + 1], in0=ssq[:, b:b + 1], scalar1=1.0 / D,
                                                            scalar2=EPS, op0=ALU.mult, op1=ALU.add),
                      ["ssq%d" % b], ["ssq%d" % b])
                S.add("pool", lambda e, b=b: e.tensor_tensor(out=rstd[:, b:b + 1], in0=ssq[:, b:b + 1], in1=mh1[:],
                                                             op=ALU.pow), ["ssq%d" % b, "mh1"], ["rstd%d" % b])
                S.add("dve", lambda e, b=b: e.scalar_tensor_tensor(out=t1[b][:], in0=xt[b][:], scalar=rstd[:, b:b + 1],
                                                                   in1=g1_b[:], op0=ALU.mult, op1=ALU.mult),
                      [xk, "rstd%d" % b, "g1_b"], ["t1_%d" % b])
                S.add("pool", lambda e, b=b: e.tensor_tensor(out=hb[b][:], in0=t1[b][:], in1=mod_b[:, 0:D], op=ALU.add),
                      ["t1_%d" % b] + ["modg%d" % g for g in range(4)], [hk])
                ptb1 = (PTf if b == 0 else PB[0])[:].bitcast(BF16)
                ptk1 = "PT1_%d" % b
                for j in range(8):
                    S.add("pe", lambda e, b=b, j=j, ptb1=ptb1: e.transpose(out=ptb1[:, j * 128:(j + 1) * 128],
                                                                           in_=hb[b][:, j * 128:(j + 1) * 128], identity=identb[:]),
                          [hk, "identb"], [ptk1])
                S.add("act", lambda e, ti=ti, ptb1=ptb1: e.activation(out=hT[:, :, ti * 128:(ti + 1) * 128],
                                                                      in_=ptb1[:].rearrange("p (j t) -> p j t", j=8), func=AF.Copy),
                      [ptk1], ["hT%d" % (ti * 128 // TW)])
            if "hT" in dbg_d:
                for j in range(8):
                    dma("sp", dbg_d["hT"][j * 128:(j + 1) * 128, :], hT[:, j, :],
                        reads=["hT%d" % t for t in range(NT)], writes=["dbg_hT"])
            S.fence()


        with ExitStack() as es2:
            sb2 = lambda name, shape, dtype=F32: es2.enter_context(nc.sbuf_tensor("s2_" + name, shape, dtype))
            D2 = lambda name, shape, dtype=F32: [sb2("%s_%d" % (name, i), shape, dtype) for i in range(2)]
            Wh = D2("wh", [128, 8, WCOLS], BF16)
            cs = D2("cs", [128, NSUB, 32]); sn = D2("sn", [128, NSUB, 32])
            th = sb2("th", [128, TW]); sq = th
            kk = sb2("kk", [128, TW]); ff = sb2("ff", [128, TW]); RR = ff
            sz = sb2("sz", [128, TW]); tga = sb2("tga", [128, TW]); srz = sb2("srz", [128, TW]); tgb = sb2("tgb", [128, TW])
            PP = D2("PP", [128, TW])
            Qt = D2("Qt", [128, TW], BF16); Kt = D2("Kt", [128, TW], BF16)
            Ktm = D2("Ktm", [128, NSUB, 2, 128], BF16)
            vAB = D2("vAB", [128, NSUB, 256], BF16)
            qkr = D2("qkr", [128, NSUB, 256], BF16)
            qkT = D2("qkT", [128, 2, TW], BF16)
            At = sb2("At", [128, NSUB, 128], BF16)
            Sc = sb2("Sc", [128, NSUB, 128], BF16)
            Sst = D2("Sst", [128, 128])
            NSB = 4
            Sbf = [sb2("Sbf%d" % i, [128, 128], BF16) for i in range(NSB)]
            NRB = 4
            Rst = D2("Rst", [128, 128])
            Rbf = [sb2("Rbf%d" % i, [128, 128], BF16) for i in range(NRB)]
            qk = sb2("qk", [128, NSUB, 128])
            ra = sb2("ra", [128, NSUB, 2, 32]); rb = sb2("rb", [128, NSUB, 2, 32])
            rc = sb2("rc", [128, NSUB, 2, 32]); rd = sb2("rd", [128, NSUB, 2, 32])
            sqoA = sb2("sqoA", [128, TW], BF16); sqoB = sb2("sqoB", [128, TW], BF16)
            uA = sb2("uA", [128, TW]); uB = sb2("uB", [128, TW])
            dgh = sb2("dgh", [128, 2 * NSUB, 128], BF16); dgl = sb2("dgl", [128, 2 * NSUB, 128], BF16)
            rsv = sb2("rsv", [128, 4])

            def load_w(h):
                hb_ = h % 2
                for half in range(2):
                    dma("pool", Wh[hb_][:, half * 4:(half + 1) * 4, :],
                        wp_d[h, half * 512:(half + 1) * 512, :].rearrange("(j p) c -> p j c", p=128),
                        writes=["wh%d_%d" % (hb_, half)])

            load_w(0)
            pbrr = [0]
            for p_ in range(2):
                S.add("pool", lambda e, p_=p_: e.memset(Ktm[p_][:], 0.0), [], ["Ktm0_%d" % p_, "Ktm1_%d" % p_])
                S.add("pool", lambda e, p_=p_: e.memset(qkT[p_][:], 0.0), [], ["qkT_%d" % p_])
                S.add("pool", lambda e, p_=p_: e.memset(qkr[p_][:], 0.0), [], ["qkr_a_%d" % p_, "qkr_b_%d" % p_])
            for i_ in range(NRB):
                S.add("pool", lambda e, i_=i_: e.memset(Rbf[i_][:], 0.0), [], ["Rbf%d" % i_])

            PB4 = PB + [PTf]
            PB4b = [b_[:].bitcast(BF16) for b_ in PB4]

            def next_pb(bf=False):
                i = pbrr[0] % 4
                pbrr[0] += 1
                return (PB4b[i] if bf else PB4[i]), "pb%d" % i

            def front(g):
                h, T = divmod(g, n_tiles_run)
                p = g % 2
                K = lambda nm: "%s_%d" % (nm, p)
                hb_ = h % 2
                W = Wh[hb_]
                whk = ["wh%d_0" % hb_, "wh%d_1" % hb_]
                hs = slice(h, h + 1)
                tok0 = T * TW
                hTk = "hT%d" % T
                dma("sp", cs[p][:], cos_d[:, T * NSUB * 32:(T + 1) * NSUB * 32].rearrange("p (s f) -> p s f", s=NSUB), writes=[K("cs")])
                dma("sp", sn[p][:], sin_d[:, T * NSUB * 32:(T + 1) * NSUB * 32].rearrange("p (s f) -> p s f", s=NSUB), writes=[K("sn")])

                def proj_fm(cblk):
                    pb, pk = next_pb()
                    for j in range(8):
                        S.add("pe", lambda e, j=j, pb=pb: e.matmul(pb[:, 0:TW], lhsT=W[:, j, cblk * 128:(cblk + 1) * 128],
                                                                   rhs=hT[:, j, tok0:tok0 + TW], start=(j == 0), stop=(j == 7)),
                              whk + [hTk], [pk])
                    return pb, pk

                pb, pk = proj_fm(1)
                S.add("act", lambda e, pb=pb: e.activation(out=th[:], in_=pb[:, 0:TW], func=AF.Tanh, scale=0.5), [pk], ["th"])
                S.add("act", lambda e: e.activation(out=kk[:], in_=th[:], func=AF.Identity, scale=nc1[:, hs], bias=c1[:, hs]), ["th"], ["kk"])
                S.add("act", lambda e: e.activation(out=ff[:], in_=th[:], func=AF.Identity, scale=c1[:, hs], bias=c0[:, hs]), ["th"], ["ff"])
                yield
                for c in range(TW // 64):
                    S.add("dve", lambda e, c=c: e.tensor_tensor_scan(out=PP[p][:, c * 64:(c + 1) * 64], data0=ff[:, c * 64:(c + 1) * 64],
                                                                    data1=zer[:, 0:64], initial=1.0, op0=ALU.mult, op1=ALU.add),
                          ["ff"], [K("PP%d" % c)])
                PPk = [K("PP%d" % c) for c in range(TW // 64)]
                S.add("dve", lambda e: e.reciprocal(out=RR[:], in_=PP[p][:]), PPk, ["ff"])
                pb, pk = proj_fm(0)
                S.add("act", lambda e, pb=pb: e.activation(out=sq[:], in_=pb[:, 0:TW], func=AF.Silu), [pk], ["th"])
                yield
                S.add("pool", lambda e: e.tensor_tensor(out=Qt[p][:], in0=sq[:], in1=PP[p][:], op=ALU.mult), ["th"] + PPk, [K("Qt")])
                S.add("pool", lambda e: e.tensor_tensor(out=Kt[p][:], in0=kk[:], in1=RR[:], op=ALU.mult), ["kk", "ff"], [K("Kt")])
                for s_ in range(NSUB):
                    pb, pk = next_pb()
                    for j in range(8):
                        S.add("pe", lambda e, j=j, pb=pb, s_=s_: e.matmul(pb[:, 0:384], lhsT=hT[:, j, tok0 + s_ * 128:tok0 + (s_ + 1) * 128],
                                                                          rhs=W[:, j, 768:1152], start=(j == 0), stop=(j == 7)),
                              whk + [hTk], [pk])
                    S.add("act", lambda e, pb=pb, s_=s_: e.activation(out=vAB[p][:, s_, :], in_=pb[:, 0:256], func=AF.Copy), [pk], [K("vAB%d" % s_)])
                    S.add("act", lambda e, pb=pb, s_=s_: e.activation(out=qk[:, s_, 0:64], in_=pb[:, 256:320], func=AF.Identity,
                                                                      scale=xi_t[:, hs]), [pk], ["qk%d" % s_])
                    S.add("act", lambda e, pb=pb, s_=s_: e.activation(out=qk[:, s_, 64:128], in_=pb[:, 320:384], func=AF.Identity,
                                                                      scale=zeta_t[:, hs]), [pk], ["qk%d" % s_])
                    yield
                ptb, ptk = next_pb(bf=True)
                for s_ in range(NSUB):
                    S.add("pe", lambda e, s_=s_, ptb=ptb: e.transpose(out=ptb[:, s_ * 128:(s_ + 1) * 128], in_=Kt[p][:, s_ * 128:(s_ + 1) * 128],
                                                                      identity=identb[:]), [K("Kt")], [ptk])
                for ci in range(2):
                    S.add("act", lambda e, ci=ci, ptb=ptb: e.activation(out=Ktm[p][ci * 64:(ci + 1) * 64, :, ci, :],
                                                                        in_=ptb[ci * 64:(ci + 1) * 64, 0:256].rearrange("p (s d) -> p s d", s=NSUB),
                                                                        func=AF.Copy), [ptk], [K("Ktm%d" % ci)])
                qk5 = qk[:].rearrange("p s (a b f) -> p s a b f", a=2, b=2)
                qa, qb = qk5[:, :, :, 0, :], qk5[:, :, :, 1, :]
                qr6 = qkr[p][:].rearrange("p s (a z b f) -> p s a z b f", a=2, z=2, b=2)
                cosb = cs[p][:].unsqueeze(2).to_broadcast([128, NSUB, 2, 32])
                sinb = sn[p][:].unsqueeze(2).to_broadcast([128, NSUB, 2, 32])
                qkk = ["qk%d" % s_ for s_ in range(NSUB)]
                S.add("dve", lambda e: e.tensor_tensor(out=ra[:], in0=qa, in1=cosb, op=ALU.mult), qkk + [K("cs")], ["ra"])
                S.add("dve", lambda e: e.tensor_tensor(out=rb[:], in0=qb, in1=sinb, op=ALU.mult), qkk + [K("sn")], ["rb"])
                S.add("dve", lambda e: e.tensor_tensor(out=qr6[:, :, :, 0, 0, :], in0=ra[:], in1=rb[:], op=ALU.subtract), ["ra", "rb"], [K("qkr_a")])
                S.add("pool", lambda e: e.tensor_tensor(out=rc[:], in0=qa, in1=sinb, op=ALU.mult), qkk + [K("sn")], ["rc"])
                S.add("pool", lambda e: e.tensor_tensor(out=rd[:], in0=qb, in1=cosb, op=ALU.mult), qkk + [K("cs")], ["rd"])
                S.add("pool", lambda e: e.tensor_tensor(out=qr6[:, :, :, 0, 1, :], in0=rc[:], in1=rd[:], op=ALU.add), ["rc", "rd"], [K("qkr_b")])
                yield

            def back(g):
                h, T = divmod(g, n_tiles_run)
                p = g % 2
                K = lambda nm: "%s_%d" % (nm, p)
                hs = slice(h, h + 1)
                tok0 = T * TW
                gc0 = g * (TW // 64)
                rn0 = g * NSUB
                PPk = [K("PP%d" % c) for c in range(TW // 64)]
                qkrk = [K("qkr_a"), K("qkr_b")]
                POb, pok = (P6, "P6") if p == 0 else (P7, "P7")

                def tap(name, ap, keys):
                    if name in dbg_d and h == 0 and T == dbg_T:
                        dma("sp", dbg_d[name], ap, reads=keys, writes=["dbg_" + name])

                import os
                if int(os.environ.get("KCUT", "99")) <= -1:
                    return
                qkrk = [K("qkr_a"), K("qkr_b")]
                ptb, ptk = next_pb(bf=True)
                for s_ in range(NSUB):
                    S.add("pe", lambda e, s_=s_, ptb=ptb: e.transpose(out=ptb[:, s_ * 128:(s_ + 1) * 128], in_=qkr[p][:, s_, 0:128],
                                                                      identity=identb[:]), qkrk, [ptk])
                    S.add("pe", lambda e, s_=s_, ptb=ptb: e.transpose(out=ptb[:, 256 + s_ * 128:256 + (s_ + 1) * 128], in_=qkr[p][:, s_, 128:256],
                                                                      identity=identb[:]), qkrk, [ptk])
                S.add("act", lambda e, ptb=ptb: e.activation(out=qkT[p][:].rearrange("p a t -> p (a t)"), in_=ptb[:, 0:512], func=AF.Copy),
                      [ptk], [K("qkT")])
                for s_ in range(NSUB if "h" not in os.environ.get("KSKIP", "") else 0):
                    S.add("pe", lambda e, s_=s_: e.matmul(P4[:, s_ * 128:(s_ + 1) * 128], lhsT=Kt[p][:, s_ * 128:(s_ + 1) * 128],
                                                          rhs=Qt[p][:, s_ * 128:(s_ + 1) * 128], start=True, stop=True),
                          [K("Kt"), K("Qt")], ["P4"])
                    S.add("dve", lambda e, s_=s_: e.tensor_tensor(out=At[:, s_, :], in0=P4[:, s_ * 128:(s_ + 1) * 128], in1=maskh[:],
                                                                  op=ALU.mult), ["P4"], ["At%d" % s_])
                def emit_ret_pre():
                  for s_ in range(NSUB):
                      S.add("pe", lambda e, s_=s_: e.matmul(P4[:, 256 + s_ * 128:256 + (s_ + 1) * 128], lhsT=qkT[p][:, 1, s_ * 128:(s_ + 1) * 128],
                                                            rhs=qkT[p][:, 0, s_ * 128:(s_ + 1) * 128], start=True, stop=True),
                            [K("qkT")], ["P4"])
                      S.add("dve", lambda e, s_=s_: e.scalar_tensor_tensor(out=Sc[:, s_, :], in0=P4[:, 256 + s_ * 128:256 + (s_ + 1) * 128],
                                                                           scalar=ginv[h], in1=maskr[:], op0=ALU.mult, op1=ALU.mult),
                            ["P4"], ["Sc%d" % s_])
                      S.add("pe", lambda e, s_=s_: e.matmul(P5[:, 256 + s_ * 128:256 + (s_ + 1) * 128], lhsT=qkr[p][:, s_, 128:256],
                                                            rhs=vAB[p][:, s_, 128:256], start=True, stop=True),
                            qkrk + [K("vAB%d" % s_)], ["P5"])
                yield

                def emit_U(cc):
                    s_, ci = divmod(cc, 2)
                    pu = cc % 2
                    S.add("pe", lambda e: e.matmul(P5[:, pu * 128:(pu + 1) * 128], lhsT=Ktm[p][:, s_, ci, :],
                                                   rhs=vAB[p][:, s_, 0:128], start=True, stop=True),
                          [K("Ktm%d" % ci), K("vAB%d" % s_)], ["P5"])

                def emit_state(cc):
                    gcn = gc0 + cc
                    pu = cc % 2
                    first = (T == 0 and cc == 0)
                    so, sn_ = gcn % 2, (gcn + 1) % 2
                    plast = PP[p][:, cc * 64 + 63:cc * 64 + 64]
                    if first:
                        S.add("dve", lambda e: e.tensor_scalar(out=Sst[sn_][:], in0=P5[:, pu * 128:(pu + 1) * 128], scalar1=plast, scalar2=None,
                                                               op0=ALU.mult), ["P5", K("PP%d" % cc)], ["Sst%d" % sn_])
                    else:
                        S.add("dve", lambda e: e.tensor_tensor(out=Sst[sn_][:], in0=P5[:, pu * 128:(pu + 1) * 128], in1=Sst[so][:], op=ALU.add),
                              ["P5", "Sst%d" % so], ["Sst%d" % sn_])
                        S.add("dve", lambda e: e.tensor_scalar(out=Sst[sn_][:], in0=Sst[sn_][:], scalar1=plast, scalar2=None, op0=ALU.mult),
                              ["Sst%d" % sn_, K("PP%d" % cc)], ["Sst%d" % sn_])
                    sl = (gcn + 1) % NSB
                    S.add("act", lambda e: e.activation(out=Sbf[sl][:], in_=Sst[sn_][:], func=AF.Copy), ["Sst%d" % sn_], ["Sbf%d" % sl])

                def emit_o(cc):
                    gcn = gc0 + cc
                    s_, ci = divmod(cc, 2)
                    first = (T == 0 and cc == 0)
                    S.add("pe", lambda e: e.matmul(POb[:, cc * 64:(cc + 1) * 64], lhsT=vAB[p][:, s_, 0:128], rhs=At[:, s_, ci * 64:(ci + 1) * 64],
                                                   start=True, stop=first), [K("vAB%d" % s_), "At%d" % s_], [pok])
                    if not first:
                        sl = gcn % NSB
                        S.add("pe", lambda e: e.matmul(POb[:, cc * 64:(cc + 1) * 64], lhsT=Sbf[sl][:], rhs=Qt[p][:, cc * 64:(cc + 1) * 64],
                                                       start=False, stop=True), ["Sbf%d" % sl, K("Qt")], [pok])

                def emit_ret(s_):
                    rnn = rn0 + s_
                    first = (T == 0 and s_ == 0)
                    ro, rnw = rnn % 2, (rnn + 1) % 2
                    S.add("pe", lambda e: e.matmul(POb[:, 256 + s_ * 128:256 + (s_ + 1) * 128], lhsT=vAB[p][:, s_, 128:256], rhs=Sc[:, s_, :],
                                                   start=True, stop=first), [K("vAB%d" % s_), "Sc%d" % s_], [pok])
                    if not first:
                        sl = rnn % NRB
                        S.add("pe", lambda e: e.matmul(POb[:, 256 + s_ * 128:256 + (s_ + 1) * 128], lhsT=Rbf[sl][:], rhs=qkT[p][:, 0, s_ * 128:(s_ + 1) * 128],
                                                       start=False, stop=True), ["Rbf%d" % sl, K("qkT")], [pok])
                        S.add("dve", lambda e: e.scalar_tensor_tensor(out=Rst[rnw][0:64, :], in0=Rst[ro][0:64, :], scalar=g128[h],
                                                                      in1=P5[0:64, 256 + s_ * 128:256 + (s_ + 1) * 128],
                                                                      op0=ALU.mult, op1=ALU.add), ["Rst%d" % ro, "P5"], ["Rst%d" % rnw])
                    else:
                        S.add("dve", lambda e: e.tensor_copy(out=Rst[rnw][0:64, :], in_=P5[0:64, 256 + s_ * 128:256 + (s_ + 1) * 128]),
                              ["P5"], ["Rst%d" % rnw])
                    sl2 = (rnn + 1) % NRB
                    S.add("act", lambda e: e.activation(out=Rbf[sl2][0:64, :], in_=Rst[rnw][0:64, :], func=AF.Copy), ["Rst%d" % rnw], ["Rbf%d" % sl2])

                import os
                CUT = int(os.environ.get("KCUT", "99"))
                if CUT <= 0:
                    return
                emit_ret_pre()
                emit_U(0); emit_U(1)
                emit_state(0)
                emit_o(0)
                yield
                emit_U(2)
                emit_state(1)
                emit_ret(0)
                yield
                emit_U(3)
                emit_o(1)
                emit_state(2)
                yield
                emit_ret(1)
                emit_o(2)
                emit_state(3)
                yield
                emit_o(3)
                yield

            def back2(g):
                h, T = divmod(g, n_tiles_run)
                p = g % 2
                hb_ = h % 2
                W = Wh[hb_]
                whk = ["wh%d_0" % hb_, "wh%d_1" % hb_]
                hs = slice(h, h + 1)
                tok0 = T * TW
                hTk = "hT%d" % T
                POb, pok = (P6, "P6") if p == 0 else (P7, "P7")

                def tap(name, ap, keys):
                    if name in dbg_d and h == 0 and T == dbg_T:
                        dma("sp", dbg_d[name], ap, reads=keys, writes=["dbg_" + name])

                def proj_gate(cblk, dst, func, key, **kw):
                    pb, pk = next_pb()
                    for j in range(8):
                        S.add("pe", lambda e, j=j, pb=pb: e.matmul(pb[:, 0:TW], lhsT=W[:, j, cblk * 128:(cblk + 1) * 128],
                                                                   rhs=hT[:, j, tok0:tok0 + TW], start=(j == 0), stop=(j == 7)),
                              whk + [hTk], [pk])
                    S.add("act", lambda e, pb=pb: e.activation(out=dst[:], in_=pb[:, 0:TW], func=func, **kw), [pk], [key])

                proj_gate(2, sz, AF.Silu, "sz")
                S.add("act", lambda e: e.activation(out=sqoA[:], in_=POb[:, 0:TW], func=AF.Square), [pok], ["sqoA"])
                S.add("act", lambda e: e.activation(out=sqoB[:], in_=POb[:, 256:256 + TW], func=AF.Square), [pok], ["sqoB"])
                yield
                ptf, ptk = next_pb()
                for bi, sqo in ((0, sqoA), (1, sqoB)):
                    for s_ in range(NSUB):
                        S.add("pe", lambda e, sqo=sqo, s_=s_, bi=bi, ptf=ptf: e.matmul(ptf[:, bi * 2 + s_:bi * 2 + s_ + 1],
                                                                                      lhsT=sqo[:, s_ * 128:(s_ + 1) * 128], rhs=onescol[:, 0:1],
                                                                                      start=True, stop=True), ["sqoA", "sqoB"], [ptk])
                S.add("dve", lambda e, ptf=ptf: e.tensor_scalar(out=rsv[:], in0=ptf[:, 0:4], scalar1=1.0 / 128.0, scalar2=EPS,
                                                                op0=ALU.mult, op1=ALU.add), [ptk], ["rsv"])
                S.add("pool", lambda e: e.tensor_tensor(out=rsv[:], in0=rsv[:], in1=mh4[:], op=ALU.pow), ["rsv"], ["rsv"])
                proj_gate(4, srz, AF.Silu, "srz")
                yield
                S.add("dve", lambda e: e.scalar_tensor_tensor(out=uA[:], in0=POb[:, 0:TW], scalar=gA[:, hs], in1=sz[:],
                                                              op0=ALU.mult, op1=ALU.mult), [pok, "sz"], ["uA"])
                S.add("dve", lambda e: e.scalar_tensor_tensor(out=uB[:], in0=POb[:, 256:256 + TW], scalar=gB[:, hs], in1=srz[:],
                                                              op0=ALU.mult, op1=ALU.mult), [pok, "srz"], ["uB"])
                for k_ in range(2 * NSUB):
                    S.add("act", lambda e, k_=k_: e.activation(out=dgh[:, k_, :], in_=identf[:], func=AF.Identity, scale=rsv[:, k_:k_ + 1]),
                          ["rsv"], ["dgh%d" % k_])
                    S.add("dve", lambda e, k_=k_: e.scalar_tensor_tensor(out=dgl[:, k_, :], in0=identf[:], scalar=rsv[:, k_:k_ + 1], in1=dgh[:, k_, :],
                                                                         op0=ALU.mult, op1=ALU.subtract), ["rsv", "dgh%d" % k_], ["dgl%d" % k_])
                proj_gate(3, tga, AF.Tanh, "tga", scale=0.5)
                yield
                pbc, pbk = next_pb()
                for k_ in range(2 * NSUB):
                    S.add("pe", lambda e, k_=k_, pbc=pbc: e.matmul(pbc[:, k_ * 128:(k_ + 1) * 128], lhsT=onesb[:], rhs=dgh[:, k_, :],
                                                                   start=True, stop=False), ["dgh%d" % k_], [pbk])
                    S.add("pe", lambda e, k_=k_, pbc=pbc: e.matmul(pbc[:, k_ * 128:(k_ + 1) * 128], lhsT=onesb[:], rhs=dgl[:, k_, :],
                                                                   start=False, stop=True), ["dgl%d" % k_], [pbk])
                proj_gate(5, tgb, AF.Tanh, "tgb", scale=0.5)
                yield
                S.add("dve", lambda e, pbc=pbc: e.tensor_tensor(out=uA[:], in0=uA[:], in1=pbc[:, 0:TW], op=ALU.mult), ["uA", pbk], ["uA"])
                S.add("dve", lambda e, pbc=pbc: e.tensor_tensor(out=uB[:], in0=uB[:], in1=pbc[:, 256:256 + TW], op=ALU.mult), ["uB", pbk], ["uB"])
                yield
                S.add("dve", lambda e: e.scalar_tensor_tensor(out=uA[:], in0=tga[:], scalar=1.0, in1=uA[:], op0=ALU.add, op1=ALU.mult),
                      ["uA", "tga"], ["uA"])
                S.add("dve", lambda e: e.scalar_tensor_tensor(out=uB[:], in0=tgb[:], scalar=1.0, in1=uB[:], op0=ALU.add, op1=ALU.mult),
                      ["uB", "tgb"], ["uB"])
                tap("uA", uA[:], ["uA"]); tap("uB", uB[:], ["uB"])
                S.add("pool", lambda e: e.tensor_tensor(out=mT[:, h, tok0:tok0 + TW], in0=uA[:], in1=uB[:], op=ALU.add), ["uA", "uB"], ["mT%d" % T])
                yield

            from itertools import zip_longest
            import os
            G = n_heads_run * n_tiles_run
            for _ in front(0):
                pass
            TLOAD = min(2, n_tiles_run - 1)
            for i in range(G + 1):
                if i < G:
                    h_i, T_i = divmod(i, n_tiles_run)
                    if T_i == TLOAD and h_i + 1 < n_heads_run:
                        load_w(h_i + 1)
                gens = []
                if i + 1 < G:
                    gens.append(front(i + 1))
                if i < G:
                    gens.append(back(i))
                if i >= 1:
                    gens.append(back2(i - 1))
                for _ in zip_longest(*gens):
                    pass
            if "mT" in dbg_d:
                for j in range(8):
                    dma("sp", dbg_d["mT"][j * 128:(j + 1) * 128, :], mT[:, j, :],
                        reads=["mT%d" % t for t in range(NT)], writes=["dbg_mT"])
            S.fence()

        with ExitStack() as es3:
            sb3 = lambda name, shape, dtype=F32: es3.enter_context(nc.sbuf_tensor("s3_" + name, shape, dtype))
            Wo = sb3("wo", [128, 8, D], BF16)
            fg_b = sb3("fg_b", [128, D])
            xt3 = [sb3("xt%d" % i, [128, D]) for i in range(2)]
            xn = [sb3("xn%d" % i, [128, D]) for i in range(2)]
            ot = [sb3("ot%d" % i, [128, D]) for i in range(2)]
            junk3 = sb3("junk", [128, D], BF16)
            ss3 = sb3("ss3", [128, 2]); rs3 = sb3("rs3", [128, 2]); mh3 = sb3("mh3", [128, 1])
            for half in range(2):
                dma("pool", Wo[:, half * 4:(half + 1) * 4, :], wo_d[half * 512:(half + 1) * 512, :].rearrange("(j p) c -> p j c", p=128),
                    writes=["wo%d" % half])
            dma("sp", fg_b[:], bcast_rows(fg_d, D), writes=["fg_b"])
            S.add("pool", lambda e: e.memset(mh3[:], -0.5), [], ["mh3"])
            for ti in range(L // 128):
                b = ti % 2
                xk = "x3_%d" % b
                dma("sp", xt3[b][:], x_d[ti * 128:(ti + 1) * 128, :], writes=[xk])
                PB3 = [PB[0], PB[1], PB[2], PTf]
                for g in range(2):
                    pb, pk = PB3[b * 2 + g], "pb3_%d" % (b * 2 + g)
                    for j in range(8):
                        S.add("pe", lambda e, j=j, g=g, pb=pb, ti=ti: e.matmul(pb[:], lhsT=mT[:, j, ti * 128:(ti + 1) * 128],
                                                                              rhs=Wo[:, j, g * 512:(g + 1) * 512], start=(j == 0), stop=(j == 7)),
                              ["wo0", "wo1", "mT"], [pk])
                    S.add("dve", lambda e, g=g, pb=pb, b=b: e.tensor_tensor(out=xn[b][:, g * 512:(g + 1) * 512], in0=pb[:],
                                                                            in1=gate_b[:, g * 512:(g + 1) * 512], op=ALU.mult),
                          [pk, "gate_b"], ["xn%d_%d" % (b, g)])
                xnk = ["xn%d_0" % b, "xn%d_1" % b]
                S.add("pool", lambda e, b=b: e.tensor_tensor(out=xn[b][:], in0=xn[b][:], in1=xt3[b][:], op=ALU.add), xnk + [xk], xnk)
                S.add("act", lambda e, b=b: e.activation(out=junk3[:], in_=xn[b][:], func=AF.Square, accum_out=ss3[:, b:b + 1]),
                      xnk, ["junk3", "ss3_%d" % b])
                S.add("dve", lambda e, b=b: e.tensor_scalar(out=ss3[:, b:b + 1], in0=ss3[:, b:b + 1], scalar1=1.0 / D, scalar2=EPS,
                                                            op0=ALU.mult, op1=ALU.add), ["ss3_%d" % b], ["ss3_%d" % b])
                S.add("pool", lambda e, b=b: e.tensor_tensor(out=rs3[:, b:b + 1], in0=ss3[:, b:b + 1], in1=mh3[:], op=ALU.pow),
                      ["ss3_%d" % b, "mh3"], ["rs3_%d" % b])
                S.add("dve", lambda e, b=b: e.scalar_tensor_tensor(out=ot[b][:], in0=xn[b][:], scalar=rs3[:, b:b + 1], in1=fg_b[:],
                                                                   op0=ALU.mult, op1=ALU.mult), xnk + ["rs3_%d" % b, "fg_b"], ["ot%d" % b])
                dma("sp", out_d[ti * 128:(ti + 1) * 128, :], ot[b][:], reads=["ot%d" % b], writes=["out"])

        S.fence()
        for sm in list(sems.values()) + dma_sems + sw_sems:
            nc.gpsimd.sem_clear(sm)
        nc.all_engine_barrier()
        with nc.Block() as block:
            S.emit(nc, block, sems, dma_sems, sw_sems)
        nc.all_engine_barrier()
    return nc


def _perm_cols():
    offs = np.cumsum([0, 1024, 1024, 1024, 1024, 512, 512, 1024, 1024, 1024, 1024])
    o_hq, o_hf, o_hi, o_hz, o_rq, o_rk, o_rv, o_rz, o_ga, o_gb = offs[:10]
    cols = []
    for h in range(NH):
        c = []
        for o in (o_hq, o_hf, o_hz, o_ga, o_rz, o_gb, o_hi, o_rv):
            c += list(range(o + h * 128, o + (h + 1) * 128))
        c += list(range(o_rq + h * 64, o_rq + (h + 1) * 64))
        c += list(range(o_rk + h * 64, o_rk + (h + 1) * 64))
        cols.append(c)
    return np.array(cols)


def make_in_maps(x, c, norm_g, w_ada, b_ada, w_in, hg_lb_logits, hg_norm_g, ret_norm_g, w_out, final_g):
    f = lambda a: np.ascontiguousarray(np.asarray(a, dtype=np.float32))
    x, c = f(x), f(c)
    cols = _perm_cols()
    w_in0 = f(w_in)[0]
    wp = np.ascontiguousarray(np.stack([w_in0[:, cols[h]] for h in range(NH)], 0))
    lb = f(hg_lb_logits)
    lbl = np.ascontiguousarray(np.concatenate([lb[0].reshape(8, 128).T, lb[1].reshape(8, 128).T], 1))
    shared = {
        "w_ada": f(w_ada)[0], "b_ada": f(b_ada)[0].reshape(1, -1), "norm_g": f(norm_g)[0].reshape(1, -1),
        "final_g": f(final_g).reshape(1, -1), "w_in_p": wp, "lbl": lbl,
        "hg_g": np.ascontiguousarray(f(hg_norm_g)[0].reshape(8, 128).T),
        "ret_g": np.ascontiguousarray(f(ret_norm_g)[0].reshape(8, 128).T),
        "w_out": f(w_out)[0],
        "cos_t": _TAB["cos"].reshape(128, -1), "sin_t": _TAB["sin"].reshape(128, -1),
        "xi_t": _TAB["xi"], "zeta_t": _TAB["zeta"], "mask_r": _TAB["mask_r"], "mask_h": _TAB["mask_h"],
        "ident": _TAB["ident"],
    }
    maps = []
    for b in range(8):
        m = dict(shared)
        m["x"] = x[b]
        m["c_col"] = np.ascontiguousarray(c[b].reshape(8, 128).T)
        maps.append(m)
    return maps


def kernel(x, c, norm_g, w_ada, b_ada, w_in, hg_lb_logits, hg_norm_g, ret_norm_g, w_out, final_g):
    maps = make_in_maps(x, c, norm_g, w_ada, b_ada, w_in, hg_lb_logits, hg_norm_g, ret_norm_g, w_out, final_g)
    nc = build_program()
    res = run_bass_kernel_spmd(nc, maps, core_ids=list(range(8)))
    return np.stack([np.asarray(r["out"], dtype=np.float32) for r in res.results], 0)
```

```python
import numpy as np
import concourse.bass as bass
import concourse.mybir as mybir
from concourse.bass_utils import run_bass_kernel_spmd

F32 = mybir.dt.float32
BF16 = mybir.dt.bfloat16
AF = mybir.ActivationFunctionType
ALU = mybir.AluOpType

D = 1024
L = 4096
NH = 8
TW = 256
NT = L // TW
NSUB = TW // 128
EPS = 1e-6
WCOLS = 1152
ENGS = ("pe", "act", "dve", "pool", "sp")


class _Op:
    __slots__ = ("eng", "fn", "deps", "signal", "tok", "dma", "name")

    def __init__(self, eng, fn, dma, name):
        self.eng, self.fn, self.dma, self.name = eng, fn, dma, name
        self.deps = []
        self.signal = False
        self.tok = None


class Sched:
    def __init__(self, n_dma_sems=12):
        self.ops = {e: [] for e in ENGS}
        self.last_w = {}
        self.readers = {}
        self.n_dma_sems = n_dma_sems
        self.all_ops = []

    def add(self, eng, fn, reads=(), writes=(), dma=False, name=""):
        op = _Op(eng, fn, dma, name)
        deps = []
        for k in reads:
            w = self.last_w.get(k)
            if w is not None:
                deps.append(w)
        for k in writes:
            w = self.last_w.get(k)
            if w is not None:
                deps.append(w)
            for r in self.readers.get(k, ()):
                deps.append(r)
        seen = set()
        for d in deps:
            if d is op or id(d) in seen:
                continue
            if eng == "pe" and d.eng == "pe" and not d.dma and not dma:
                continue
            seen.add(id(d))
            op.deps.append(d)
            d.signal = True
        for k in reads:
            self.readers.setdefault(k, []).append(op)
        for k in writes:
            self.last_w[k] = op
            self.readers[k] = []
        self.ops[eng].append(op)
        self.all_ops.append(op)
        return op

    def fence(self):
        lasts = [self.ops[e][-1] for e in ENGS if self.ops[e]]
        dmas = [o for o in self.all_ops if o.dma]
        for e in ENGS:
            op = _Op(e, None, False, "fence")
            for d in lasts + dmas:
                if d.eng == e and not d.dma:
                    continue
                op.deps.append(d)
                d.signal = True
            self.ops[e].append(op)
            self.all_ops.append(op)
        self.last_w = {}
        self.readers = {}

    def emit(self, nc, block, sems, dma_sems, sw_sems):
        cnt = {e: 0 for e in ENGS}
        dcnt = [0] * len(dma_sems)
        dnext = 0
        swnext = 0
        dma_prev = {}
        for e in ENGS:
            pass
        for op in self.all_ops:
            if op.fn is None:
                continue
            if op.dma and op.eng == "pool":
                op.tok = ("w", swnext, 16)
                swnext += 1
                op.signal = True
            elif op.dma:
                i = dnext % len(dma_sems)
                dnext += 1
                dcnt[i] += 16
                op.tok = ("d", i, dcnt[i])
                prev = dma_prev.get(i)
                if prev is not None:
                    op.deps.append(prev)
                dma_prev[i] = op
                op.signal = True
            elif op.signal:
                cnt[op.eng] += 1
                op.tok = ("e", op.eng, cnt[op.eng])

        def run(engname, eng):
            waited = {}
            for op in self.ops[engname]:
                for d in op.deps:
                    t = d.tok
                    if t is None:
                        continue
                    key = (t[0], t[1])
                    if waited.get(key, 0) >= t[2]:
                        continue
                    waited[key] = t[2]
                    sem = dma_sems[t[1]] if t[0] == "d" else (sw_sems[t[1]] if t[0] == "w" else sems[t[1]])
                    eng.wait_ge(sem, t[2])
                if op.fn is None:
                    continue
                ins = op.fn(eng)
                if op.tok is not None:
                    if op.tok[0] == "d":
                        ins.then_inc(dma_sems[op.tok[1]], 16)
                    elif op.tok[0] == "w":
                        ins.then_inc(sw_sems[op.tok[1]], 16)
                    else:
                        ins.then_inc(sems[engname], 1)

        @block.tensor
        def _(eng):
            run("pe", eng)

        @block.scalar
        def _(eng):
            run("act", eng)

        @block.vector
        def _(eng):
            run("dve", eng)

        @block.gpsimd
        def _(eng):
            run("pool", eng)

        @block.sync
        def _(eng):
            run("sp", eng)


def _tables():
    try:
        import jax
        import jax.numpy as jnp
        with jax.default_device(jax.devices("cpu")[0]):
            inv_freq = 1.0 / (10000.0 ** jnp.linspace(0.0, 1.0, 32, dtype=jnp.float32))
            ang = jnp.arange(L, dtype=jnp.int32).astype(jnp.float32)[:, None] * inv_freq[None, :]
            cos = np.asarray(jnp.cos(ang), dtype=np.float32)
            sin = np.asarray(jnp.sin(ang), dtype=np.float32)
            log_gamma = np.asarray(jnp.log(1.0 - jnp.exp2(-5.0 - jnp.arange(NH, dtype=jnp.float32))), dtype=np.float32)
    except Exception:
        inv_freq = (1.0 / (np.float32(10000.0) ** np.linspace(0.0, 1.0, 32, dtype=np.float32))).astype(np.float32)
        ang = (np.arange(L, dtype=np.float32)[:, None] * inv_freq[None, :]).astype(np.float32).astype(np.float64)
        cos = np.cos(ang).astype(np.float32)
        sin = np.sin(ang).astype(np.float32)
        log_gamma = np.log(np.float32(1.0) - np.exp2(np.float32(-5.0) - np.arange(NH, dtype=np.float32))).astype(np.float32)
    cos = cos.reshape(L // 128, 128, 32).transpose(1, 0, 2)
    sin = sin.reshape(L // 128, 128, 32).transpose(1, 0, 2)
    lg = log_gamma.astype(np.float64)
    i = np.arange(128, dtype=np.float64)
    xi = np.exp(lg[None, :] * (i[:, None] + 1.0))
    zeta = np.exp(lg[None, :] * (127.0 - i[:, None])) * (64.0 ** -0.5)
    g128 = np.exp(lg * 128.0)
    ginv = np.exp(-lg * 128.0)
    s = np.arange(128)[:, None]
    t = np.arange(128)[None, :]
    mask_r = (t >= s).astype(np.float32)
    mask_h = ((t >= s) & ((t // 64) == (s // 64))).astype(np.float32)
    ident = np.eye(128, dtype=np.float32)
    return dict(cos=np.ascontiguousarray(cos), sin=np.ascontiguousarray(sin),
                xi=xi.astype(np.float32), zeta=zeta.astype(np.float32),
                g128=g128, ginv=ginv, mask_r=mask_r, mask_h=mask_h, ident=ident)


_TAB = _tables()


def build_program(dbg=None, n_heads_run=NH, n_tiles_run=NT, dbg_T=0):
    nc = bass.Bass("TRN2", target_bir_lowering=False)
    dt = lambda name, shape, dtype=F32, kind="ExternalInput": nc.dram_tensor(name, shape, dtype, kind=kind).ap()
    x_d = dt("x", [L, D])
    ccol_d = dt("c_col", [128, 8])
    wada_d = dt("w_ada", [D, 3 * D])
    bada_d = dt("b_ada", [1, 3 * D])
    ng_d = dt("norm_g", [1, D])
    fg_d = dt("final_g", [1, D])
    wp_d = dt("w_in_p", [NH, D, WCOLS])
    lbl_d = dt("lbl", [128, 16])
    ga_d = dt("hg_g", [128, 8])
    gb_d = dt("ret_g", [128, 8])
    wo_d = dt("w_out", [D, D])
    cos_d = dt("cos_t", [128, 32 * 32])
    sin_d = dt("sin_t", [128, 32 * 32])
    xi_d = dt("xi_t", [128, 8])
    zeta_d = dt("zeta_t", [128, 8])
    maskr_d = dt("mask_r", [128, 128])
    maskh_d = dt("mask_h", [128, 128])
    ident_d = dt("ident", [128, 128])
    out_d = dt("out", [L, D], kind="ExternalOutput")
    dbg_d = {}
    if dbg:
        for name, shape, dtype in dbg:
            dbg_d[name] = dt("dbg_" + name, shape, dtype, kind="ExternalOutput")

    S = Sched()
    g128 = [float(v) for v in _TAB["g128"]]
    ginv = [float(v) for v in _TAB["ginv"]]

    def bcast_rows(ap2d, n):
        return bass.AP(ap2d.tensor, ap2d.offset, [[0, 128], [1, n]])

    from contextlib import ExitStack
    with nc.cleanup_on_exit(), ExitStack() as es:
        sb = lambda name, shape, dtype=F32: es.enter_context(nc.sbuf_tensor("sb_" + name, shape, dtype))
        ps = lambda name, shape, dtype=F32: es.enter_context(nc.psum_tensor("ps_" + name, shape, dtype))
        hT = sb("hT", [128, 8, L], BF16)
        mT = sb("mT", [128, 8, L], BF16)
        gate_b = sb("gate_b", [128, D])
        identb = sb("identb", [128, 128], BF16)
        onesb = sb("onesb", [128, 128], BF16)
        maskh = sb("maskh", [128, 128])
        maskr = sb("maskr", [128, 128])
        xi_t = sb("xi_t", [128, 8])
        zeta_t = sb("zeta_t", [128, 8])
        c0 = sb("c0", [128, 8])
        c1 = sb("c1", [128, 8])
        nc1 = sb("nc1", [128, 8])
        gA = sb("gA", [128, 8])
        gB = sb("gB", [128, 8])
        zer = sb("zer", [128, 64])
        mh4 = sb("mh4", [128, 4])
        identf = sb("identf", [128, 128])
        onesf = sb("onesf", [128, 128])
        onescol = sb("onescol", [128, 2], BF16)
        PB = [ps("pb%d" % i, [128, 512]) for i in range(3)]
        PTf = ps("ptr", [128, 512])
        PT = PTf[:].bitcast(BF16)
        P4 = ps("p4", [128, 512])
        P5 = ps("p5", [128, 512])
        P6 = ps("p6", [128, 512])
        P7 = ps("p7", [128, 512])

        sems = {e: nc.alloc_semaphore("s_" + e) for e in ENGS}
        dma_sems = [nc.alloc_semaphore("d%d" % i) for i in range(S.n_dma_sems)]
        sw_sems = [nc.alloc_semaphore("w%d" % i) for i in range(2 * NH + 2)]

        def dma(eng, out, in_, reads=(), writes=(), name=""):
            return S.add(eng, lambda e: e.dma_start(out=out, in_=in_), reads, writes, dma=True, name=name)

        def dbg_out(name, src_ap, key, dst=None):
            if name in dbg_d:
                dma("sp", dbg_d[name] if dst is None else dst, src_ap, reads=[key], writes=["dbg_" + name])

        with ExitStack() as es0:
            sb0 = lambda name, shape, dtype=F32: es0.enter_context(nc.sbuf_tensor("s0_" + name, shape, dtype))
            ccol = sb0("ccol", [128, 8])
            scol = sb0("scol", [128, 8])
            screp = sb0("screp", [128, 8, 128])
            wa = [sb0("wa%d" % i, [128, 8, 256]) for i in range(2)]
            ng_b = sb0("ng_b", [128, D])
            mod_b = sb0("mod_b", [128, 2 * D])
            g1_b = sb0("g1_b", [128, D])
            lbl = sb0("lbl", [128, 16])
            dl = sb0("dl", [128, 8])
            thl = sb0("thl", [128, 8])
            graw = sb0("graw", [128, 16])
            xt = [sb0("xt%d" % i, [128, D]) for i in range(2)]
            junk = sb0("junk", [128, D], BF16)
            t1 = [sb0("t1_%d" % i, [128, D]) for i in range(2)]
            hb = [sb0("hb%d" % i, [128, D], BF16) for i in range(2)]
            ssq = sb0("ssq", [128, 2])
            rstd = sb0("rstd", [128, 2])
            mh1 = sb0("mh1", [128, 1])

            dma("sp", identf[:], ident_d, writes=["identf"])
            dma("sp", maskh[:], maskh_d, writes=["maskh"])
            dma("sp", maskr[:], maskr_d, writes=["maskr"])
            dma("sp", xi_t[:], xi_d, writes=["xi"])
            dma("sp", zeta_t[:], zeta_d, writes=["zeta"])
            dma("sp", ccol[:], ccol_d, writes=["ccol"])
            dma("sp", lbl[:], lbl_d, writes=["lbl"])
            dma("sp", graw[:, 0:8], ga_d, writes=["graw0"])
            dma("sp", graw[:, 8:16], gb_d, writes=["graw1"])
            dma("sp", mod_b[:], bcast_rows(bada_d[:, 0:2 * D], 2 * D), writes=["bias_b"])
            dma("sp", gate_b[:], bcast_rows(bada_d[:, 2 * D:3 * D], D), writes=["bias_b"])
            dma("sp", ng_b[:], bcast_rows(ng_d, D), writes=["ng_b"])

            S.add("dve", lambda e: e.tensor_copy(out=identb[:], in_=identf[:]), ["identf"], ["identb"])
            S.add("pool", lambda e: e.memset(onesb[:], 1.0), [], ["onesb"])
            S.add("pool", lambda e: e.memset(zer[:], 0.0), [], ["zer"])
            S.add("pool", lambda e: e.memset(mh4[:], -0.5), [], ["mh4"])
            S.add("pool", lambda e: e.memset(onesf[:], 1.0), [], ["onesf"])
            S.add("pool", lambda e: e.memset(onescol[:], 1.0), [], ["onescol"])
            S.add("pool", lambda e: e.memset(mh1[:], -0.5), [], ["mh1"])
            S.add("dve", lambda e: e.tensor_tensor(out=dl[:], in0=lbl[:, 0:8], in1=lbl[:, 8:16], op=ALU.subtract),
                  ["lbl"], ["dl"])
            S.add("act", lambda e: e.activation(out=thl[:], in_=dl[:], func=AF.Tanh, scale=0.5), ["dl"], ["thl"])
            S.add("dve", lambda e: e.tensor_scalar(out=c0[:], in0=thl[:], scalar1=0.25, scalar2=0.75,
                                                   op0=ALU.mult, op1=ALU.add), ["thl"], ["c0"])
            S.add("dve", lambda e: e.tensor_scalar(out=c1[:], in0=thl[:], scalar1=-0.25, scalar2=0.25,
                                                   op0=ALU.mult, op1=ALU.add), ["thl"], ["c1"])
            S.add("dve", lambda e: e.tensor_scalar(out=nc1[:], in0=thl[:], scalar1=0.25, scalar2=-0.25,
                                                   op0=ALU.mult, op1=ALU.add), ["thl"], ["nc1"])
            S.add("dve", lambda e: e.tensor_scalar(out=gA[:], in0=graw[:, 0:8], scalar1=0.5, scalar2=None,
                                                   op0=ALU.mult), ["graw0"], ["gA"])
            S.add("dve", lambda e: e.tensor_scalar(out=gB[:], in0=graw[:, 8:16], scalar1=0.5, scalar2=None,
                                                   op0=ALU.mult), ["graw1"], ["gB"])
            S.add("act", lambda e: e.activation(out=scol[:], in_=ccol[:], func=AF.Silu), ["ccol"], ["scol"])
            S.add("dve", lambda e: e.tensor_copy(out=screp[:], in_=scol[:].unsqueeze(2).to_broadcast([128, 8, 128])),
                  ["scol"], ["screp"])
            for g in range(12):
                wb = wa[g % 2]
                wk = "wa%d" % (g % 2)
                dma("sp", wb[:], wada_d[:, g * 256:(g + 1) * 256].rearrange("(j p) c -> p j c", p=128), writes=[wk])
                pb = PB[g % 3]
                pk = "pb%d" % (g % 3)
                for j in range(8):
                    S.add("pe", lambda e, j=j, wb=wb, pb=pb: e.matmul(pb[:, 0:256], lhsT=screp[:, j, :], rhs=wb[:, j, :],
                                                                     start=(j == 0), stop=(j == 7)),
                          ["screp", wk], [pk])
                if g < 8:
                    dst = mod_b[:, g * 256:(g + 1) * 256]
                else:
                    dst = gate_b[:, (g - 8) * 256:(g - 7) * 256]
                S.add("dve", lambda e, dst=dst, pb=pb: e.tensor_tensor(out=dst, in0=pb[:, 0:256], in1=dst, op=ALU.add),
                      [pk, "bias_b"], ["modg%d" % g])
            S.add("dve", lambda e: e.scalar_tensor_tensor(out=g1_b[:], in0=mod_b[:, D:2 * D], scalar=1.0, in1=ng_b[:],
                                                          op0=ALU.add, op1=ALU.mult),
                  ["modg%d" % g for g in range(4, 8)] + ["ng_b"], ["g1_b"])
            dbg_out("mod", mod_b[0:1, :], "g1_b")

            for ti in range(L // 128):
                b = ti % 2
                xk, hk = "xt%d" % b, "hb%d" % b
                dma("sp", xt[b][:], x_d[ti * 128:(ti + 1) * 128, :], writes=[xk])
                S.add("act", lambda e, b=b: e.activation(out=junk[:], in_=xt[b][:], func=AF.Square,
                                                         accum_out=ssq[:, b:b + 1]), [xk], ["junk", "ssq%d" % b])
                S.add("dve", lambda e, b=b: e.tensor_scalar(out=ssq[:, b:b + 1], in0=ssq[:, b:b + 1], scalar1=1.0 / D,
                                                            scalar2=EPS, op0=ALU.mult, op1=ALU.add),
                      ["ssq%d" % b], ["ssq%d" % b])
                S.add("pool", lambda e, b=b: e.tensor_tensor(out=rstd[:, b:b + 1], in0=ssq[:, b:b + 1], in1=mh1[:],
                                                             op=ALU.pow), ["ssq%d" % b, "mh1"], ["rstd%d" % b])
                S.add("dve", lambda e, b=b: e.scalar_tensor_tensor(out=t1[b][:], in0=xt[b][:], scalar=rstd[:, b:b + 1],
                                                                   in1=g1_b[:], op0=ALU.mult, op1=ALU.mult),
                      [xk, "rstd%d" % b, "g1_b"], ["t1_%d" % b])
                S.add("pool", lambda e, b=b: e.tensor_tensor(out=hb[b][:], in0=t1[b][:], in1=mod_b[:, 0:D], op=ALU.add),
                      ["t1_%d" % b] + ["modg%d" % g for g in range(4)], [hk])
                ptb1 = (PTf if b == 0 else PB[0])[:].bitcast(BF16)
                ptk1 = "PT1_%d" % b
                for j in range(8):
                    S.add("pe", lambda e, b=b, j=j, ptb1=ptb1: e.transpose(out=ptb1[:, j * 128:(j + 1) * 128],
                                                                           in_=hb[b][:, j * 128:(j + 1) * 128], identity=identb[:]),
                          [hk, "identb"], [ptk1])
                S.add("act", lambda e, ti=ti, ptb1=ptb1: e.activation(out=hT[:, :, ti * 128:(ti + 1) * 128],
                                                                      in_=ptb1[:].rearrange("p (j t) -> p j t", j=8), func=AF.Copy),
                      [ptk1], ["hT%d" % (ti * 128 // TW)])
            if "hT" in dbg_d:
                for j in range(8):
                    dma("sp", dbg_d["hT"][j * 128:(j + 1) * 128, :], hT[:, j, :],
                        reads=["hT%d" % t for t in range(NT)], writes=["dbg_hT"])
            S.fence()


        with ExitStack() as es2:
            sb2 = lambda name, shape, dtype=F32: es2.enter_context(nc.sbuf_tensor("s2_" + name, shape, dtype))
            D2 = lambda name, shape, dtype=F32: [sb2("%s_%d" % (name, i), shape, dtype) for i in range(2)]
            Wh = D2("wh", [128, 8, WCOLS], BF16)
            cs = D2("cs", [128, NSUB, 32]); sn = D2("sn", [128, NSUB, 32])
            th = sb2("th", [128, TW]); sq = th
            kk = sb2("kk", [128, TW]); ff = sb2("ff", [128, TW]); RR = ff
            sz = sb2("sz", [128, TW]); tga = sb2("tga", [128, TW]); srz = sb2("srz", [128, TW]); tgb = sb2("tgb", [128, TW])
            PP = D2("PP", [128, TW])
            Qt = D2("Qt", [128, TW], BF16); Kt = D2("Kt", [128, TW], BF16)
            Ktm = D2("Ktm", [128, NSUB, 2, 128], BF16)
            vAB = D2("vAB", [128, NSUB, 256], BF16)
            qkr = D2("qkr", [128, NSUB, 256], BF16)
            qkT = D2("qkT", [128, 2, TW], BF16)
            At = sb2("At", [128, NSUB, 128], BF16)
            Sc = sb2("Sc", [128, NSUB, 128], BF16)
            Sst = D2("Sst", [128, 128])
            NSB = 4
            Sbf = [sb2("Sbf%d" % i, [128, 128], BF16) for i in range(NSB)]
            NRB = 4
            Rst = D2("Rst", [128, 128])
            Rbf = [sb2("Rbf%d" % i, [128, 128], BF16) for i in range(NRB)]
            qk = sb2("qk", [128, NSUB, 128])
            ra = sb2("ra", [128, NSUB, 2, 32]); rb = sb2("rb", [128, NSUB, 2, 32])
            rc = sb2("rc", [128, NSUB, 2, 32]); rd = sb2("rd", [128, NSUB, 2, 32])
            sqoA = sb2("sqoA", [128, TW], BF16); sqoB = sb2("sqoB", [128, TW], BF16)
            uA = sb2("uA", [128, TW]); uB = sb2("uB", [128, TW])
            dgh = sb2("dgh", [128, 2 * NSUB, 128], BF16); dgl = sb2("dgl", [128, 2 * NSUB, 128], BF16)
            rsv = sb2("rsv", [128, 4])

            def load_w(h):
                hb_ = h % 2
                for half in range(2):
                    dma("pool", Wh[hb_][:, half * 4:(half + 1) * 4, :],
                        wp_d[h, half * 512:(half + 1) * 512, :].rearrange("(j p) c -> p j c", p=128),
                        writes=["wh%d_%d" % (hb_, half)])

            load_w(0)
            pbrr = [0]
            for p_ in range(2):
                S.add("pool", lambda e, p_=p_: e.memset(Ktm[p_][:], 0.0), [], ["Ktm0_%d" % p_, "Ktm1_%d" % p_])
                S.add("pool", lambda e, p_=p_: e.memset(qkT[p_][:], 0.0), [], ["qkT_%d" % p_])
                S.add("pool", lambda e, p_=p_: e.memset(qkr[p_][:], 0.0), [], ["qkr_a_%d" % p_, "qkr_b_%d" % p_])
            for i_ in range(NRB):
                S.add("pool", lambda e, i_=i_: e.memset(Rbf[i_][:], 0.0), [], ["Rbf%d" % i_])

            PB4 = PB + [PTf]
            PB4b = [b_[:].bitcast(BF16) for b_ in PB4]

            def next_pb(bf=False):
                i = pbrr[0] % 4
                pbrr[0] += 1
                return (PB4b[i] if bf else PB4[i]), "pb%d" % i

            def front(g):
                h, T = divmod(g, n_tiles_run)
                p = g % 2
                K = lambda nm: "%s_%d" % (nm, p)
                hb_ = h % 2
                W = Wh[hb_]
                whk = ["wh%d_0" % hb_, "wh%d_1" % hb_]
                hs = slice(h, h + 1)
                tok0 = T * TW
                hTk = "hT%d" % T
                dma("sp", cs[p][:], cos_d[:, T * NSUB * 32:(T + 1) * NSUB * 32].rearrange("p (s f) -> p s f", s=NSUB), writes=[K("cs")])
                dma("sp", sn[p][:], sin_d[:, T * NSUB * 32:(T + 1) * NSUB * 32].rearrange("p (s f) -> p s f", s=NSUB), writes=[K("sn")])

                def proj_fm(cblk):
                    pb, pk = next_pb()
                    for j in range(8):
                        S.add("pe", lambda e, j=j, pb=pb: e.matmul(pb[:, 0:TW], lhsT=W[:, j, cblk * 128:(cblk + 1) * 128],
                                                                   rhs=hT[:, j, tok0:tok0 + TW], start=(j == 0), stop=(j == 7)),
                              whk + [hTk], [pk])
                    return pb, pk

                pb, pk = proj_fm(1)
                S.add("act", lambda e, pb=pb: e.activation(out=th[:], in_=pb[:, 0:TW], func=AF.Tanh, scale=0.5), [pk], ["th"])
                S.add("act", lambda e: e.activation(out=kk[:], in_=th[:], func=AF.Identity, scale=nc1[:, hs], bias=c1[:, hs]), ["th"], ["kk"])
                S.add("act", lambda e: e.activation(out=ff[:], in_=th[:], func=AF.Identity, scale=c1[:, hs], bias=c0[:, hs]), ["th"], ["ff"])
                yield
                for c in range(TW // 64):
                    S.add("dve", lambda e, c=c: e.tensor_tensor_scan(out=PP[p][:, c * 64:(c + 1) * 64], data0=ff[:, c * 64:(c + 1) * 64],
                                                                    data1=zer[:, 0:64], initial=1.0, op0=ALU.mult, op1=ALU.add),
                          ["ff"], [K("PP%d" % c)])
                PPk = [K("PP%d" % c) for c in range(TW // 64)]
                S.add("dve", lambda e: e.reciprocal(out=RR[:], in_=PP[p][:]), PPk, ["ff"])
                pb, pk = proj_fm(0)
                S.add("act", lambda e, pb=pb: e.activation(out=sq[:], in_=pb[:, 0:TW], func=AF.Silu), [pk], ["th"])
                yield
                S.add("pool", lambda e: e.tensor_tensor(out=Qt[p][:], in0=sq[:], in1=PP[p][:], op=ALU.mult), ["th"] + PPk, [K("Qt")])
                S.add("pool", lambda e: e.tensor_tensor(out=Kt[p][:], in0=kk[:], in1=RR[:], op=ALU.mult), ["kk", "ff"], [K("Kt")])
                for s_ in range(NSUB):
                    pb, pk = next_pb()
                    for j in range(8):
                        S.add("pe", lambda e, j=j, pb=pb, s_=s_: e.matmul(pb[:, 0:384], lhsT=hT[:, j, tok0 + s_ * 128:tok0 + (s_ + 1) * 128],
                                                                          rhs=W[:, j, 768:1152], start=(j == 0), stop=(j == 7)),
                              whk + [hTk], [pk])
                    S.add("act", lambda e, pb=pb, s_=s_: e.activation(out=vAB[p][:, s_, :], in_=pb[:, 0:256], func=AF.Copy), [pk], [K("vAB%d" % s_)])
                    S.add("act", lambda e, pb=pb, s_=s_: e.activation(out=qk[:, s_, 0:64], in_=pb[:, 256:320], func=AF.Identity,
                                                                      scale=xi_t[:, hs]), [pk], ["qk%d" % s_])
                    S.add("act", lambda e, pb=pb, s_=s_: e.activation(out=qk[:, s_, 64:128], in_=pb[:, 320:384], func=AF.Identity,
                                                                      scale=zeta_t[:, hs]), [pk], ["qk%d" % s_])
                    yield
                ptb, ptk = next_pb(bf=True)
                for s_ in range(NSUB):
                    S.add("pe", lambda e, s_=s_, ptb=ptb: e.transpose(out=ptb[:, s_ * 128:(s_ + 1) * 128], in_=Kt[p][:, s_ * 128:(s_ + 1) * 128],
                                                                      identity=identb[:]), [K("Kt")], [ptk])
                for ci in range(2):
                    S.add("act", lambda e, ci=ci, ptb=ptb: e.activation(out=Ktm[p][ci * 64:(ci + 1) * 64, :, ci, :],
                                                                        in_=ptb[ci * 64:(ci + 1) * 64, 0:256].rearrange("p (s d) -> p s d", s=NSUB),
                                                                        func=AF.Copy), [ptk], [K("Ktm%d" % ci)])
                qk5 = qk[:].rearrange("p s (a b f) -> p s a b f", a=2, b=2)
                qa, qb = qk5[:, :, :, 0, :], qk5[:, :, :, 1, :]
                qr6 = qkr[p][:].rearrange("p s (a z b f) -> p s a z b f", a=2, z=2, b=2)
                cosb = cs[p][:].unsqueeze(2).to_broadcast([128, NSUB, 2, 32])
                sinb = sn[p][:].unsqueeze(2).to_broadcast([128, NSUB, 2, 32])
                qkk = ["qk%d" % s_ for s_ in range(NSUB)]
                S.add("dve", lambda e: e.tensor_tensor(out=ra[:], in0=qa, in1=cosb, op=ALU.mult), qkk + [K("cs")], ["ra"])
                S.add("dve", lambda e: e.tensor_tensor(out=rb[:], in0=qb, in1=sinb, op=ALU.mult), qkk + [K("sn")], ["rb"])
                S.add("dve", lambda e: e.tensor_tensor(out=qr6[:, :, :, 0, 0, :], in0=ra[:], in1=rb[:], op=ALU.subtract), ["ra", "rb"], [K("qkr_a")])
                S.add("pool", lambda e: e.tensor_tensor(out=rc[:], in0=qa, in1=sinb, op=ALU.mult), qkk + [K("sn")], ["rc"])
                S.add("pool", lambda e: e.tensor_tensor(out=rd[:], in0=qb, in1=cosb, op=ALU.mult), qkk + [K("cs")], ["rd"])
                S.add("pool", lambda e: e.tensor_tensor(out=qr6[:, :, :, 0, 1, :], in0=rc[:], in1=rd[:], op=ALU.add), ["rc", "rd"], [K("qkr_b")])
                yield

            def back(g):
                h, T = divmod(g, n_tiles_run)
                p = g % 2
                K = lambda nm: "%s_%d" % (nm, p)
                hs = slice(h, h + 1)
                tok0 = T * TW
                gc0 = g * (TW // 64)
                rn0 = g * NSUB
                PPk = [K("PP%d" % c) for c in range(TW // 64)]
                qkrk = [K("qkr_a"), K("qkr_b")]
                POb, pok = (P6, "P6") if p == 0 else (P7, "P7")

                def tap(name, ap, keys):
                    if name in dbg_d and h == 0 and T == dbg_T:
                        dma("sp", dbg_d[name], ap, reads=keys, writes=["dbg_" + name])

                import os
                if int(os.environ.get("KCUT", "99")) <= -1:
                    return
                qkrk = [K("qkr_a"), K("qkr_b")]
                ptb, ptk = next_pb(bf=True)
                for s_ in range(NSUB):
                    S.add("pe", lambda e, s_=s_, ptb=ptb: e.transpose(out=ptb[:, s_ * 128:(s_ + 1) * 128], in_=qkr[p][:, s_, 0:128],
                                                                      identity=identb[:]), qkrk, [ptk])
                    S.add("pe", lambda e, s_=s_, ptb=ptb: e.transpose(out=ptb[:, 256 + s_ * 128:256 + (s_ + 1) * 128], in_=qkr[p][:, s_, 128:256],
                                                                      identity=identb[:]), qkrk, [ptk])
                S.add("act", lambda e, ptb=ptb: e.activation(out=qkT[p][:].rearrange("p a t -> p (a t)"), in_=ptb[:, 0:512], func=AF.Copy),
                      [ptk], [K("qkT")])
                for s_ in range(NSUB):
                    S.add("pe", lambda e, s_=s_: e.matmul(P4[:, s_ * 128:(s_ + 1) * 128], lhsT=Kt[p][:, s_ * 128:(s_ + 1) * 128],
                                                          rhs=Qt[p][:, s_ * 128:(s_ + 1) * 128], start=True, stop=True),
                          [K("Kt"), K("Qt")], ["P4"])
                for s_ in range(NSUB):
                    S.add("dve", lambda e, s_=s_: e.tensor_tensor(out=At[:, s_, :], in0=P4[:, s_ * 128:(s_ + 1) * 128], in1=maskh[:],
                                                                  op=ALU.mult), ["P4"], ["At%d" % s_])
                def emit_ret_pre():
                    for s_ in range(NSUB):
                        S.add("pe", lambda e, s_=s_: e.matmul(P4[:, 256 + s_ * 128:256 + (s_ + 1) * 128], lhsT=qkT[p][:, 1, s_ * 128:(s_ + 1) * 128],
                                                              rhs=qkT[p][:, 0, s_ * 128:(s_ + 1) * 128], start=True, stop=True),
                              [K("qkT")], ["P4"])
                    for s_ in range(NSUB):
                        S.add("pe", lambda e, s_=s_: e.matmul(P5[:, 256 + s_ * 128:256 + (s_ + 1) * 128], lhsT=qkr[p][:, s_, 128:256],
                                                              rhs=vAB[p][:, s_, 128:256], start=True, stop=True),
                              qkrk + [K("vAB%d" % s_)], ["P5"])
                    for s_ in range(NSUB):
                        S.add("dve", lambda e, s_=s_: e.scalar_tensor_tensor(out=Sc[:, s_, :], in0=P4[:, 256 + s_ * 128:256 + (s_ + 1) * 128],
                                                                             scalar=ginv[h], in1=maskr[:], op0=ALU.mult, op1=ALU.mult),
                              ["P4"], ["Sc%d" % s_])
                yield

                def emit_U(cc):
                    s_, ci = divmod(cc, 2)
                    pu = cc % 2
                    S.add("pe", lambda e: e.matmul(P5[:, pu * 128:(pu + 1) * 128], lhsT=Ktm[p][:, s_, ci, :],
                                                   rhs=vAB[p][:, s_, 0:128], start=True, stop=True),
                          [K("Ktm%d" % ci), K("vAB%d" % s_)], ["P5"])

                def emit_state(cc):
                    gcn = gc0 + cc
                    pu = cc % 2
                    first = (T == 0 and cc == 0)
                    so, sn_ = gcn % 2, (gcn + 1) % 2
                    plast = PP[p][:, cc * 64 + 63:cc * 64 + 64]
                    if first:
                        S.add("dve", lambda e: e.tensor_scalar(out=Sst[sn_][:], in0=P5[:, pu * 128:(pu + 1) * 128], scalar1=plast, scalar2=None,
                                                               op0=ALU.mult), ["P5", K("PP%d" % cc)], ["Sst%d" % sn_])
                    else:
                        S.add("dve", lambda e: e.tensor_tensor(out=Sst[sn_][:], in0=P5[:, pu * 128:(pu + 1) * 128], in1=Sst[so][:], op=ALU.add),
                              ["P5", "Sst%d" % so], ["Sst%d" % sn_])
                        S.add("dve", lambda e: e.tensor_scalar(out=Sst[sn_][:], in0=Sst[sn_][:], scalar1=plast, scalar2=None, op0=ALU.mult),
                              ["Sst%d" % sn_, K("PP%d" % cc)], ["Sst%d" % sn_])
                    sl = (gcn + 1) % NSB
                    S.add("act", lambda e: e.activation(out=Sbf[sl][:], in_=Sst[sn_][:], func=AF.Copy), ["Sst%d" % sn_], ["Sbf%d" % sl])

                def emit_o(cc):
                    gcn = gc0 + cc
                    s_, ci = divmod(cc, 2)
                    first = (T == 0 and cc == 0)
                    S.add("pe", lambda e: e.matmul(POb[:, cc * 64:(cc + 1) * 64], lhsT=vAB[p][:, s_, 0:128], rhs=At[:, s_, ci * 64:(ci + 1) * 64],
                                                   start=True, stop=first), [K("vAB%d" % s_), "At%d" % s_], [pok])
                    if not first:
                        sl = gcn % NSB
                        S.add("pe", lambda e: e.matmul(POb[:, cc * 64:(cc + 1) * 64], lhsT=Sbf[sl][:], rhs=Qt[p][:, cc * 64:(cc + 1) * 64],
                                                       start=False, stop=True), ["Sbf%d" % sl, K("Qt")], [pok])

                def emit_ret(s_):
                    rnn = rn0 + s_
                    first = (T == 0 and s_ == 0)
                    ro, rnw = rnn % 2, (rnn + 1) % 2
                    S.add("pe", lambda e: e.matmul(POb[:, 256 + s_ * 128:256 + (s_ + 1) * 128], lhsT=vAB[p][:, s_, 128:256], rhs=Sc[:, s_, :],
                                                   start=True, stop=first), [K("vAB%d" % s_), "Sc%d" % s_], [pok])
                    if not first:
                        sl = rnn % NRB
                        S.add("pe", lambda e: e.matmul(POb[:, 256 + s_ * 128:256 + (s_ + 1) * 128], lhsT=Rbf[sl][:], rhs=qkT[p][:, 0, s_ * 128:(s_ + 1) * 128],
                                                       start=False, stop=True), ["Rbf%d" % sl, K("qkT")], [pok])
                        S.add("dve", lambda e: e.scalar_tensor_tensor(out=Rst[rnw][0:64, :], in0=Rst[ro][0:64, :], scalar=g128[h],
                                                                      in1=P5[0:64, 256 + s_ * 128:256 + (s_ + 1) * 128],
                                                                      op0=ALU.mult, op1=ALU.add), ["Rst%d" % ro, "P5"], ["Rst%d" % rnw])
                    else:
                        S.add("dve", lambda e: e.tensor_copy(out=Rst[rnw][0:64, :], in_=P5[0:64, 256 + s_ * 128:256 + (s_ + 1) * 128]),
                              ["P5"], ["Rst%d" % rnw])
                    sl2 = (rnn + 1) % NRB
                    S.add("act", lambda e: e.activation(out=Rbf[sl2][0:64, :], in_=Rst[rnw][0:64, :], func=AF.Copy), ["Rst%d" % rnw], ["Rbf%d" % sl2])

                import os
                CUT = int(os.environ.get("KCUT", "99"))
                if CUT <= 0:
                    return
                emit_ret_pre()
                emit_U(0); emit_U(1)
                emit_state(0)
                emit_o(0)
                yield
                emit_U(2)
                emit_state(1)
                emit_ret(0)
                yield
                emit_U(3)
                emit_o(1)
                emit_state(2)
                yield
                emit_ret(1)
                emit_o(2)
                emit_state(3)
                yield
                emit_o(3)
                yield

            def back2(g):
                h, T = divmod(g, n_tiles_run)
                p = g % 2
                hb_ = h % 2
                W = Wh[hb_]
                whk = ["wh%d_0" % hb_, "wh%d_1" % hb_]
                hs = slice(h, h + 1)
                tok0 = T * TW
                hTk = "hT%d" % T
                POb, pok = (P6, "P6") if p == 0 else (P7, "P7")

                def tap(name, ap, keys):
                    if name in dbg_d and h == 0 and T == dbg_T:
                        dma("sp", dbg_d[name], ap, reads=keys, writes=["dbg_" + name])

                def proj_gate(cblk, dst, func, key, **kw):
                    pb, pk = next_pb()
                    for j in range(8):
                        S.add("pe", lambda e, j=j, pb=pb: e.matmul(pb[:, 0:TW], lhsT=W[:, j, cblk * 128:(cblk + 1) * 128],
                                                                   rhs=hT[:, j, tok0:tok0 + TW], start=(j == 0), stop=(j == 7)),
                              whk + [hTk], [pk])
                    S.add("act", lambda e, pb=pb: e.activation(out=dst[:], in_=pb[:, 0:TW], func=func, **kw), [pk], [key])

                proj_gate(2, sz, AF.Silu, "sz")
                S.add("act", lambda e: e.activation(out=sqoA[:], in_=POb[:, 0:TW], func=AF.Square), [pok], ["sqoA"])
                S.add("act", lambda e: e.activation(out=sqoB[:], in_=POb[:, 256:256 + TW], func=AF.Square), [pok], ["sqoB"])
                yield
                ptf, ptk = next_pb()
                for bi, sqo in ((0, sqoA), (1, sqoB)):
                    for s_ in range(NSUB):
                        S.add("pe", lambda e, sqo=sqo, s_=s_, bi=bi, ptf=ptf: e.matmul(ptf[:, bi * 2 + s_:bi * 2 + s_ + 1],
                                                                                      lhsT=sqo[:, s_ * 128:(s_ + 1) * 128], rhs=onescol[:, 0:1],
                                                                                      start=True, stop=True), ["sqoA", "sqoB"], [ptk])
                S.add("dve", lambda e, ptf=ptf: e.tensor_scalar(out=rsv[:], in0=ptf[:, 0:4], scalar1=1.0 / 128.0, scalar2=EPS,
                                                                op0=ALU.mult, op1=ALU.add), [ptk], ["rsv"])
                S.add("pool", lambda e: e.tensor_tensor(out=rsv[:], in0=rsv[:], in1=mh4[:], op=ALU.pow), ["rsv"], ["rsv"])
                proj_gate(4, srz, AF.Silu, "srz")
                yield
                S.add("dve", lambda e: e.scalar_tensor_tensor(out=uA[:], in0=POb[:, 0:TW], scalar=gA[:, hs], in1=sz[:],
                                                              op0=ALU.mult, op1=ALU.mult), [pok, "sz"], ["uA"])
                S.add("dve", lambda e: e.scalar_tensor_tensor(out=uB[:], in0=POb[:, 256:256 + TW], scalar=gB[:, hs], in1=srz[:],
                                                              op0=ALU.mult, op1=ALU.mult), [pok, "srz"], ["uB"])
                for k_ in range(2 * NSUB):
                    S.add("act", lambda e, k_=k_: e.activation(out=dgh[:, k_, :], in_=identf[:], func=AF.Identity, scale=rsv[:, k_:k_ + 1]),
                          ["rsv"], ["dgh%d" % k_])
                    S.add("dve", lambda e, k_=k_: e.scalar_tensor_tensor(out=dgl[:, k_, :], in0=identf[:], scalar=rsv[:, k_:k_ + 1], in1=dgh[:, k_, :],
                                                                         op0=ALU.mult, op1=ALU.subtract), ["rsv", "dgh%d" % k_], ["dgl%d" % k_])
                proj_gate(3, tga, AF.Tanh, "tga", scale=0.5)
                yield
                pbc, pbk = next_pb()
                for k_ in range(2 * NSUB):
                    S.add("pe", lambda e, k_=k_, pbc=pbc: e.matmul(pbc[:, k_ * 128:(k_ + 1) * 128], lhsT=onesb[:], rhs=dgh[:, k_, :],
                                                                   start=True, stop=False), ["dgh%d" % k_], [pbk])
                    S.add("pe", lambda e, k_=k_, pbc=pbc: e.matmul(pbc[:, k_ * 128:(k_ + 1) * 128], lhsT=onesb[:], rhs=dgl[:, k_, :],
                                                                   start=False, stop=True), ["dgl%d" % k_], [pbk])
                proj_gate(5, tgb, AF.Tanh, "tgb", scale=0.5)
                yield
                S.add("dve", lambda e, pbc=pbc: e.tensor_tensor(out=uA[:], in0=uA[:], in1=pbc[:, 0:TW], op=ALU.mult), ["uA", pbk], ["uA"])
                S.add("dve", lambda e, pbc=pbc: e.tensor_tensor(out=uB[:], in0=uB[:], in1=pbc[:, 256:256 + TW], op=ALU.mult), ["uB", pbk], ["uB"])
                yield
                S.add("dve", lambda e: e.scalar_tensor_tensor(out=uA[:], in0=tga[:], scalar=1.0, in1=uA[:], op0=ALU.add, op1=ALU.mult),
                      ["uA", "tga"], ["uA"])
                S.add("dve", lambda e: e.scalar_tensor_tensor(out=uB[:], in0=tgb[:], scalar=1.0, in1=uB[:], op0=ALU.add, op1=ALU.mult),
                      ["uB", "tgb"], ["uB"])
                tap("uA", uA[:], ["uA"]); tap("uB", uB[:], ["uB"])
                S.add("pool", lambda e: e.tensor_tensor(out=mT[:, h, tok0:tok0 + TW], in0=uA[:], in1=uB[:], op=ALU.add), ["uA", "uB"], ["mT%d" % T])
                yield

            from itertools import zip_longest
            import os
            G = n_heads_run * n_tiles_run
            for _ in front(0):
                pass
            TLOAD = min(2, n_tiles_run - 1)
            for i in range(G + 1):
                if i < G:
                    h_i, T_i = divmod(i, n_tiles_run)
                    if T_i == TLOAD and h_i + 1 < n_heads_run:
                        load_w(h_i + 1)
                gens = []
                if i + 1 < G:
                    gens.append(front(i + 1))
                if i < G:
                    gens.append(back(i))
                if i >= 1:
                    gens.append(back2(i - 1))
                for _ in zip_longest(*gens):
                    pass
            if "mT" in dbg_d:
                for j in range(8):
                    dma("sp", dbg_d["mT"][j * 128:(j + 1) * 128, :], mT[:, j, :],
                        reads=["mT%d" % t for t in range(NT)], writes=["dbg_mT"])
            S.fence()

        with ExitStack() as es3:
            sb3 = lambda name, shape, dtype=F32: es3.enter_context(nc.sbuf_tensor("s3_" + name, shape, dtype))
            Wo = sb3("wo", [128, 8, D], BF16)
            fg_b = sb3("fg_b", [128, D])
            xt3 = [sb3("xt%d" % i, [128, D]) for i in range(2)]
            xn = [sb3("xn%d" % i, [128, D]) for i in range(2)]
            ot = [sb3("ot%d" % i, [128, D]) for i in range(2)]
            junk3 = sb3("junk", [128, D], BF16)
            ss3 = sb3("ss3", [128, 2]); rs3 = sb3("rs3", [128, 2]); mh3 = sb3("mh3", [128, 1])
            for half in range(2):
                dma("pool", Wo[:, half * 4:(half + 1) * 4, :], wo_d[half * 512:(half + 1) * 512, :].rearrange("(j p) c -> p j c", p=128),
                    writes=["wo%d" % half])
            dma("sp", fg_b[:], bcast_rows(fg_d, D), writes=["fg_b"])
            S.add("pool", lambda e: e.memset(mh3[:], -0.5), [], ["mh3"])
            for ti in range(L // 128):
                b = ti % 2
                xk = "x3_%d" % b
                dma("sp", xt3[b][:], x_d[ti * 128:(ti + 1) * 128, :], writes=[xk])
                PB3 = [PB[0], PB[1], PB[2], PTf]
                for g in range(2):
                    pb, pk = PB3[b * 2 + g], "pb3_%d" % (b * 2 + g)
                    for j in range(8):
                        S.add("pe", lambda e, j=j, g=g, pb=pb, ti=ti: e.matmul(pb[:], lhsT=mT[:, j, ti * 128:(ti + 1) * 128],
                                                                              rhs=Wo[:, j, g * 512:(g + 1) * 512], start=(j == 0), stop=(j == 7)),
                              ["wo0", "wo1", "mT"], [pk])
                    S.add("dve", lambda e, g=g, pb=pb, b=b: e.tensor_tensor(out=xn[b][:, g * 512:(g + 1) * 512], in0=pb[:],
                                                                            in1=gate_b[:, g * 512:(g + 1) * 512], op=ALU.mult),
                          [pk, "gate_b"], ["xn%d_%d" % (b, g)])
                xnk = ["xn%d_0" % b, "xn%d_1" % b]
                S.add("pool", lambda e, b=b: e.tensor_tensor(out=xn[b][:], in0=xn[b][:], in1=xt3[b][:], op=ALU.add), xnk + [xk], xnk)
                S.add("act", lambda e, b=b: e.activation(out=junk3[:], in_=xn[b][:], func=AF.Square, accum_out=ss3[:, b:b + 1]),
                      xnk, ["junk3", "ss3_%d" % b])
                S.add("dve", lambda e, b=b: e.tensor_scalar(out=ss3[:, b:b + 1], in0=ss3[:, b:b + 1], scalar1=1.0 / D, scalar2=EPS,
                                                            op0=ALU.mult, op1=ALU.add), ["ss3_%d" % b], ["ss3_%d" % b])
                S.add("pool", lambda e, b=b: e.tensor_tensor(out=rs3[:, b:b + 1], in0=ss3[:, b:b + 1], in1=mh3[:], op=ALU.pow),
                      ["ss3_%d" % b, "mh3"], ["rs3_%d" % b])
                S.add("dve", lambda e, b=b: e.scalar_tensor_tensor(out=ot[b][:], in0=xn[b][:], scalar=rs3[:, b:b + 1], in1=fg_b[:],
                                                                   op0=ALU.mult, op1=ALU.mult), xnk + ["rs3_%d" % b, "fg_b"], ["ot%d" % b])
                dma("sp", out_d[ti * 128:(ti + 1) * 128, :], ot[b][:], reads=["ot%d" % b], writes=["out"])

        S.fence()
        for sm in list(sems.values()) + dma_sems + sw_sems:
            nc.gpsimd.sem_clear(sm)
        nc.all_engine_barrier()
        with nc.Block() as block:
            S.emit(nc, block, sems, dma_sems, sw_sems)
        nc.all_engine_barrier()
    return nc


def _perm_cols():
    offs = np.cumsum([0, 1024, 1024, 1024, 1024, 512, 512, 1024, 1024, 1024, 1024])
    o_hq, o_hf, o_hi, o_hz, o_rq, o_rk, o_rv, o_rz, o_ga, o_gb = offs[:10]
    cols = []
    for h in range(NH):
        c = []
        for o in (o_hq, o_hf, o_hz, o_ga, o_rz, o_gb, o_hi, o_rv):
            c += list(range(o + h * 128, o + (h + 1) * 128))
        c += list(range(o_rq + h * 64, o_rq + (h + 1) * 64))
        c += list(range(o_rk + h * 64, o_rk + (h + 1) * 64))
        cols.append(c)
    return np.array(cols)


def make_in_maps(x, c, norm_g, w_ada, b_ada, w_in, hg_lb_logits, hg_norm_g, ret_norm_g, w_out, final_g):
    f = lambda a: np.ascontiguousarray(np.asarray(a, dtype=np.float32))
    x, c = f(x), f(c)
    cols = _perm_cols()
    w_in0 = f(w_in)[0]
    wp = np.ascontiguousarray(np.stack([w_in0[:, cols[h]] for h in range(NH)], 0))
    lb = f(hg_lb_logits)
    lbl = np.ascontiguousarray(np.concatenate([lb[0].reshape(8, 128).T, lb[1].reshape(8, 128).T], 1))
    shared = {
        "w_ada": f(w_ada)[0], "b_ada": f(b_ada)[0].reshape(1, -1), "norm_g": f(norm_g)[0].reshape(1, -1),
        "final_g": f(final_g).reshape(1, -1), "w_in_p": wp, "lbl": lbl,
        "hg_g": np.ascontiguousarray(f(hg_norm_g)[0].reshape(8, 128).T),
        "ret_g": np.ascontiguousarray(f(ret_norm_g)[0].reshape(8, 128).T),
        "w_out": f(w_out)[0],
        "cos_t": _TAB["cos"].reshape(128, -1), "sin_t": _TAB["sin"].reshape(128, -1),
        "xi_t": _TAB["xi"], "zeta_t": _TAB["zeta"], "mask_r": _TAB["mask_r"], "mask_h": _TAB["mask_h"],
        "ident": _TAB["ident"],
    }
    maps = []
    for b in range(8):
        m = dict(shared)
        m["x"] = x[b]
        m["c_col"] = np.ascontiguousarray(c[b].reshape(8, 128).T)
        maps.append(m)
    return maps


def kernel(x, c, norm_g, w_ada, b_ada, w_in, hg_lb_logits, hg_norm_g, ret_norm_g, w_out, final_g):
    maps = make_in_maps(x, c, norm_g, w_ada, b_ada, w_in, hg_lb_logits, hg_norm_g, ret_norm_g, w_out, final_g)
    nc = build_program()
    res = run_bass_kernel_spmd(nc, maps, core_ids=list(range(8)))
    return np.stack([np.asarray(r["out"], dtype=np.float32) for r in res.results], 0)
```

```python
import numpy as np
import concourse.bass as bass
import concourse.mybir as mybir
from concourse.bass_utils import run_bass_kernel_spmd

F32 = mybir.dt.float32
BF16 = mybir.dt.bfloat16
AF = mybir.ActivationFunctionType
ALU = mybir.AluOpType

D = 1024
L = 4096
NH = 8
TW = 256
NT = L // TW
NSUB = TW // 128
EPS = 1e-6
WCOLS = 1152
ENGS = ("pe", "act", "dve", "pool", "sp")


class _Op:
    __slots__ = ("eng", "fn", "deps", "signal", "tok", "dma", "name")

    def __init__(self, eng, fn, dma, name):
        self.eng, self.fn, self.dma, self.name = eng, fn, dma, name
        self.deps = []
        self.signal = False
        self.tok = None


class Sched:
    def __init__(self, n_dma_sems=12):
        self.ops = {e: [] for e in ENGS}
        self.last_w = {}
        self.readers = {}
        self.n_dma_sems = n_dma_sems
        self.all_ops = []

    def add(self, eng, fn, reads=(), writes=(), dma=False, name=""):
        op = _Op(eng, fn, dma, name)
        deps = []
        for k in reads:
            w = self.last_w.get(k)
            if w is not None:
                deps.append(w)
        for k in writes:
            w = self.last_w.get(k)
            if w is not None:
                deps.append(w)
            for r in self.readers.get(k, ()):
                deps.append(r)
        seen = set()
        for d in deps:
            if d is op or id(d) in seen:
                continue
            if eng == "pe" and d.eng == "pe" and not d.dma and not dma:
                continue
            seen.add(id(d))
            op.deps.append(d)
            d.signal = True
        for k in reads:
            self.readers.setdefault(k, []).append(op)
        for k in writes:
            self.last_w[k] = op
            self.readers[k] = []
        self.ops[eng].append(op)
        self.all_ops.append(op)
        return op

    def fence(self):
        lasts = [self.ops[e][-1] for e in ENGS if self.ops[e]]
        dmas = [o for o in self.all_ops if o.dma]
        for e in ENGS:
            op = _Op(e, None, False, "fence")
            for d in lasts + dmas:
                if d.eng == e and not d.dma:
                    continue
                op.deps.append(d)
                d.signal = True
            self.ops[e].append(op)
            self.all_ops.append(op)
        self.last_w = {}
        self.readers = {}

    def emit(self, nc, block, sems, dma_sems, sw_sems):
        cnt = {e: 0 for e in ENGS}
        dcnt = [0] * len(dma_sems)
        dnext = 0
        swnext = 0
        dma_prev = {}
        for e in ENGS:
            pass
        for op in self.all_ops:
            if op.fn is None:
                continue
            if op.dma and op.eng == "pool":
                op.tok = ("w", swnext, 16)
                swnext += 1
                op.signal = True
            elif op.dma:
                i = dnext % len(dma_sems)
                dnext += 1
                dcnt[i] += 16
                op.tok = ("d", i, dcnt[i])
                prev = dma_prev.get(i)
                if prev is not None:
                    op.deps.append(prev)
                dma_prev[i] = op
                op.signal = True
            elif op.signal:
                cnt[op.eng] += 1
                op.tok = ("e", op.eng, cnt[op.eng])

        def run(engname, eng):
            waited = {}
            for op in self.ops[engname]:
                for d in op.deps:
                    t = d.tok
                    if t is None:
                        continue
                    key = (t[0], t[1])
                    if waited.get(key, 0) >= t[2]:
                        continue
                    waited[key] = t[2]
                    sem = dma_sems[t[1]] if t[0] == "d" else (sw_sems[t[1]] if t[0] == "w" else sems[t[1]])
                    eng.wait_ge(sem, t[2])
                if op.fn is None:
                    continue
                ins = op.fn(eng)
                if op.tok is not None:
                    if op.tok[0] == "d":
                        ins.then_inc(dma_sems[op.tok[1]], 16)
                    elif op.tok[0] == "w":
                        ins.then_inc(sw_sems[op.tok[1]], 16)
                    else:
                        ins.then_inc(sems[engname], 1)

        @block.tensor
        def _(eng):
            run("pe", eng)

        @block.scalar
        def _(eng):
            run("act", eng)

        @block.vector
        def _(eng):
            run("dve", eng)

        @block.gpsimd
        def _(eng):
            run("pool", eng)

        @block.sync
        def _(eng):
            run("sp", eng)


def _tables():
    try:
        import jax
        import jax.numpy as jnp
        with jax.default_device(jax.devices("cpu")[0]):
            inv_freq = 1.0 / (10000.0 ** jnp.linspace(0.0, 1.0, 32, dtype=jnp.float32))
            ang = jnp.arange(L, dtype=jnp.int32).astype(jnp.float32)[:, None] * inv_freq[None, :]
            cos = np.asarray(jnp.cos(ang), dtype=np.float32)
            sin = np.asarray(jnp.sin(ang), dtype=np.float32)
            log_gamma = np.asarray(jnp.log(1.0 - jnp.exp2(-5.0 - jnp.arange(NH, dtype=jnp.float32))), dtype=np.float32)
    except Exception:
        inv_freq = (1.0 / (np.float32(10000.0) ** np.linspace(0.0, 1.0, 32, dtype=np.float32))).astype(np.float32)
        ang = (np.arange(L, dtype=np.float32)[:, None] * inv_freq[None, :]).astype(np.float32).astype(np.float64)
        cos = np.cos(ang).astype(np.float32)
        sin = np.sin(ang).astype(np.float32)
        log_gamma = np.log(np.float32(1.0) - np.exp2(np.float32(-5.0) - np.arange(NH, dtype=np.float32))).astype(np.float32)
    cos = cos.reshape(L // 128, 128, 32).transpose(1, 0, 2)
    sin = sin.reshape(L // 128, 128, 32).transpose(1, 0, 2)
    lg = log_gamma.astype(np.float64)
    i = np.arange(128, dtype=np.float64)
    xi = np.exp(lg[None, :] * (i[:, None] + 1.0))
    zeta = np.exp(lg[None, :] * (127.0 - i[:, None])) * (64.0 ** -0.5)
    g128 = np.exp(lg * 128.0)
    ginv = np.exp(-lg * 128.0)
    s = np.arange(128)[:, None]
    t = np.arange(128)[None, :]
    mask_r = (t >= s).astype(np.float32)
    mask_h = ((t >= s) & ((t // 64) == (s // 64))).astype(np.float32)
    ident = np.eye(128, dtype=np.float32)
    return dict(cos=np.ascontiguousarray(cos), sin=np.ascontiguousarray(sin),
                xi=xi.astype(np.float32), zeta=zeta.astype(np.float32),
                g128=g128, ginv=ginv, mask_r=mask_r, mask_h=mask_h, ident=ident)


_TAB = _tables()


def build_program(dbg=None, n_heads_run=NH, n_tiles_run=NT, dbg_T=0):
    nc = bass.Bass("TRN2", target_bir_lowering=False)
    dt = lambda name, shape, dtype=F32, kind="ExternalInput": nc.dram_tensor(name, shape, dtype, kind=kind).ap()
    x_d = dt("x", [L, D])
    ccol_d = dt("c_col", [128, 8])
    wada_d = dt("w_ada", [D, 3 * D])
    bada_d = dt("b_ada", [1, 3 * D])
    ng_d = dt("norm_g", [1, D])
    fg_d = dt("final_g", [1, D])
    wp_d = dt("w_in_p", [NH, D, WCOLS])
    lbl_d = dt("lbl", [128, 16])
    ga_d = dt("hg_g", [128, 8])
    gb_d = dt("ret_g", [128, 8])
    wo_d = dt("w_out", [D, D])
    cos_d = dt("cos_t", [128, 32 * 32])
    sin_d = dt("sin_t", [128, 32 * 32])
    xi_d = dt("xi_t", [128, 8])
    zeta_d = dt("zeta_t", [128, 8])
    maskr_d = dt("mask_r", [128, 128])
    maskh_d = dt("mask_h", [128, 128])
    ident_d = dt("ident", [128, 128])
    out_d = dt("out", [L, D], kind="ExternalOutput")
    dbg_d = {}
    if dbg:
        for name, shape, dtype in dbg:
            dbg_d[name] = dt("dbg_" + name, shape, dtype, kind="ExternalOutput")

    S = Sched()
    g128 = [float(v) for v in _TAB["g128"]]
    ginv = [float(v) for v in _TAB["ginv"]]

    def bcast_rows(ap2d, n):
        return bass.AP(ap2d.tensor, ap2d.offset, [[0, 128], [1, n]])

    from contextlib import ExitStack
    with nc.cleanup_on_exit(), ExitStack() as es:
        sb = lambda name, shape, dtype=F32: es.enter_context(nc.sbuf_tensor("sb_" + name, shape, dtype))
        ps = lambda name, shape, dtype=F32: es.enter_context(nc.psum_tensor("ps_" + name, shape, dtype))
        hT = sb("hT", [128, 8, L], BF16)
        mT = sb("mT", [128, 8, L], BF16)
        gate_b = sb("gate_b", [128, D])
        identb = sb("identb", [128, 128], BF16)
        onesb = sb("onesb", [128, 128], BF16)
        maskh = sb("maskh", [128, 128])
        maskr = sb("maskr", [128, 128])
        xi_t = sb("xi_t", [128, 8])
        zeta_t = sb("zeta_t", [128, 8])
        c0 = sb("c0", [128, 8])
        c1 = sb("c1", [128, 8])
        nc1 = sb("nc1", [128, 8])
        gA = sb("gA", [128, 8])
        gB = sb("gB", [128, 8])
        zer = sb("zer", [128, 64])
        mh4 = sb("mh4", [128, 4])
        identf = sb("identf", [128, 128])
        onesf = sb("onesf", [128, 128])
        onescol = sb("onescol", [128, 2], BF16)
        PB = [ps("pb%d" % i, [128, 512]) for i in range(3)]
        PTf = ps("ptr", [128, 512])
        PT = PTf[:].bitcast(BF16)
        P4 = ps("p4", [128, 512])
        P5 = ps("p5", [128, 512])
        P6 = ps("p6", [128, 512])
        P7 = ps("p7", [128, 512])

        sems = {e: nc.alloc_semaphore("s_" + e) for e in ENGS}
        dma_sems = [nc.alloc_semaphore("d%d" % i) for i in range(S.n_dma_sems)]
        sw_sems = [nc.alloc_semaphore("w%d" % i) for i in range(2 * NH + 2)]

        def dma(eng, out, in_, reads=(), writes=(), name=""):
            return S.add(eng, lambda e: e.dma_start(out=out, in_=in_), reads, writes, dma=True, name=name)

        def dbg_out(name, src_ap, key, dst=None):
            if name in dbg_d:
                dma("sp", dbg_d[name] if dst is None else dst, src_ap, reads=[key], writes=["dbg_" + name])

        with ExitStack() as es0:
            sb0 = lambda name, shape, dtype=F32: es0.enter_context(nc.sbuf_tensor("s0_" + name, shape, dtype))
            ccol = sb0("ccol", [128, 8])
            scol = sb0("scol", [128, 8])
            screp = sb0("screp", [128, 8, 128])
            wa = [sb0("wa%d" % i, [128, 8, 256]) for i in range(2)]
            ng_b = sb0("ng_b", [128, D])
            mod_b = sb0("mod_b", [128, 2 * D])
            g1_b = sb0("g1_b", [128, D])
            lbl = sb0("lbl", [128, 16])
            dl = sb0("dl", [128, 8])
            thl = sb0("thl", [128, 8])
            graw = sb0("graw", [128, 16])
            xt = [sb0("xt%d" % i, [128, D]) for i in range(2)]
            junk = sb0("junk", [128, D], BF16)
            t1 = [sb0("t1_%d" % i, [128, D]) for i in range(2)]
            hb = [sb0("hb%d" % i, [128, D], BF16) for i in range(2)]
            ssq = sb0("ssq", [128, 2])
            rstd = sb0("rstd", [128, 2])
            mh1 = sb0("mh1", [128, 1])

            dma("sp", identf[:], ident_d, writes=["identf"])
            dma("sp", maskh[:], maskh_d, writes=["maskh"])
            dma("sp", maskr[:], maskr_d, writes=["maskr"])
            dma("sp", xi_t[:], xi_d, writes=["xi"])
            dma("sp", zeta_t[:], zeta_d, writes=["zeta"])
            dma("sp", ccol[:], ccol_d, writes=["ccol"])
            dma("sp", lbl[:], lbl_d, writes=["lbl"])
            dma("sp", graw[:, 0:8], ga_d, writes=["graw0"])
            dma("sp", graw[:, 8:16], gb_d, writes=["graw1"])
            dma("sp", mod_b[:], bcast_rows(bada_d[:, 0:2 * D], 2 * D), writes=["bias_b"])
            dma("sp", gate_b[:], bcast_rows(bada_d[:, 2 * D:3 * D], D), writes=["bias_b"])
            dma("sp", ng_b[:], bcast_rows(ng_d, D), writes=["ng_b"])

            S.add("dve", lambda e: e.tensor_copy(out=identb[:], in_=identf[:]), ["identf"], ["identb"])
            S.add("pool", lambda e: e.memset(onesb[:], 1.0), [], ["onesb"])
            S.add("pool", lambda e: e.memset(zer[:], 0.0), [], ["zer"])
            S.add("pool", lambda e: e.memset(mh4[:], -0.5), [], ["mh4"])
            S.add("pool", lambda e: e.memset(onesf[:], 1.0), [], ["onesf"])
            S.add("pool", lambda e: e.memset(onescol[:], 1.0), [], ["onescol"])
            S.add("pool", lambda e: e.memset(mh1[:], -0.5), [], ["mh1"])
            S.add("dve", lambda e: e.tensor_tensor(out=dl[:], in0=lbl[:, 0:8], in1=lbl[:, 8:16], op=ALU.subtract),
                  ["lbl"], ["dl"])
            S.add("act", lambda e: e.activation(out=thl[:], in_=dl[:], func=AF.Tanh, scale=0.5), ["dl"], ["thl"])
            S.add("dve", lambda e: e.tensor_scalar(out=c0[:], in0=thl[:], scalar1=0.25, scalar2=0.75,
                                                   op0=ALU.mult, op1=ALU.add), ["thl"], ["c0"])
            S.add("dve", lambda e: e.tensor_scalar(out=c1[:], in0=thl[:], scalar1=-0.25, scalar2=0.25,
                                                   op0=ALU.mult, op1=ALU.add), ["thl"], ["c1"])
            S.add("dve", lambda e: e.tensor_scalar(out=nc1[:], in0=thl[:], scalar1=0.25, scalar2=-0.25,
                                                   op0=ALU.mult, op1=ALU.add), ["thl"], ["nc1"])
            S.add("dve", lambda e: e.tensor_scalar(out=gA[:], in0=graw[:, 0:8], scalar1=0.5, scalar2=None,
                                                   op0=ALU.mult), ["graw0"], ["gA"])
            S.add("dve", lambda e: e.tensor_scalar(out=gB[:], in0=graw[:, 8:16], scalar1=0.5, scalar2=None,
                                                   op0=ALU.mult), ["graw1"], ["gB"])
            S.add("act", lambda e: e.activation(out=scol[:], in_=ccol[:], func=AF.Silu), ["ccol"], ["scol"])
            S.add("dve", lambda e: e.tensor_copy(out=screp[:], in_=scol[:].unsqueeze(2).to_broadcast([128, 8, 128])),
                  ["scol"], ["screp"])
            for g in range(12):
                wb = wa[g % 2]
                wk = "wa%d" % (g % 2)
                dma("sp", wb[:], wada_d[:, g * 256:(g + 1) * 256].rearrange("(j p) c -> p j c", p=128), writes=[wk])
                pb = PB[g % 3]
                pk = "pb%d" % (g % 3)
                for j in range(8):
                    S.add("pe", lambda e, j=j, wb=wb, pb=pb: e.matmul(pb[:, 0:256], lhsT=screp[:, j, :], rhs=wb[:, j, :],
                                                                     start=(j == 0), stop=(j == 7)),
                          ["screp", wk], [pk])
                if g < 8:
                    dst = mod_b[:, g * 256:(g + 1) * 256]
                else:
                    dst = gate_b[:, (g - 8) * 256:(g - 7) * 256]
                S.add("dve", lambda e, dst=dst, pb=pb: e.tensor_tensor(out=dst, in0=pb[:, 0:256], in1=dst, op=ALU.add),
                      [pk, "bias_b"], ["modg%d" % g])
            S.add("dve", lambda e: e.scalar_tensor_tensor(out=g1_b[:], in0=mod_b[:, D:2 * D], scalar=1.0, in1=ng_b[:],
                                                          op0=ALU.add, op1=ALU.mult),
                  ["modg%d" % g for g in range(4, 8)] + ["ng_b"], ["g1_b"])
            dbg_out("mod", mod_b[0:1, :], "g1_b")

            for ti in range(L // 128):
                b = ti % 2
                xk, hk = "xt%d" % b, "hb%d" % b
                dma("sp", xt[b][:], x_d[ti * 128:(ti + 1) * 128, :], writes=[xk])
                S.add("act", lambda e, b=b: e.activation(out=junk[:], in_=xt[b][:], func=AF.Square,
                                                         accum_out=ssq[:, b:b + 1]), [xk], ["junk", "ssq%d" % b])
                S.add("dve", lambda e, b=b: e.tensor_scalar(out=ssq[:, b:b + 1], in0=ssq[:, b:b + 1], scalar1=1.0 / D,
                                                            scalar2=EPS, op0=ALU.mult, op1=ALU.add),
                      ["ssq%d" % b], ["ssq%d" % b])
                S.add("pool", lambda e, b=b: e.tensor_tensor(out=rstd[:, b:b + 1], in0=ssq[:, b:b + 1], in1=mh1[:],
                                                             op=ALU.pow), ["ssq%d" % b, "mh1"], ["rstd%d" % b])
                S.add("dve", lambda e, b=b: e.scalar_tensor_tensor(out=t1[b][:], in0=xt[b][:], scalar=rstd[:, b:b + 1],
                                                                   in1=g1_b[:], op0=ALU.mult, op1=ALU.mult),
                      [xk, "rstd%d" % b, "g1_b"], ["t1_%d" % b])
                S.add("pool", lambda e, b=b: e.tensor_tensor(out=hb[b][:], in0=t1[b][:], in1=mod_b[:, 0:D], op=ALU.add),
                      ["t1_%d" % b] + ["modg%d" % g for g in range(4)], [hk])
                ptb1 = (PTf if b == 0 else PB[0])[:].bitcast(BF16)
                ptk1 = "PT1_%d" % b
                for j in range(8):
                    S.add("pe", lambda e, b=b, j=j, ptb1=ptb1: e.transpose(out=ptb1[:, j * 128:(j + 1) * 128],
                                                                           in_=hb[b][:, j * 128:(j + 1) * 128], identity=identb[:]),
                          [hk, "identb"], [ptk1])
                S.add("act", lambda e, ti=ti, ptb1=ptb1: e.activation(out=hT[:, :, ti * 128:(ti + 1) * 128],
                                                                      in_=ptb1[:].rearrange("p (j t) -> p j t", j=8), func=AF.Copy),
                      [ptk1], ["hT%d" % (ti * 128 // TW)])
            if "hT" in dbg_d:
                for j in range(8):
                    dma("sp", dbg_d["hT"][j * 128:(j + 1) * 128, :], hT[:, j, :],
                        reads=["hT%d" % t for t in range(NT)], writes=["dbg_hT"])
            S.fence()


        with ExitStack() as es2:
            sb2 = lambda name, shape, dtype=F32: es2.enter_context(nc.sbuf_tensor("s2_" + name, shape, dtype))
            D2 = lambda name, shape, dtype=F32: [sb2("%s_%d" % (name, i), shape, dtype) for i in range(2)]
            Wh = D2("wh", [128, 8, WCOLS], BF16)
            cs = D2("cs", [128, NSUB, 32]); sn = D2("sn", [128, NSUB, 32])
            th = sb2("th", [128, TW]); sq = th
            kk = sb2("kk", [128, TW]); ff = sb2("ff", [128, TW]); RR = ff
            sz = sb2("sz", [128, TW]); tga = sb2("tga", [128, TW]); srz = sb2("srz", [128, TW]); tgb = sb2("tgb", [128, TW])
            PP = D2("PP", [128, TW])
            Qt = D2("Qt", [128, TW], BF16); Kt = D2("Kt", [128, TW], BF16)
            Ktm = D2("Ktm", [128, NSUB, 2, 128], BF16)
            vAB = D2("vAB", [128, NSUB, 256], BF16)
            qkr = D2("qkr", [128, NSUB, 256], BF16)
            qkT = D2("qkT", [128, 2, TW], BF16)
            At = sb2("At", [128, NSUB, 128], BF16)
            Sc = sb2("Sc", [128, NSUB, 128], BF16)
            Sst = D2("Sst", [128, 128])
            NSB = 4
            Sbf = [sb2("Sbf%d" % i, [128, 128], BF16) for i in range(NSB)]
            NRB = 4
            Rst = D2("Rst", [128, 128])
            Rbf = [sb2("Rbf%d" % i, [128, 128], BF16) for i in range(NRB)]
            qk = sb2("qk", [128, NSUB, 128])
            ra = sb2("ra", [128, NSUB, 2, 32]); rb = sb2("rb", [128, NSUB, 2, 32])
            rc = sb2("rc", [128, NSUB, 2, 32]); rd = sb2("rd", [128, NSUB, 2, 32])
            sqoA = sb2("sqoA", [128, TW], BF16); sqoB = sb2("sqoB", [128, TW], BF16)
            uA = sb2("uA", [128, TW]); uB = sb2("uB", [128, TW])
            dgh = sb2("dgh", [128, 2 * NSUB, 128], BF16); dgl = sb2("dgl", [128, 2 * NSUB, 128], BF16)
            rsv = sb2("rsv", [128, 4])

            def load_w(h):
                hb_ = h % 2
                for half in range(2):
                    dma("pool", Wh[hb_][:, half * 4:(half + 1) * 4, :],
                        wp_d[h, half * 512:(half + 1) * 512, :].rearrange("(j p) c -> p j c", p=128),
                        writes=["wh%d_%d" % (hb_, half)])

            load_w(0)
            pbrr = [0]
            for p_ in range(2):
                S.add("pool", lambda e, p_=p_: e.memset(Ktm[p_][:], 0.0), [], ["Ktm0_%d" % p_, "Ktm1_%d" % p_])
                S.add("pool", lambda e, p_=p_: e.memset(qkT[p_][:], 0.0), [], ["qkT_%d" % p_])
                S.add("pool", lambda e, p_=p_: e.memset(qkr[p_][:], 0.0), [], ["qkr_a_%d" % p_, "qkr_b_%d" % p_])
            for i_ in range(NRB):
                S.add("pool", lambda e, i_=i_: e.memset(Rbf[i_][:], 0.0), [], ["Rbf%d" % i_])

            PB4 = PB + [PTf]
            PB4b = [b_[:].bitcast(BF16) for b_ in PB4]

            def next_pb(bf=False):
                i = pbrr[0] % 4
                pbrr[0] += 1
                return (PB4b[i] if bf else PB4[i]), "pb%d" % i

            def front(g):
                h, T = divmod(g, n_tiles_run)
                p = g % 2
                K = lambda nm: "%s_%d" % (nm, p)
                hb_ = h % 2
                W = Wh[hb_]
                whk = ["wh%d_0" % hb_, "wh%d_1" % hb_]
                hs = slice(h, h + 1)
                tok0 = T * TW
                hTk = "hT%d" % T
                dma("sp", cs[p][:], cos_d[:, T * NSUB * 32:(T + 1) * NSUB * 32].rearrange("p (s f) -> p s f", s=NSUB), writes=[K("cs")])
                dma("sp", sn[p][:], sin_d[:, T * NSUB * 32:(T + 1) * NSUB * 32].rearrange("p (s f) -> p s f", s=NSUB), writes=[K("sn")])

                def proj_fm(cblk):
                    pb, pk = next_pb()
                    for j in range(8):
                        S.add("pe", lambda e, j=j, pb=pb: e.matmul(pb[:, 0:TW], lhsT=W[:, j, cblk * 128:(cblk + 1) * 128],
                                                                   rhs=hT[:, j, tok0:tok0 + TW], start=(j == 0), stop=(j == 7)),
                              whk + [hTk], [pk])
                    return pb, pk

                pb, pk = proj_fm(1)
                S.add("act", lambda e, pb=pb: e.activation(out=th[:], in_=pb[:, 0:TW], func=AF.Tanh, scale=0.5), [pk], ["th"])
                S.add("act", lambda e: e.activation(out=kk[:], in_=th[:], func=AF.Identity, scale=nc1[:, hs], bias=c1[:, hs]), ["th"], ["kk"])
                S.add("act", lambda e: e.activation(out=ff[:], in_=th[:], func=AF.Identity, scale=c1[:, hs], bias=c0[:, hs]), ["th"], ["ff"])
                yield
                for c in range(TW // 64):
                    S.add("dve", lambda e, c=c: e.tensor_tensor_scan(out=PP[p][:, c * 64:(c + 1) * 64], data0=ff[:, c * 64:(c + 1) * 64],
                                                                    data1=zer[:, 0:64], initial=1.0, op0=ALU.mult, op1=ALU.add),
                          ["ff"], [K("PP%d" % c)])
                PPk = [K("PP%d" % c) for c in range(TW // 64)]
                S.add("dve", lambda e: e.reciprocal(out=RR[:], in_=PP[p][:]), PPk, ["ff"])
                pb, pk = proj_fm(0)
                S.add("act", lambda e, pb=pb: e.activation(out=sq[:], in_=pb[:, 0:TW], func=AF.Silu), [pk], ["th"])
                yield
                S.add("pool", lambda e: e.tensor_tensor(out=Qt[p][:], in0=sq[:], in1=PP[p][:], op=ALU.mult), ["th"] + PPk, [K("Qt")])
                S.add("pool", lambda e: e.tensor_tensor(out=Kt[p][:], in0=kk[:], in1=RR[:], op=ALU.mult), ["kk", "ff"], [K("Kt")])
                for s_ in range(NSUB):
                    pb, pk = next_pb()
                    for j in range(8):
                        S.add("pe", lambda e, j=j, pb=pb, s_=s_: e.matmul(pb[:, 0:384], lhsT=hT[:, j, tok0 + s_ * 128:tok0 + (s_ + 1) * 128],
                                                                          rhs=W[:, j, 768:1152], start=(j == 0), stop=(j == 7)),
                              whk + [hTk], [pk])
                    S.add("act", lambda e, pb=pb, s_=s_: e.activation(out=vAB[p][:, s_, :], in_=pb[:, 0:256], func=AF.Copy), [pk], [K("vAB%d" % s_)])
                    S.add("act", lambda e, pb=pb, s_=s_: e.activation(out=qk[:, s_, 0:64], in_=pb[:, 256:320], func=AF.Identity,
                                                                      scale=xi_t[:, hs]), [pk], ["qk%d" % s_])
                    S.add("act", lambda e, pb=pb, s_=s_: e.activation(out=qk[:, s_, 64:128], in_=pb[:, 320:384], func=AF.Identity,
                                                                      scale=zeta_t[:, hs]), [pk], ["qk%d" % s_])
                    yield
                ptb, ptk = next_pb(bf=True)
                for s_ in range(NSUB):
                    S.add("pe", lambda e, s_=s_, ptb=ptb: e.transpose(out=ptb[:, s_ * 128:(s_ + 1) * 128], in_=Kt[p][:, s_ * 128:(s_ + 1) * 128],
                                                                      identity=identb[:]), [K("Kt")], [ptk])
                for ci in range(2):
                    S.add("act", lambda e, ci=ci, ptb=ptb: e.activation(out=Ktm[p][ci * 64:(ci + 1) * 64, :, ci, :],
                                                                        in_=ptb[ci * 64:(ci + 1) * 64, 0:256].rearrange("p (s d) -> p s d", s=NSUB),
                                                                        func=AF.Copy), [ptk], [K("Ktm%d" % ci)])
                qk5 = qk[:].rearrange("p s (a b f) -> p s a b f", a=2, b=2)
                qa, qb = qk5[:, :, :, 0, :], qk5[:, :, :, 1, :]
                qr6 = qkr[p][:].rearrange("p s (a z b f) -> p s a z b f", a=2, z=2, b=2)
                cosb = cs[p][:].unsqueeze(2).to_broadcast([128, NSUB, 2, 32])
                sinb = sn[p][:].unsqueeze(2).to_broadcast([128, NSUB, 2, 32])
                qkk = ["qk%d" % s_ for s_ in range(NSUB)]
                S.add("dve", lambda e: e.tensor_tensor(out=ra[:], in0=qa, in1=cosb, op=ALU.mult), qkk + [K("cs")], ["ra"])
                S.add("dve", lambda e: e.tensor_tensor(out=rb[:], in0=qb, in1=sinb, op=ALU.mult), qkk + [K("sn")], ["rb"])
                S.add("dve", lambda e: e.tensor_tensor(out=qr6[:, :, :, 0, 0, :], in0=ra[:], in1=rb[:], op=ALU.subtract), ["ra", "rb"], [K("qkr_a")])
                S.add("pool", lambda e: e.tensor_tensor(out=rc[:], in0=qa, in1=sinb, op=ALU.mult), qkk + [K("sn")], ["rc"])
                S.add("pool", lambda e: e.tensor_tensor(out=rd[:], in0=qb, in1=cosb, op=ALU.mult), qkk + [K("cs")], ["rd"])
                S.add("pool", lambda e: e.tensor_tensor(out=qr6[:, :, :, 0, 1, :], in0=rc[:], in1=rd[:], op=ALU.add), ["rc", "rd"], [K("qkr_b")])
                yield

            def back(g):
                h, T = divmod(g, n_tiles_run)
                p = g % 2
                K = lambda nm: "%s_%d" % (nm, p)
                hs = slice(h, h + 1)
                tok0 = T * TW
                gc0 = g * (TW // 64)
                rn0 = g * NSUB
                PPk = [K("PP%d" % c) for c in range(TW // 64)]
                qkrk = [K("qkr_a"), K("qkr_b")]
                POb, pok = (P6, "P6") if p == 0 else (P7, "P7")

                def tap(name, ap, keys):
                    if name in dbg_d and h == 0 and T == dbg_T:
                        dma("sp", dbg_d[name], ap, reads=keys, writes=["dbg_" + name])

                import os
                if int(os.environ.get("KCUT", "99")) <= -1:
                    return
                def emit_qkT():
                    qkrk = [K("qkr_a"), K("qkr_b")]
                    ptb, ptk = next_pb(bf=True)
                    for s_ in range(NSUB):
                        S.add("pe", lambda e, s_=s_, ptb=ptb: e.transpose(out=ptb[:, s_ * 128:(s_ + 1) * 128], in_=qkr[p][:, s_, 0:128],
                                                                          identity=identb[:]), qkrk, [ptk])
                        S.add("pe", lambda e, s_=s_, ptb=ptb: e.transpose(out=ptb[:, 256 + s_ * 128:256 + (s_ + 1) * 128], in_=qkr[p][:, s_, 128:256],
                                                                          identity=identb[:]), qkrk, [ptk])
                    S.add("act", lambda e, ptb=ptb: e.activation(out=qkT[p][:].rearrange("p a t -> p (a t)"), in_=ptb[:, 0:512], func=AF.Copy),
                          [ptk], [K("qkT")])
                for s_ in range(NSUB):
                    S.add("pe", lambda e, s_=s_: e.matmul(P4[:, s_ * 128:(s_ + 1) * 128], lhsT=Kt[p][:, s_ * 128:(s_ + 1) * 128],
                                                          rhs=Qt[p][:, s_ * 128:(s_ + 1) * 128], start=True, stop=True),
                          [K("Kt"), K("Qt")], ["P4"])
                for s_ in range(NSUB):
                    S.add("dve", lambda e, s_=s_: e.tensor_tensor(out=At[:, s_, :], in0=P4[:, s_ * 128:(s_ + 1) * 128], in1=maskh[:],
                                                                  op=ALU.mult), ["P4"], ["At%d" % s_])
                def emit_ret_pre():
                    for s_ in range(NSUB):
                        S.add("pe", lambda e, s_=s_: e.matmul(P4[:, 256 + s_ * 128:256 + (s_ + 1) * 128], lhsT=qkT[p][:, 1, s_ * 128:(s_ + 1) * 128],
                                                              rhs=qkT[p][:, 0, s_ * 128:(s_ + 1) * 128], start=True, stop=True),
                              [K("qkT")], ["P4"])
                    for s_ in range(NSUB):
                        S.add("pe", lambda e, s_=s_: e.matmul(P5[:, 256 + s_ * 128:256 + (s_ + 1) * 128], lhsT=qkr[p][:, s_, 128:256],
                                                              rhs=vAB[p][:, s_, 128:256], start=True, stop=True),
                              qkrk + [K("vAB%d" % s_)], ["P5"])
                    for s_ in range(NSUB):
                        S.add("dve", lambda e, s_=s_: e.scalar_tensor_tensor(out=Sc[:, s_, :], in0=P4[:, 256 + s_ * 128:256 + (s_ + 1) * 128],
                                                                             scalar=ginv[h], in1=maskr[:], op0=ALU.mult, op1=ALU.mult),
                              ["P4"], ["Sc%d" % s_])
                yield

                def emit_U(cc):
                    s_, ci = divmod(cc, 2)
                    pu = cc % 2
                    S.add("pe", lambda e: e.matmul(P5[:, pu * 128:(pu + 1) * 128], lhsT=Ktm[p][:, s_, ci, :],
                                                   rhs=vAB[p][:, s_, 0:128], start=True, stop=True),
                          [K("Ktm%d" % ci), K("vAB%d" % s_)], ["P5"])

                def emit_state(cc):
                    gcn = gc0 + cc
                    pu = cc % 2
                    first = (T == 0 and cc == 0)
                    so, sn_ = gcn % 2, (gcn + 1) % 2
                    plast = PP[p][:, cc * 64 + 63:cc * 64 + 64]
                    if first:
                        S.add("dve", lambda e: e.tensor_scalar(out=Sst[sn_][:], in0=P5[:, pu * 128:(pu + 1) * 128], scalar1=plast, scalar2=None,
                                                               op0=ALU.mult), ["P5", K("PP%d" % cc)], ["Sst%d" % sn_])
                    else:
                        S.add("dve", lambda e: e.tensor_tensor(out=Sst[sn_][:], in0=P5[:, pu * 128:(pu + 1) * 128], in1=Sst[so][:], op=ALU.add),
                              ["P5", "Sst%d" % so], ["Sst%d" % sn_])
                        S.add("dve", lambda e: e.tensor_scalar(out=Sst[sn_][:], in0=Sst[sn_][:], scalar1=plast, scalar2=None, op0=ALU.mult),
                              ["Sst%d" % sn_, K("PP%d" % cc)], ["Sst%d" % sn_])
                    sl = (gcn + 1) % NSB
                    S.add("act", lambda e: e.activation(out=Sbf[sl][:], in_=Sst[sn_][:], func=AF.Copy), ["Sst%d" % sn_], ["Sbf%d" % sl])

                def emit_o(cc):
                    gcn = gc0 + cc
                    s_, ci = divmod(cc, 2)
                    first = (T == 0 and cc == 0)
                    S.add("pe", lambda e: e.matmul(POb[:, cc * 64:(cc + 1) * 64], lhsT=vAB[p][:, s_, 0:128], rhs=At[:, s_, ci * 64:(ci + 1) * 64],
                                                   start=True, stop=first), [K("vAB%d" % s_), "At%d" % s_], [pok])
                    if not first:
                        sl = gcn % NSB
                        S.add("pe", lambda e: e.matmul(POb[:, cc * 64:(cc + 1) * 64], lhsT=Sbf[sl][:], rhs=Qt[p][:, cc * 64:(cc + 1) * 64],
                                                       start=False, stop=True), ["Sbf%d" % sl, K("Qt")], [pok])

                def emit_ret(s_):
                    rnn = rn0 + s_
                    first = (T == 0 and s_ == 0)
                    ro, rnw = rnn % 2, (rnn + 1) % 2
                    S.add("pe", lambda e: e.matmul(POb[:, 256 + s_ * 128:256 + (s_ + 1) * 128], lhsT=vAB[p][:, s_, 128:256], rhs=Sc[:, s_, :],
                                                   start=True, stop=first), [K("vAB%d" % s_), "Sc%d" % s_], [pok])
                    if not first:
                        sl = rnn % NRB
                        S.add("pe", lambda e: e.matmul(POb[:, 256 + s_ * 128:256 + (s_ + 1) * 128], lhsT=Rbf[sl][:], rhs=qkT[p][:, 0, s_ * 128:(s_ + 1) * 128],
                                                       start=False, stop=True), ["Rbf%d" % sl, K("qkT")], [pok])
                        S.add("dve", lambda e: e.scalar_tensor_tensor(out=Rst[rnw][0:64, :], in0=Rst[ro][0:64, :], scalar=g128[h],
                                                                      in1=P5[0:64, 256 + s_ * 128:256 + (s_ + 1) * 128],
                                                                      op0=ALU.mult, op1=ALU.add), ["Rst%d" % ro, "P5"], ["Rst%d" % rnw])
                    else:
                        S.add("dve", lambda e: e.tensor_copy(out=Rst[rnw][0:64, :], in_=P5[0:64, 256 + s_ * 128:256 + (s_ + 1) * 128]),
                              ["P5"], ["Rst%d" % rnw])
                    sl2 = (rnn + 1) % NRB
                    S.add("act", lambda e: e.activation(out=Rbf[sl2][0:64, :], in_=Rst[rnw][0:64, :], func=AF.Copy), ["Rst%d" % rnw], ["Rbf%d" % sl2])

                import os
                CUT = int(os.environ.get("KCUT", "99"))
                if CUT <= 0:
                    return
                emit_qkT()
                emit_U(0); emit_U(1)
                emit_state(0)
                emit_o(0)
                yield
                emit_ret_pre()
                emit_U(2)
                emit_state(1)
                yield
                emit_U(3)
                emit_o(1)
                emit_state(2)
                emit_ret(0)
                yield
                emit_ret(1)
                emit_o(2)
                emit_state(3)
                yield
                emit_o(3)
                yield

            def back2(g):
                h, T = divmod(g, n_tiles_run)
                p = g % 2
                hb_ = h % 2
                W = Wh[hb_]
                whk = ["wh%d_0" % hb_, "wh%d_1" % hb_]
                hs = slice(h, h + 1)
                tok0 = T * TW
                hTk = "hT%d" % T
                POb, pok = (P6, "P6") if p == 0 else (P7, "P7")

                def tap(name, ap, keys):
                    if name in dbg_d and h == 0 and T == dbg_T:
                        dma("sp", dbg_d[name], ap, reads=keys, writes=["dbg_" + name])

                def proj_gate(cblk, dst, func, key, **kw):
                    pb, pk = next_pb()
                    for j in range(8):
                        S.add("pe", lambda e, j=j, pb=pb: e.matmul(pb[:, 0:TW], lhsT=W[:, j, cblk * 128:(cblk + 1) * 128],
                                                                   rhs=hT[:, j, tok0:tok0 + TW], start=(j == 0), stop=(j == 7)),
                              whk + [hTk], [pk])
                    S.add("act", lambda e, pb=pb: e.activation(out=dst[:], in_=pb[:, 0:TW], func=func, **kw), [pk], [key])

                proj_gate(2, sz, AF.Silu, "sz")
                S.add("act", lambda e: e.activation(out=sqoA[:], in_=POb[:, 0:TW], func=AF.Square), [pok], ["sqoA"])
                S.add("act", lambda e: e.activation(out=sqoB[:], in_=POb[:, 256:256 + TW], func=AF.Square), [pok], ["sqoB"])
                yield
                ptf, ptk = next_pb()
                for bi, sqo in ((0, sqoA), (1, sqoB)):
                    for s_ in range(NSUB):
                        S.add("pe", lambda e, sqo=sqo, s_=s_, bi=bi, ptf=ptf: e.matmul(ptf[:, bi * 2 + s_:bi * 2 + s_ + 1],
                                                                                      lhsT=sqo[:, s_ * 128:(s_ + 1) * 128], rhs=onescol[:, 0:1],
                                                                                      start=True, stop=True), ["sqoA", "sqoB"], [ptk])
                S.add("dve", lambda e, ptf=ptf: e.tensor_scalar(out=rsv[:], in0=ptf[:, 0:4], scalar1=1.0 / 128.0, scalar2=EPS,
                                                                op0=ALU.mult, op1=ALU.add), [ptk], ["rsv"])
                S.add("pool", lambda e: e.tensor_tensor(out=rsv[:], in0=rsv[:], in1=mh4[:], op=ALU.pow), ["rsv"], ["rsv"])
                proj_gate(4, srz, AF.Silu, "srz")
                yield
                S.add("dve", lambda e: e.scalar_tensor_tensor(out=uA[:], in0=POb[:, 0:TW], scalar=gA[:, hs], in1=sz[:],
                                                              op0=ALU.mult, op1=ALU.mult), [pok, "sz"], ["uA"])
                S.add("dve", lambda e: e.scalar_tensor_tensor(out=uB[:], in0=POb[:, 256:256 + TW], scalar=gB[:, hs], in1=srz[:],
                                                              op0=ALU.mult, op1=ALU.mult), [pok, "srz"], ["uB"])
                for k_ in range(2 * NSUB):
                    S.add("act", lambda e, k_=k_: e.activation(out=dgh[:, k_, :], in_=identf[:], func=AF.Identity, scale=rsv[:, k_:k_ + 1]),
                          ["rsv"], ["dgh%d" % k_])
                    S.add("dve", lambda e, k_=k_: e.scalar_tensor_tensor(out=dgl[:, k_, :], in0=identf[:], scalar=rsv[:, k_:k_ + 1], in1=dgh[:, k_, :],
                                                                         op0=ALU.mult, op1=ALU.subtract), ["rsv", "dgh%d" % k_], ["dgl%d" % k_])
                proj_gate(3, tga, AF.Tanh, "tga", scale=0.5)
                yield
                pbc, pbk = next_pb()
                for k_ in range(2 * NSUB):
                    S.add("pe", lambda e, k_=k_, pbc=pbc: e.matmul(pbc[:, k_ * 128:(k_ + 1) * 128], lhsT=onesb[:], rhs=dgh[:, k_, :],
                                                                   start=True, stop=False), ["dgh%d" % k_], [pbk])
                    S.add("pe", lambda e, k_=k_, pbc=pbc: e.matmul(pbc[:, k_ * 128:(k_ + 1) * 128], lhsT=onesb[:], rhs=dgl[:, k_, :],
                                                                   start=False, stop=True), ["dgl%d" % k_], [pbk])
                proj_gate(5, tgb, AF.Tanh, "tgb", scale=0.5)
                yield
                S.add("dve", lambda e, pbc=pbc: e.tensor_tensor(out=uA[:], in0=uA[:], in1=pbc[:, 0:TW], op=ALU.mult), ["uA", pbk], ["uA"])
                S.add("dve", lambda e, pbc=pbc: e.tensor_tensor(out=uB[:], in0=uB[:], in1=pbc[:, 256:256 + TW], op=ALU.mult), ["uB", pbk], ["uB"])
                yield
                S.add("dve", lambda e: e.scalar_tensor_tensor(out=uA[:], in0=tga[:], scalar=1.0, in1=uA[:], op0=ALU.add, op1=ALU.mult),
                      ["uA", "tga"], ["uA"])
                S.add("dve", lambda e: e.scalar_tensor_tensor(out=uB[:], in0=tgb[:], scalar=1.0, in1=uB[:], op0=ALU.add, op1=ALU.mult),
                      ["uB", "tgb"], ["uB"])
                tap("uA", uA[:], ["uA"]); tap("uB", uB[:], ["uB"])
                S.add("pool", lambda e: e.tensor_tensor(out=mT[:, h, tok0:tok0 + TW], in0=uA[:], in1=uB[:], op=ALU.add), ["uA", "uB"], ["mT%d" % T])
                yield

            from itertools import zip_longest
            import os
            G = n_heads_run * n_tiles_run
            for _ in front(0):
                pass
            TLOAD = min(2, n_tiles_run - 1)
            for i in range(G + 1):
                if i < G:
                    h_i, T_i = divmod(i, n_tiles_run)
                    if T_i == TLOAD and h_i + 1 < n_heads_run:
                        load_w(h_i + 1)
                gens = []
                if i + 1 < G:
                    gens.append(front(i + 1))
                if i < G:
                    gens.append(back(i))
                if i >= 1:
                    gens.append(back2(i - 1))
                for _ in zip_longest(*gens):
                    pass
            if "mT" in dbg_d:
                for j in range(8):
                    dma("sp", dbg_d["mT"][j * 128:(j + 1) * 128, :], mT[:, j, :],
                        reads=["mT%d" % t for t in range(NT)], writes=["dbg_mT"])
            S.fence()

        with ExitStack() as es3:
            sb3 = lambda name, shape, dtype=F32: es3.enter_context(nc.sbuf_tensor("s3_" + name, shape, dtype))
            Wo = sb3("wo", [128, 8, D], BF16)
            fg_b = sb3("fg_b", [128, D])
            xt3 = [sb3("xt%d" % i, [128, D]) for i in range(2)]
            xn = [sb3("xn%d" % i, [128, D]) for i in range(2)]
            ot = [sb3("ot%d" % i, [128, D]) for i in range(2)]
            junk3 = sb3("junk", [128, D], BF16)
            ss3 = sb3("ss3", [128, 2]); rs3 = sb3("rs3", [128, 2]); mh3 = sb3("mh3", [128, 1])
            for half in range(2):
                dma("pool", Wo[:, half * 4:(half + 1) * 4, :], wo_d[half * 512:(half + 1) * 512, :].rearrange("(j p) c -> p j c", p=128),
                    writes=["wo%d" % half])
            dma("sp", fg_b[:], bcast_rows(fg_d, D), writes=["fg_b"])
            S.add("pool", lambda e: e.memset(mh3[:], -0.5), [], ["mh3"])
            for ti in range(L // 128):
                b = ti % 2
                xk = "x3_%d" % b
                dma("sp", xt3[b][:], x_d[ti * 128:(ti + 1) * 128, :], writes=[xk])
                PB3 = [PB[0], PB[1], PB[2], PTf]
                for g in range(2):
                    pb, pk = PB3[b * 2 + g], "pb3_%d" % (b * 2 + g)
                    for j in range(8):
                        S.add("pe", lambda e, j=j, g=g, pb=pb, ti=ti: e.matmul(pb[:], lhsT=mT[:, j, ti * 128:(ti + 1) * 128],
                                                                              rhs=Wo[:, j, g * 512:(g + 1) * 512], start=(j == 0), stop=(j == 7)),
                              ["wo0", "wo1", "mT"], [pk])
                    S.add("dve", lambda e, g=g, pb=pb, b=b: e.tensor_tensor(out=xn[b][:, g * 512:(g + 1) * 512], in0=pb[:],
                                                                            in1=gate_b[:, g * 512:(g + 1) * 512], op=ALU.mult),
                          [pk, "gate_b"], ["xn%d_%d" % (b, g)])
                xnk = ["xn%d_0" % b, "xn%d_1" % b]
                S.add("pool", lambda e, b=b: e.tensor_tensor(out=xn[b][:], in0=xn[b][:], in1=xt3[b][:], op=ALU.add), xnk + [xk], xnk)
                S.add("act", lambda e, b=b: e.activation(out=junk3[:], in_=xn[b][:], func=AF.Square, accum_out=ss3[:, b:b + 1]),
                      xnk, ["junk3", "ss3_%d" % b])
                S.add("dve", lambda e, b=b: e.tensor_scalar(out=ss3[:, b:b + 1], in0=ss3[:, b:b + 1], scalar1=1.0 / D, scalar2=EPS,
                                                            op0=ALU.mult, op1=ALU.add), ["ss3_%d" % b], ["ss3_%d" % b])
                S.add("pool", lambda e, b=b: e.tensor_tensor(out=rs3[:, b:b + 1], in0=ss3[:, b:b + 1], in1=mh3[:], op=ALU.pow),
                      ["ss3_%d" % b, "mh3"], ["rs3_%d" % b])
                S.add("dve", lambda e, b=b: e.scalar_tensor_tensor(out=ot[b][:], in0=xn[b][:], scalar=rs3[:, b:b + 1], in1=fg_b[:],
                                                                   op0=ALU.mult, op1=ALU.mult), xnk + ["rs3_%d" % b, "fg_b"], ["ot%d" % b])
                dma("sp", out_d[ti * 128:(ti + 1) * 128, :], ot[b][:], reads=["ot%d" % b], writes=["out"])

        S.fence()
        for sm in list(sems.values()) + dma_sems + sw_sems:
            nc.gpsimd.sem_clear(sm)
        nc.all_engine_barrier()
        with nc.Block() as block:
            S.emit(nc, block, sems, dma_sems, sw_sems)
        nc.all_engine_barrier()
    return nc


def _perm_cols():
    offs = np.cumsum([0, 1024, 1024, 1024, 1024, 512, 512, 1024, 1024, 1024, 1024])
    o_hq, o_hf, o_hi, o_hz, o_rq, o_rk, o_rv, o_rz, o_ga, o_gb = offs[:10]
    cols = []
    for h in range(NH):
        c = []
        for o in (o_hq, o_hf, o_hz, o_ga, o_rz, o_gb, o_hi, o_rv):
            c += list(range(o + h * 128, o + (h + 1) * 128))
        c += list(range(o_rq + h * 64, o_rq + (h + 1) * 64))
        c += list(range(o_rk + h * 64, o_rk + (h + 1) * 64))
        cols.append(c)
    return np.array(cols)


def make_in_maps(x, c, norm_g, w_ada, b_ada, w_in, hg_lb_logits, hg_norm_g, ret_norm_g, w_out, final_g):
    f = lambda a: np.ascontiguousarray(np.asarray(a, dtype=np.float32))
    x, c = f(x), f(c)
    cols = _perm_cols()
    w_in0 = f(w_in)[0]
    wp = np.ascontiguousarray(np.stack([w_in0[:, cols[h]] for h in range(NH)], 0))
    lb = f(hg_lb_logits)
    lbl = np.ascontiguousarray(np.concatenate([lb[0].reshape(8, 128).T, lb[1].reshape(8, 128).T], 1))
    shared = {
        "w_ada": f(w_ada)[0], "b_ada": f(b_ada)[0].reshape(1, -1), "norm_g": f(norm_g)[0].reshape(1, -1),
        "final_g": f(final_g).reshape(1, -1), "w_in_p": wp, "lbl": lbl,
        "hg_g": np.ascontiguousarray(f(hg_norm_g)[0].reshape(8, 128).T),
        "ret_g": np.ascontiguousarray(f(ret_norm_g)[0].reshape(8, 128).T),
        "w_out": f(w_out)[0],
        "cos_t": _TAB["cos"].reshape(128, -1), "sin_t": _TAB["sin"].reshape(128, -1),
        "xi_t": _TAB["xi"], "zeta_t": _TAB["zeta"], "mask_r": _TAB["mask_r"], "mask_h": _TAB["mask_h"],
        "ident": _TAB["ident"],
    }
    maps = []
    for b in range(8):
        m = dict(shared)
        m["x"] = x[b]
        m["c_col"] = np.ascontiguousarray(c[b].reshape(8, 128).T)
        maps.append(m)
    return maps


def kernel(x, c, norm_g, w_ada, b_ada, w_in, hg_lb_logits, hg_norm_g, ret_norm_g, w_out, final_g):
    maps = make_in_maps(x, c, norm_g, w_ada, b_ada, w_in, hg_lb_logits, hg_norm_g, ret_norm_g, w_out, final_g)
    nc = build_program()
    res = run_bass_kernel_spmd(nc, maps, core_ids=list(range(8)))
    return np.stack([np.asarray(r["out"], dtype=np.float32) for r in res.results], 0)
```

```python
import numpy as np
import concourse.bass as bass
import concourse.mybir as mybir
from concourse.bass_utils import run_bass_kernel_spmd

F32 = mybir.dt.float32
BF16 = mybir.dt.bfloat16
AF = mybir.ActivationFunctionType
ALU = mybir.AluOpType

D = 1024
L = 4096
NH = 8
TW = 256
NT = L // TW
NSUB = TW // 128
EPS = 1e-6
WCOLS = 1152
ENGS = ("pe", "act", "dve", "pool", "sp")


class _Op:
    __slots__ = ("eng", "fn", "deps", "signal", "tok", "dma", "name")

    def __init__(self, eng, fn, dma, name):
        self.eng, self.fn, self.dma, self.name = eng, fn, dma, name
        self.deps = []
        self.signal = False
        self.tok = None


class Sched:
    def __init__(self, n_dma_sems=12):
        self.ops = {e: [] for e in ENGS}
        self.last_w = {}
        self.readers = {}
        self.n_dma_sems = n_dma_sems
        self.all_ops = []

    def add(self, eng, fn, reads=(), writes=(), dma=False, name=""):
        op = _Op(eng, fn, dma, name)
        deps = []
        for k in reads:
            w = self.last_w.get(k)
            if w is not None:
                deps.append(w)
        for k in writes:
            w = self.last_w.get(k)
            if w is not None:
                deps.append(w)
            for r in self.readers.get(k, ()):
                deps.append(r)
        seen = set()
        for d in deps:
            if d is op or id(d) in seen:
                continue
            if eng == "pe" and d.eng == "pe" and not d.dma and not dma:
                continue
            seen.add(id(d))
            op.deps.append(d)
            d.signal = True
        for k in reads:
            self.readers.setdefault(k, []).append(op)
        for k in writes:
            self.last_w[k] = op
            self.readers[k] = []
        self.ops[eng].append(op)
        self.all_ops.append(op)
        return op

    def fence(self):
        lasts = [self.ops[e][-1] for e in ENGS if self.ops[e]]
        dmas = [o for o in self.all_ops if o.dma]
        for e in ENGS:
            op = _Op(e, None, False, "fence")
            for d in lasts + dmas:
                if d.eng == e and not d.dma:
                    continue
                op.deps.append(d)
                d.signal = True
            self.ops[e].append(op)
            self.all_ops.append(op)
        self.last_w = {}
        self.readers = {}

    def emit(self, nc, block, sems, dma_sems, sw_sems):
        cnt = {e: 0 for e in ENGS}
        dcnt = [0] * len(dma_sems)
        dnext = 0
        swnext = 0
        dma_prev = {}
        for e in ENGS:
            pass
        for op in self.all_ops:
            if op.fn is None:
                continue
            if op.dma and op.eng == "pool":
                op.tok = ("w", swnext, 16)
                swnext += 1
                op.signal = True
            elif op.dma:
                i = dnext % len(dma_sems)
                dnext += 1
                dcnt[i] += 16
                op.tok = ("d", i, dcnt[i])
                prev = dma_prev.get(i)
                if prev is not None:
                    op.deps.append(prev)
                dma_prev[i] = op
                op.signal = True
            elif op.signal:
                cnt[op.eng] += 1
                op.tok = ("e", op.eng, cnt[op.eng])

        def run(engname, eng):
            waited = {}
            for op in self.ops[engname]:
                for d in op.deps:
                    t = d.tok
                    if t is None:
                        continue
                    key = (t[0], t[1])
                    if waited.get(key, 0) >= t[2]:
                        continue
                    waited[key] = t[2]
                    sem = dma_sems[t[1]] if t[0] == "d" else (sw_sems[t[1]] if t[0] == "w" else sems[t[1]])
                    eng.wait_ge(sem, t[2])
                if op.fn is None:
                    continue
                ins = op.fn(eng)
                if op.tok is not None:
                    if op.tok[0] == "d":
                        ins.then_inc(dma_sems[op.tok[1]], 16)
                    elif op.tok[0] == "w":
                        ins.then_inc(sw_sems[op.tok[1]], 16)
                    else:
                        ins.then_inc(sems[engname], 1)

        @block.tensor
        def _(eng):
            run("pe", eng)

        @block.scalar
        def _(eng):
            run("act", eng)

        @block.vector
        def _(eng):
            run("dve", eng)

        @block.gpsimd
        def _(eng):
            run("pool", eng)

        @block.sync
        def _(eng):
            run("sp", eng)


def _tables():
    try:
        import jax
        import jax.numpy as jnp
        with jax.default_device(jax.devices("cpu")[0]):
            inv_freq = 1.0 / (10000.0 ** jnp.linspace(0.0, 1.0, 32, dtype=jnp.float32))
            ang = jnp.arange(L, dtype=jnp.int32).astype(jnp.float32)[:, None] * inv_freq[None, :]
            cos = np.asarray(jnp.cos(ang), dtype=np.float32)
            sin = np.asarray(jnp.sin(ang), dtype=np.float32)
            log_gamma = np.asarray(jnp.log(1.0 - jnp.exp2(-5.0 - jnp.arange(NH, dtype=jnp.float32))), dtype=np.float32)
    except Exception:
        inv_freq = (1.0 / (np.float32(10000.0) ** np.linspace(0.0, 1.0, 32, dtype=np.float32))).astype(np.float32)
        ang = (np.arange(L, dtype=np.float32)[:, None] * inv_freq[None, :]).astype(np.float32).astype(np.float64)
        cos = np.cos(ang).astype(np.float32)
        sin = np.sin(ang).astype(np.float32)
        log_gamma = np.log(np.float32(1.0) - np.exp2(np.float32(-5.0) - np.arange(NH, dtype=np.float32))).astype(np.float32)
    cos = cos.reshape(L // 128, 128, 32).transpose(1, 0, 2)
    sin = sin.reshape(L // 128, 128, 32).transpose(1, 0, 2)
    lg = log_gamma.astype(np.float64)
    i = np.arange(128, dtype=np.float64)
    xi = np.exp(lg[None, :] * (i[:, None] + 1.0))
    zeta = np.exp(lg[None, :] * (127.0 - i[:, None])) * (64.0 ** -0.5)
    g128 = np.exp(lg * 128.0)
    ginv = np.exp(-lg * 128.0)
    s = np.arange(128)[:, None]
    t = np.arange(128)[None, :]
    mask_r = (t >= s).astype(np.float32)
    mask_h = ((t >= s) & ((t // 64) == (s // 64))).astype(np.float32)
    ident = np.eye(128, dtype=np.float32)
    return dict(cos=np.ascontiguousarray(cos), sin=np.ascontiguousarray(sin),
                xi=xi.astype(np.float32), zeta=zeta.astype(np.float32),
                g128=g128, ginv=ginv, mask_r=mask_r, mask_h=mask_h, ident=ident)


_TAB = _tables()


def build_program(dbg=None, n_heads_run=NH, n_tiles_run=NT, dbg_T=0):
    nc = bass.Bass("TRN2", target_bir_lowering=False)
    dt = lambda name, shape, dtype=F32, kind="ExternalInput": nc.dram_tensor(name, shape, dtype, kind=kind).ap()
    x_d = dt("x", [L, D])
    ccol_d = dt("c_col", [128, 8])
    wada_d = dt("w_ada", [D, 3 * D])
    bada_d = dt("b_ada", [1, 3 * D])
    ng_d = dt("norm_g", [1, D])
    fg_d = dt("final_g", [1, D])
    wp_d = dt("w_in_p", [NH, D, WCOLS])
    lbl_d = dt("lbl", [128, 16])
    ga_d = dt("hg_g", [128, 8])
    gb_d = dt("ret_g", [128, 8])
    wo_d = dt("w_out", [D, D])
    cos_d = dt("cos_t", [128, 32 * 32])
    sin_d = dt("sin_t", [128, 32 * 32])
    xi_d = dt("xi_t", [128, 8])
    zeta_d = dt("zeta_t", [128, 8])
    maskr_d = dt("mask_r", [128, 128])
    maskh_d = dt("mask_h", [128, 128])
    ident_d = dt("ident", [128, 128])
    out_d = dt("out", [L, D], kind="ExternalOutput")
    dbg_d = {}
    if dbg:
        for name, shape, dtype in dbg:
            dbg_d[name] = dt("dbg_" + name, shape, dtype, kind="ExternalOutput")

    S = Sched()
    g128 = [float(v) for v in _TAB["g128"]]
    ginv = [float(v) for v in _TAB["ginv"]]

    def bcast_rows(ap2d, n):
        return bass.AP(ap2d.tensor, ap2d.offset, [[0, 128], [1, n]])

    from contextlib import ExitStack
    with nc.cleanup_on_exit(), ExitStack() as es:
        sb = lambda name, shape, dtype=F32: es.enter_context(nc.sbuf_tensor("sb_" + name, shape, dtype))
        ps = lambda name, shape, dtype=F32: es.enter_context(nc.psum_tensor("ps_" + name, shape, dtype))
        hT = sb("hT", [128, 8, L], BF16)
        mT = sb("mT", [128, 8, L], BF16)
        gate_b = sb("gate_b", [128, D])
        identb = sb("identb", [128, 128], BF16)
        onesb = sb("onesb", [128, 128], BF16)
        maskh = sb("maskh", [128, 128])
        maskr = sb("maskr", [128, 128])
        xi_t = sb("xi_t", [128, 8])
        zeta_t = sb("zeta_t", [128, 8])
        c0 = sb("c0", [128, 8])
        c1 = sb("c1", [128, 8])
        nc1 = sb("nc1", [128, 8])
        gA = sb("gA", [128, 8])
        gB = sb("gB", [128, 8])
        zer = sb("zer", [128, 64])
        mh4 = sb("mh4", [128, 4])
        identf = sb("identf", [128, 128])
        onesf = sb("onesf", [128, 128])
        onescol = sb("onescol", [128, 2], BF16)
        PB = [ps("pb%d" % i, [128, 512]) for i in range(3)]
        PTf = ps("ptr", [128, 512])
        PT = PTf[:].bitcast(BF16)
        P4 = ps("p4", [128, 512])
        P5 = ps("p5", [128, 512])
        P6 = ps("p6", [128, 512])
        P7 = ps("p7", [128, 512])

        sems = {e: nc.alloc_semaphore("s_" + e) for e in ENGS}
        dma_sems = [nc.alloc_semaphore("d%d" % i) for i in range(S.n_dma_sems)]
        sw_sems = [nc.alloc_semaphore("w%d" % i) for i in range(2 * NH + 2)]

        def dma(eng, out, in_, reads=(), writes=(), name=""):
            return S.add(eng, lambda e: e.dma_start(out=out, in_=in_), reads, writes, dma=True, name=name)

        def dbg_out(name, src_ap, key, dst=None):
            if name in dbg_d:
                dma("sp", dbg_d[name] if dst is None else dst, src_ap, reads=[key], writes=["dbg_" + name])

        with ExitStack() as es0:
            sb0 = lambda name, shape, dtype=F32: es0.enter_context(nc.sbuf_tensor("s0_" + name, shape, dtype))
            ccol = sb0("ccol", [128, 8])
            scol = sb0("scol", [128, 8])
            screp = sb0("screp", [128, 8, 128])
            wa = [sb0("wa%d" % i, [128, 8, 256]) for i in range(2)]
            ng_b = sb0("ng_b", [128, D])
            mod_b = sb0("mod_b", [128, 2 * D])
            g1_b = sb0("g1_b", [128, D])
            lbl = sb0("lbl", [128, 16])
            dl = sb0("dl", [128, 8])
            thl = sb0("thl", [128, 8])
            graw = sb0("graw", [128, 16])
            xt = [sb0("xt%d" % i, [128, D]) for i in range(2)]
            junk = sb0("junk", [128, D], BF16)
            t1 = [sb0("t1_%d" % i, [128, D]) for i in range(2)]
            hb = [sb0("hb%d" % i, [128, D], BF16) for i in range(2)]
            ssq = sb0("ssq", [128, 2])
            rstd = sb0("rstd", [128, 2])
            mh1 = sb0("mh1", [128, 1])

            dma("sp", identf[:], ident_d, writes=["identf"])
            dma("sp", maskh[:], maskh_d, writes=["maskh"])
            dma("sp", maskr[:], maskr_d, writes=["maskr"])
            dma("sp", xi_t[:], xi_d, writes=["xi"])
            dma("sp", zeta_t[:], zeta_d, writes=["zeta"])
            dma("sp", ccol[:], ccol_d, writes=["ccol"])
            dma("sp", lbl[:], lbl_d, writes=["lbl"])
            dma("sp", graw[:, 0:8], ga_d, writes=["graw0"])
            dma("sp", graw[:, 8:16], gb_d, writes=["graw1"])
            dma("sp", mod_b[:], bcast_rows(bada_d[:, 0:2 * D], 2 * D), writes=["bias_b"])
            dma("sp", gate_b[:], bcast_rows(bada_d[:, 2 * D:3 * D], D), writes=["bias_b"])
            dma("sp", ng_b[:], bcast_rows(ng_d, D), writes=["ng_b"])

            S.add("dve", lambda e: e.tensor_copy(out=identb[:], in_=identf[:]), ["identf"], ["identb"])
            S.add("pool", lambda e: e.memset(onesb[:], 1.0), [], ["onesb"])
            S.add("pool", lambda e: e.memset(zer[:], 0.0), [], ["zer"])
            S.add("pool", lambda e: e.memset(mh4[:], -0.5), [], ["mh4"])
            S.add("pool", lambda e: e.memset(onesf[:], 1.0), [], ["onesf"])
            S.add("pool", lambda e: e.memset(onescol[:], 1.0), [], ["onescol"])
            S.add("pool", lambda e: e.memset(mh1[:], -0.5), [], ["mh1"])
            S.add("dve", lambda e: e.tensor_tensor(out=dl[:], in0=lbl[:, 0:8], in1=lbl[:, 8:16], op=ALU.subtract),
                  ["lbl"], ["dl"])
            S.add("act", lambda e: e.activation(out=thl[:], in_=dl[:], func=AF.Tanh, scale=0.5), ["dl"], ["thl"])
            S.add("dve", lambda e: e.tensor_scalar(out=c0[:], in0=thl[:], scalar1=0.25, scalar2=0.75,
                                                   op0=ALU.mult, op1=ALU.add), ["thl"], ["c0"])
            S.add("dve", lambda e: e.tensor_scalar(out=c1[:], in0=thl[:], scalar1=-0.25, scalar2=0.25,
                                                   op0=ALU.mult, op1=ALU.add), ["thl"], ["c1"])
            S.add("dve", lambda e: e.tensor_scalar(out=nc1[:], in0=thl[:], scalar1=0.25, scalar2=-0.25,
                                                   op0=ALU.mult, op1=ALU.add), ["thl"], ["nc1"])
            S.add("dve", lambda e: e.tensor_scalar(out=gA[:], in0=graw[:, 0:8], scalar1=0.5, scalar2=None,
                                                   op0=ALU.mult), ["graw0"], ["gA"])
            S.add("dve", lambda e: e.tensor_scalar(out=gB[:], in0=graw[:, 8:16], scalar1=0.5, scalar2=None,
                                                   op0=ALU.mult), ["graw1"], ["gB"])
            S.add("act", lambda e: e.activation(out=scol[:], in_=ccol[:], func=AF.Silu), ["ccol"], ["scol"])
            S.add("dve", lambda e: e.tensor_copy(out=screp[:], in_=scol[:].unsqueeze(2).to_broadcast([128, 8, 128])),
                  ["scol"], ["screp"])
            for g in range(12):
                wb = wa[g % 2]
                wk = "wa%d" % (g % 2)
                dma("sp", wb[:], wada_d[:, g * 256:(g + 1) * 256].rearrange("(j p) c -> p j c", p=128), writes=[wk])
                pb = PB[g % 3]
                pk = "pb%d" % (g % 3)
                for j in range(8):
                    S.add("pe", lambda e, j=j, wb=wb, pb=pb: e.matmul(pb[:, 0:256], lhsT=screp[:, j, :], rhs=wb[:, j, :],
                                                                     start=(j == 0), stop=(j == 7)),
                          ["screp", wk], [pk])
                if g < 8:
                    dst = mod_b[:, g * 256:(g + 1) * 256]
                else:
                    dst = gate_b[:, (g - 8) * 256:(g - 7) * 256]
                S.add("dve", lambda e, dst=dst, pb=pb: e.tensor_tensor(out=dst, in0=pb[:, 0:256], in1=dst, op=ALU.add),
                      [pk, "bias_b"], ["modg%d" % g])
            S.add("dve", lambda e: e.scalar_tensor_tensor(out=g1_b[:], in0=mod_b[:, D:2 * D], scalar=1.0, in1=ng_b[:],
                                                          op0=ALU.add, op1=ALU.mult),
                  ["modg%d" % g for g in range(4, 8)] + ["ng_b"], ["g1_b"])
            dbg_out("mod", mod_b[0:1, :], "g1_b")

            for ti in range(L // 128):
                b = ti % 2
                xk, hk = "xt%d" % b, "hb%d" % b
                dma("sp", xt[b][:], x_d[ti * 128:(ti + 1) * 128, :], writes=[xk])
                S.add("act", lambda e, b=b: e.activation(out=junk[:], in_=xt[b][:], func=AF.Square,
                                                         accum_out=ssq[:, b:b + 1]), [xk], ["junk", "ssq%d" % b])
                S.add("dve", lambda e, b=b: e.tensor_scalar(out=ssq[:, b:b + 1], in0=ssq[:, b:b + 1], scalar1=1.0 / D,
                                                            scalar2=EPS, op0=ALU.mult, op1=ALU.add),
                      ["ssq%d" % b], ["ssq%d" % b])
                S.add("pool", lambda e, b=b: e.tensor_tensor(out=rstd[:, b:b + 1], in0=ssq[:, b:b + 1], in1=mh1[:],
                                                             op=ALU.pow), ["ssq%d" % b, "mh1"], ["rstd%d" % b])
                S.add("dve", lambda e, b=b: e.scalar_tensor_tensor(out=t1[b][:], in0=xt[b][:], scalar=rstd[:, b:b + 1],
                                                                   in1=g1_b[:], op0=ALU.mult, op1=ALU.mult),
                      [xk, "rstd%d" % b, "g1_b"], ["t1_%d" % b])
                S.add("pool", lambda e, b=b: e.tensor_tensor(out=hb[b][:], in0=t1[b][:], in1=mod_b[:, 0:D], op=ALU.add),
                      ["t1_%d" % b] + ["modg%d" % g for g in range(4)], [hk])
                ptb1 = (PTf if b == 0 else PB[0])[:].bitcast(BF16)
                ptk1 = "PT1_%d" % b
                for j in range(8):
                    S.add("pe", lambda e, b=b, j=j, ptb1=ptb1: e.transpose(out=ptb1[:, j * 128:(j + 1) * 128],
                                                                           in_=hb[b][:, j * 128:(j + 1) * 128], identity=identb[:]),
                          [hk, "identb"], [ptk1])
                S.add("act", lambda e, ti=ti, ptb1=ptb1: e.activation(out=hT[:, :, ti * 128:(ti + 1) * 128],
                                                                      in_=ptb1[:].rearrange("p (j t) -> p j t", j=8), func=AF.Copy),
                      [ptk1], ["hT%d" % (ti * 128 // TW)])
            if "hT" in dbg_d:
                for j in range(8):
                    dma("sp", dbg_d["hT"][j * 128:(j + 1) * 128, :], hT[:, j, :],
                        reads=["hT%d" % t for t in range(NT)], writes=["dbg_hT"])
            S.fence()


        with ExitStack() as es2:
            sb2 = lambda name, shape, dtype=F32: es2.enter_context(nc.sbuf_tensor("s2_" + name, shape, dtype))
            D2 = lambda name, shape, dtype=F32: [sb2("%s_%d" % (name, i), shape, dtype) for i in range(2)]
            Wh = D2("wh", [128, 8, WCOLS], BF16)
            cs = D2("cs", [128, NSUB, 32]); sn = D2("sn", [128, NSUB, 32])
            th = sb2("th", [128, TW]); sq = th
            kk = sb2("kk", [128, TW]); ff = sb2("ff", [128, TW]); RR = ff
            sz = sb2("sz", [128, TW]); tga = sb2("tga", [128, TW]); srz = sb2("srz", [128, TW]); tgb = sb2("tgb", [128, TW])
            PP = D2("PP", [128, TW])
            Qt = D2("Qt", [128, TW], BF16); Kt = D2("Kt", [128, TW], BF16)
            Ktm = D2("Ktm", [128, NSUB, 2, 128], BF16)
            vAB = D2("vAB", [128, NSUB, 256], BF16)
            qkr = D2("qkr", [128, NSUB, 256], BF16)
            qkT = D2("qkT", [128, 2, TW], BF16)
            At = sb2("At", [128, NSUB, 128], BF16)
            Sc = sb2("Sc", [128, NSUB, 128], BF16)
            Sst = D2("Sst", [128, 128])
            NSB = 4
            Sbf = [sb2("Sbf%d" % i, [128, 128], BF16) for i in range(NSB)]
            NRB = 4
            Rst = D2("Rst", [128, 128])
            Rbf = [sb2("Rbf%d" % i, [128, 128], BF16) for i in range(NRB)]
            qk = sb2("qk", [128, NSUB, 128])
            ra = sb2("ra", [128, NSUB, 2, 32]); rb = sb2("rb", [128, NSUB, 2, 32])
            rc = sb2("rc", [128, NSUB, 2, 32]); rd = sb2("rd", [128, NSUB, 2, 32])
            sqoA = sb2("sqoA", [128, TW], BF16); sqoB = sb2("sqoB", [128, TW], BF16)
            uA = sb2("uA", [128, TW]); uB = sb2("uB", [128, TW])
            dgh = sb2("dgh", [128, 2 * NSUB, 128], BF16); dgl = sb2("dgl", [128, 2 * NSUB, 128], BF16)
            rsv = sb2("rsv", [128, 4])

            def load_w(h):
                hb_ = h % 2
                for half in range(2):
                    dma("pool", Wh[hb_][:, half * 4:(half + 1) * 4, :],
                        wp_d[h, half * 512:(half + 1) * 512, :].rearrange("(j p) c -> p j c", p=128),
                        writes=["wh%d_%d" % (hb_, half)])

            load_w(0)
            pbrr = [0]
            for p_ in range(2):
                S.add("pool", lambda e, p_=p_: e.memset(Ktm[p_][:], 0.0), [], ["Ktm0_%d" % p_, "Ktm1_%d" % p_])
                S.add("pool", lambda e, p_=p_: e.memset(qkT[p_][:], 0.0), [], ["qkT_%d" % p_])
                S.add("pool", lambda e, p_=p_: e.memset(qkr[p_][:], 0.0), [], ["qkr_a_%d" % p_, "qkr_b_%d" % p_])
            for i_ in range(NRB):
                S.add("pool", lambda e, i_=i_: e.memset(Rbf[i_][:], 0.0), [], ["Rbf%d" % i_])

            PB4 = PB + [PTf]
            PB4b = [b_[:].bitcast(BF16) for b_ in PB4]

            def next_pb(bf=False):
                i = pbrr[0] % 4
                pbrr[0] += 1
                return (PB4b[i] if bf else PB4[i]), "pb%d" % i

            def front(g):
                h, T = divmod(g, n_tiles_run)
                p = g % 2
                K = lambda nm: "%s_%d" % (nm, p)
                hb_ = h % 2
                W = Wh[hb_]
                whk = ["wh%d_0" % hb_, "wh%d_1" % hb_]
                hs = slice(h, h + 1)
                tok0 = T * TW
                hTk = "hT%d" % T
                dma("sp", cs[p][:], cos_d[:, T * NSUB * 32:(T + 1) * NSUB * 32].rearrange("p (s f) -> p s f", s=NSUB), writes=[K("cs")])
                dma("sp", sn[p][:], sin_d[:, T * NSUB * 32:(T + 1) * NSUB * 32].rearrange("p (s f) -> p s f", s=NSUB), writes=[K("sn")])

                def proj_fm(cblk):
                    pb, pk = next_pb()
                    for j in range(8):
                        S.add("pe", lambda e, j=j, pb=pb: e.matmul(pb[:, 0:TW], lhsT=W[:, j, cblk * 128:(cblk + 1) * 128],
                                                                   rhs=hT[:, j, tok0:tok0 + TW], start=(j == 0), stop=(j == 7)),
                              whk + [hTk], [pk])
                    return pb, pk

                pb, pk = proj_fm(1)
                S.add("act", lambda e, pb=pb: e.activation(out=th[:], in_=pb[:, 0:TW], func=AF.Tanh, scale=0.5), [pk], ["th"])
                S.add("act", lambda e: e.activation(out=kk[:], in_=th[:], func=AF.Identity, scale=nc1[:, hs], bias=c1[:, hs]), ["th"], ["kk"])
                S.add("act", lambda e: e.activation(out=ff[:], in_=th[:], func=AF.Identity, scale=c1[:, hs], bias=c0[:, hs]), ["th"], ["ff"])
                yield
                for c in range(TW // 64):
                    S.add("dve", lambda e, c=c: e.tensor_tensor_scan(out=PP[p][:, c * 64:(c + 1) * 64], data0=ff[:, c * 64:(c + 1) * 64],
                                                                    data1=zer[:, 0:64], initial=1.0, op0=ALU.mult, op1=ALU.add),
                          ["ff"], [K("PP%d" % c)])
                PPk = [K("PP%d" % c) for c in range(TW // 64)]
                S.add("dve", lambda e: e.reciprocal(out=RR[:], in_=PP[p][:]), PPk, ["ff"])
                pb, pk = proj_fm(0)
                S.add("act", lambda e, pb=pb: e.activation(out=sq[:], in_=pb[:, 0:TW], func=AF.Silu), [pk], ["th"])
                yield
                S.add("pool", lambda e: e.tensor_tensor(out=Qt[p][:], in0=sq[:], in1=PP[p][:], op=ALU.mult), ["th"] + PPk, [K("Qt")])
                S.add("pool", lambda e: e.tensor_tensor(out=Kt[p][:], in0=kk[:], in1=RR[:], op=ALU.mult), ["kk", "ff"], [K("Kt")])
                for s_ in range(NSUB):
                    pb, pk = next_pb()
                    for j in range(8):
                        S.add("pe", lambda e, j=j, pb=pb, s_=s_: e.matmul(pb[:, 0:384], lhsT=hT[:, j, tok0 + s_ * 128:tok0 + (s_ + 1) * 128],
                                                                          rhs=W[:, j, 768:1152], start=(j == 0), stop=(j == 7)),
                              whk + [hTk], [pk])
                    S.add("act", lambda e, pb=pb, s_=s_: e.activation(out=vAB[p][:, s_, :], in_=pb[:, 0:256], func=AF.Copy), [pk], [K("vAB%d" % s_)])
                    S.add("act", lambda e, pb=pb, s_=s_: e.activation(out=qk[:, s_, 0:64], in_=pb[:, 256:320], func=AF.Identity,
                                                                      scale=xi_t[:, hs]), [pk], ["qk%d" % s_])
                    S.add("act", lambda e, pb=pb, s_=s_: e.activation(out=qk[:, s_, 64:128], in_=pb[:, 320:384], func=AF.Identity,
                                                                      scale=zeta_t[:, hs]), [pk], ["qk%d" % s_])
                    yield
                ptb, ptk = next_pb(bf=True)
                for s_ in range(NSUB):
                    S.add("pe", lambda e, s_=s_, ptb=ptb: e.transpose(out=ptb[:, s_ * 128:(s_ + 1) * 128], in_=Kt[p][:, s_ * 128:(s_ + 1) * 128],
                                                                      identity=identb[:]), [K("Kt")], [ptk])
                for ci in range(2):
                    S.add("act", lambda e, ci=ci, ptb=ptb: e.activation(out=Ktm[p][ci * 64:(ci + 1) * 64, :, ci, :],
                                                                        in_=ptb[ci * 64:(ci + 1) * 64, 0:256].rearrange("p (s d) -> p s d", s=NSUB),
                                                                        func=AF.Copy), [ptk], [K("Ktm%d" % ci)])
                qk5 = qk[:].rearrange("p s (a b f) -> p s a b f", a=2, b=2)
                qa, qb = qk5[:, :, :, 0, :], qk5[:, :, :, 1, :]
                qr6 = qkr[p][:].rearrange("p s (a z b f) -> p s a z b f", a=2, z=2, b=2)
                cosb = cs[p][:].unsqueeze(2).to_broadcast([128, NSUB, 2, 32])
                sinb = sn[p][:].unsqueeze(2).to_broadcast([128, NSUB, 2, 32])
                qkk = ["qk%d" % s_ for s_ in range(NSUB)]
                S.add("dve", lambda e: e.tensor_tensor(out=ra[:], in0=qa, in1=cosb, op=ALU.mult), qkk + [K("cs")], ["ra"])
                S.add("dve", lambda e: e.tensor_tensor(out=rb[:], in0=qb, in1=sinb, op=ALU.mult), qkk + [K("sn")], ["rb"])
                S.add("dve", lambda e: e.tensor_tensor(out=qr6[:, :, :, 0, 0, :], in0=ra[:], in1=rb[:], op=ALU.subtract), ["ra", "rb"], [K("qkr_a")])
                S.add("pool", lambda e: e.tensor_tensor(out=rc[:], in0=qa, in1=sinb, op=ALU.mult), qkk + [K("sn")], ["rc"])
                S.add("pool", lambda e: e.tensor_tensor(out=rd[:], in0=qb, in1=cosb, op=ALU.mult), qkk + [K("cs")], ["rd"])
                S.add("pool", lambda e: e.tensor_tensor(out=qr6[:, :, :, 0, 1, :], in0=rc[:], in1=rd[:], op=ALU.add), ["rc", "rd"], [K("qkr_b")])
                yield

            def back(g):
                h, T = divmod(g, n_tiles_run)
                p = g % 2
                K = lambda nm: "%s_%d" % (nm, p)
                hs = slice(h, h + 1)
                tok0 = T * TW
                gc0 = g * (TW // 64)
                rn0 = g * NSUB
                PPk = [K("PP%d" % c) for c in range(TW // 64)]
                qkrk = [K("qkr_a"), K("qkr_b")]
                POb, pok = (P6, "P6") if p == 0 else (P7, "P7")

                def tap(name, ap, keys):
                    if name in dbg_d and h == 0 and T == dbg_T:
                        dma("sp", dbg_d[name], ap, reads=keys, writes=["dbg_" + name])

                import os
                if int(os.environ.get("KCUT", "99")) <= -1:
                    return
                def emit_qkT():
                    qkrk = [K("qkr_a"), K("qkr_b")]
                    ptb, ptk = next_pb(bf=True)
                    for s_ in range(NSUB):
                        S.add("pe", lambda e, s_=s_, ptb=ptb: e.transpose(out=ptb[:, s_ * 128:(s_ + 1) * 128], in_=qkr[p][:, s_, 0:128],
                                                                          identity=identb[:]), qkrk, [ptk])
                        S.add("pe", lambda e, s_=s_, ptb=ptb: e.transpose(out=ptb[:, 256 + s_ * 128:256 + (s_ + 1) * 128], in_=qkr[p][:, s_, 128:256],
                                                                          identity=identb[:]), qkrk, [ptk])
                    S.add("act", lambda e, ptb=ptb: e.activation(out=qkT[p][:].rearrange("p a t -> p (a t)"), in_=ptb[:, 0:512], func=AF.Copy),
                          [ptk], [K("qkT")])
                for s_ in range(NSUB):
                    S.add("pe", lambda e, s_=s_: e.matmul(P4[:, s_ * 128:(s_ + 1) * 128], lhsT=Kt[p][:, s_ * 128:(s_ + 1) * 128],
                                                          rhs=Qt[p][:, s_ * 128:(s_ + 1) * 128], start=True, stop=True),
                          [K("Kt"), K("Qt")], ["P4"])
                for s_ in range(NSUB):
                    S.add("dve", lambda e, s_=s_: e.tensor_tensor(out=At[:, s_, :], in0=P4[:, s_ * 128:(s_ + 1) * 128], in1=maskh[:],
                                                                  op=ALU.mult), ["P4"], ["At%d" % s_])
                def emit_ret_pre():
                    for s_ in range(NSUB):
                        S.add("pe", lambda e, s_=s_: e.matmul(P4[:, 256 + s_ * 128:256 + (s_ + 1) * 128], lhsT=qkT[p][:, 1, s_ * 128:(s_ + 1) * 128],
                                                              rhs=qkT[p][:, 0, s_ * 128:(s_ + 1) * 128], start=True, stop=True),
                              [K("qkT")], ["P4"])
                    for s_ in range(NSUB):
                        S.add("pe", lambda e, s_=s_: e.matmul(P5[:, 256 + s_ * 128:256 + (s_ + 1) * 128], lhsT=qkr[p][:, s_, 128:256],
                                                              rhs=vAB[p][:, s_, 128:256], start=True, stop=True),
                              qkrk + [K("vAB%d" % s_)], ["P5"])
                    for s_ in range(NSUB):
                        S.add("dve", lambda e, s_=s_: e.scalar_tensor_tensor(out=Sc[:, s_, :], in0=P4[:, 256 + s_ * 128:256 + (s_ + 1) * 128],
                                                                             scalar=ginv[h], in1=maskr[:], op0=ALU.mult, op1=ALU.mult),
                              ["P4"], ["Sc%d" % s_])
                yield

                def emit_U(cc):
                    s_, ci = divmod(cc, 2)
                    pu = cc % 2
                    S.add("pe", lambda e: e.matmul(P5[:, pu * 128:(pu + 1) * 128], lhsT=Ktm[p][:, s_, ci, :],
                                                   rhs=vAB[p][:, s_, 0:128], start=True, stop=True),
                          [K("Ktm%d" % ci), K("vAB%d" % s_)], ["P5"])

                def emit_state(cc):
                    gcn = gc0 + cc
                    pu = cc % 2
                    first = (T == 0 and cc == 0)
                    so, sn_ = gcn % 2, (gcn + 1) % 2
                    plast = PP[p][:, cc * 64 + 63:cc * 64 + 64]
                    if first:
                        S.add("dve", lambda e: e.tensor_scalar(out=Sst[sn_][:], in0=P5[:, pu * 128:(pu + 1) * 128], scalar1=plast, scalar2=None,
                                                               op0=ALU.mult), ["P5", K("PP%d" % cc)], ["Sst%d" % sn_])
                    else:
                        S.add("dve", lambda e: e.tensor_tensor(out=Sst[sn_][:], in0=P5[:, pu * 128:(pu + 1) * 128], in1=Sst[so][:], op=ALU.add),
                              ["P5", "Sst%d" % so], ["Sst%d" % sn_])
                        S.add("dve", lambda e: e.tensor_scalar(out=Sst[sn_][:], in0=Sst[sn_][:], scalar1=plast, scalar2=None, op0=ALU.mult),
                              ["Sst%d" % sn_, K("PP%d" % cc)], ["Sst%d" % sn_])
                    sl = (gcn + 1) % NSB
                    S.add("act", lambda e: e.activation(out=Sbf[sl][:], in_=Sst[sn_][:], func=AF.Copy), ["Sst%d" % sn_], ["Sbf%d" % sl])

                def emit_o(cc):
                    gcn = gc0 + cc
                    s_, ci = divmod(cc, 2)
                    first = (T == 0 and cc == 0)
                    S.add("pe", lambda e: e.matmul(POb[:, cc * 64:(cc + 1) * 64], lhsT=vAB[p][:, s_, 0:128], rhs=At[:, s_, ci * 64:(ci + 1) * 64],
                                                   start=True, stop=first), [K("vAB%d" % s_), "At%d" % s_], [pok])
                    if not first:
                        sl = gcn % NSB
                        S.add("pe", lambda e: e.matmul(POb[:, cc * 64:(cc + 1) * 64], lhsT=Sbf[sl][:], rhs=Qt[p][:, cc * 64:(cc + 1) * 64],
                                                       start=False, stop=True), ["Sbf%d" % sl, K("Qt")], [pok])

                def emit_ret(s_):
                    rnn = rn0 + s_
                    first = (T == 0 and s_ == 0)
                    ro, rnw = rnn % 2, (rnn + 1) % 2
                    S.add("pe", lambda e: e.matmul(POb[:, 256 + s_ * 128:256 + (s_ + 1) * 128], lhsT=vAB[p][:, s_, 128:256], rhs=Sc[:, s_, :],
                                                   start=True, stop=first), [K("vAB%d" % s_), "Sc%d" % s_], [pok])
                    if not first:
                        sl = rnn % NRB
                        S.add("pe", lambda e: e.matmul(POb[:, 256 + s_ * 128:256 + (s_ + 1) * 128], lhsT=Rbf[sl][:], rhs=qkT[p][:, 0, s_ * 128:(s_ + 1) * 128],
                                                       start=False, stop=True), ["Rbf%d" % sl, K("qkT")], [pok])
                        S.add("dve", lambda e: e.scalar_tensor_tensor(out=Rst[rnw][0:64, :], in0=Rst[ro][0:64, :], scalar=g128[h],
                                                                      in1=P5[0:64, 256 + s_ * 128:256 + (s_ + 1) * 128],
                                                                      op0=ALU.mult, op1=ALU.add), ["Rst%d" % ro, "P5"], ["Rst%d" % rnw])
                    else:
                        S.add("dve", lambda e: e.tensor_copy(out=Rst[rnw][0:64, :], in_=P5[0:64, 256 + s_ * 128:256 + (s_ + 1) * 128]),
                              ["P5"], ["Rst%d" % rnw])
                    sl2 = (rnn + 1) % NRB
                    S.add("act", lambda e: e.activation(out=Rbf[sl2][0:64, :], in_=Rst[rnw][0:64, :], func=AF.Copy), ["Rst%d" % rnw], ["Rbf%d" % sl2])

                import os
                CUT = int(os.environ.get("KCUT", "99"))
                if CUT <= 0:
                    return
                emit_qkT()
                emit_U(0); emit_U(1)
                emit_state(0)
                emit_o(0)
                yield
                emit_ret_pre()
                emit_U(2)
                emit_state(1)
                yield
                emit_U(3)
                emit_o(1)
                emit_state(2)
                emit_ret(0)
                yield
                emit_ret(1)
                emit_o(2)
                emit_state(3)
                yield
                emit_o(3)
                yield

            def back2(g):
                h, T = divmod(g, n_tiles_run)
                p = g % 2
                hb_ = h % 2
                W = Wh[hb_]
                whk = ["wh%d_0" % hb_, "wh%d_1" % hb_]
                hs = slice(h, h + 1)
                tok0 = T * TW
                hTk = "hT%d" % T
                POb, pok = (P6, "P6") if p == 0 else (P7, "P7")

                def tap(name, ap, keys):
                    if name in dbg_d and h == 0 and T == dbg_T:
                        dma("sp", dbg_d[name], ap, reads=keys, writes=["dbg_" + name])

                def proj_gate(cblk, dst, func, key, **kw):
                    pb, pk = next_pb()
                    for j in range(8):
                        S.add("pe", lambda e, j=j, pb=pb: e.matmul(pb[:, 0:TW], lhsT=W[:, j, cblk * 128:(cblk + 1) * 128],
                                                                   rhs=hT[:, j, tok0:tok0 + TW], start=(j == 0), stop=(j == 7)),
                              whk + [hTk], [pk])
                    S.add("act", lambda e, pb=pb: e.activation(out=dst[:], in_=pb[:, 0:TW], func=func, **kw), [pk], [key])

                proj_gate(2, sz, AF.Silu, "sz")
                S.add("act", lambda e: e.activation(out=sqoA[:], in_=POb[:, 0:TW], func=AF.Square), [pok], ["sqoA"])
                S.add("act", lambda e: e.activation(out=sqoB[:], in_=POb[:, 256:256 + TW], func=AF.Square), [pok], ["sqoB"])
                yield
                ptf, ptk = next_pb()
                for bi, sqo in ((0, sqoA), (1, sqoB)):
                    for s_ in range(NSUB):
                        S.add("pe", lambda e, sqo=sqo, s_=s_, bi=bi, ptf=ptf: e.matmul(ptf[:, bi * 2 + s_:bi * 2 + s_ + 1],
                                                                                      lhsT=sqo[:, s_ * 128:(s_ + 1) * 128], rhs=onescol[:, 0:1],
                                                                                      start=True, stop=True), ["sqoA", "sqoB"], [ptk])
                S.add("dve", lambda e, ptf=ptf: e.tensor_scalar(out=rsv[:], in0=ptf[:, 0:4], scalar1=1.0 / 128.0, scalar2=EPS,
                                                                op0=ALU.mult, op1=ALU.add), [ptk], ["rsv"])
                S.add("pool", lambda e: e.tensor_tensor(out=rsv[:], in0=rsv[:], in1=mh4[:], op=ALU.pow), ["rsv"], ["rsv"])
                proj_gate(4, srz, AF.Silu, "srz")
                yield
                S.add("dve", lambda e: e.scalar_tensor_tensor(out=uA[:], in0=POb[:, 0:TW], scalar=gA[:, hs], in1=sz[:],
                                                              op0=ALU.mult, op1=ALU.mult), [pok, "sz"], ["uA"])
                S.add("dve", lambda e: e.scalar_tensor_tensor(out=uB[:], in0=POb[:, 256:256 + TW], scalar=gB[:, hs], in1=srz[:],
                                                              op0=ALU.mult, op1=ALU.mult), [pok, "srz"], ["uB"])
                for k_ in range(2 * NSUB):
                    S.add("act", lambda e, k_=k_: e.activation(out=dgh[:, k_, :], in_=identf[:], func=AF.Identity, scale=rsv[:, k_:k_ + 1]),
                          ["rsv"], ["dgh%d" % k_])
                    S.add("dve", lambda e, k_=k_: e.scalar_tensor_tensor(out=dgl[:, k_, :], in0=identf[:], scalar=rsv[:, k_:k_ + 1], in1=dgh[:, k_, :],
                                                                         op0=ALU.mult, op1=ALU.subtract), ["rsv", "dgh%d" % k_], ["dgl%d" % k_])
                proj_gate(3, tga, AF.Tanh, "tga", scale=0.5)
                yield
                pbc, pbk = next_pb()
                for k_ in range(2 * NSUB):
                    S.add("pe", lambda e, k_=k_, pbc=pbc: e.matmul(pbc[:, k_ * 128:(k_ + 1) * 128], lhsT=onesb[:], rhs=dgh[:, k_, :],
                                                                   start=True, stop=False), ["dgh%d" % k_], [pbk])
                    S.add("pe", lambda e, k_=k_, pbc=pbc: e.matmul(pbc[:, k_ * 128:(k_ + 1) * 128], lhsT=onesb[:], rhs=dgl[:, k_, :],
                                                                   start=False, stop=True), ["dgl%d" % k_], [pbk])
                proj_gate(5, tgb, AF.Tanh, "tgb", scale=0.5)
                yield
                S.add("dve", lambda e, pbc=pbc: e.tensor_tensor(out=uA[:], in0=uA[:], in1=pbc[:, 0:TW], op=ALU.mult), ["uA", pbk], ["uA"])
                S.add("dve", lambda e, pbc=pbc: e.tensor_tensor(out=uB[:], in0=uB[:], in1=pbc[:, 256:256 + TW], op=ALU.mult), ["uB", pbk], ["uB"])
                yield
                S.add("dve", lambda e: e.scalar_tensor_tensor(out=uA[:], in0=tga[:], scalar=1.0, in1=uA[:], op0=ALU.add, op1=ALU.mult),
                      ["uA", "tga"], ["uA"])
                S.add("dve", lambda e: e.scalar_tensor_tensor(out=uB[:], in0=tgb[:], scalar=1.0, in1=uB[:], op0=ALU.add, op1=ALU.mult),
                      ["uB", "tgb"], ["uB"])
                tap("uA", uA[:], ["uA"]); tap("uB", uB[:], ["uB"])
                S.add("pool", lambda e: e.tensor_tensor(out=mT[:, h, tok0:tok0 + TW], in0=uA[:], in1=uB[:], op=ALU.add), ["uA", "uB"], ["mT%d" % T])
                yield

            from itertools import zip_longest
            import os
            G = n_heads_run * n_tiles_run
            for _ in front(0):
                pass
            TLOAD = min(2, n_tiles_run - 1)
            for i in range(G + 1):
                if i < G:
                    h_i, T_i = divmod(i, n_tiles_run)
                    if T_i == TLOAD and h_i + 1 < n_heads_run:
                        load_w(h_i + 1)
                gens = []
                if i + 1 < G:
                    gens.append(front(i + 1))
                if i < G:
                    gens.append(back(i))
                if i >= 1:
                    gens.append(back2(i - 1))
                for _ in zip_longest(*gens):
                    pass
            if "mT" in dbg_d:
                for j in range(8):
                    dma("sp", dbg_d["mT"][j * 128:(j + 1) * 128, :], mT[:, j, :],
                        reads=["mT%d" % t for t in range(NT)], writes=["dbg_mT"])
            S.fence()

        with ExitStack() as es3:
            sb3 = lambda name, shape, dtype=F32: es3.enter_context(nc.sbuf_tensor("s3_" + name, shape, dtype))
            Wo = sb3("wo", [128, 8, D], BF16)
            fg_b = sb3("fg_b", [128, D])
            xt3 = [sb3("xt%d" % i, [128, D]) for i in range(4)]
            xn = [sb3("xn%d" % i, [128, D]) for i in range(4)]
            ot = [sb3("ot%d" % i, [128, D]) for i in range(4)]
            junk3 = sb3("junk", [128, D], BF16)
            ss3 = sb3("ss3", [128, 4]); rs3 = sb3("rs3", [128, 4]); mh3 = sb3("mh3", [128, 1])
            for half in range(2):
                dma("pool", Wo[:, half * 4:(half + 1) * 4, :], wo_d[half * 512:(half + 1) * 512, :].rearrange("(j p) c -> p j c", p=128),
                    writes=["wo%d" % half])
            dma("sp", fg_b[:], bcast_rows(fg_d, D), writes=["fg_b"])
            S.add("pool", lambda e: e.memset(mh3[:], -0.5), [], ["mh3"])
            for ti in range(L // 128):
                b = ti % 4
                pbi = ti % 2
                xk = "x3_%d" % b
                dma("sp", xt3[b][:], x_d[ti * 128:(ti + 1) * 128, :], writes=[xk])
                PB3 = [PB[0], PB[1], PB[2], PTf]
                for g in range(2):
                    pb, pk = PB3[pbi * 2 + g], "pb3_%d" % (pbi * 2 + g)
                    for j in range(8):
                        S.add("pe", lambda e, j=j, g=g, pb=pb, ti=ti: e.matmul(pb[:], lhsT=mT[:, j, ti * 128:(ti + 1) * 128],
                                                                              rhs=Wo[:, j, g * 512:(g + 1) * 512], start=(j == 0), stop=(j == 7)),
                              ["wo0", "wo1", "mT"], [pk])
                    S.add("dve", lambda e, g=g, pb=pb, b=b: e.tensor_tensor(out=xn[b][:, g * 512:(g + 1) * 512], in0=pb[:],
                                                                            in1=gate_b[:, g * 512:(g + 1) * 512], op=ALU.mult),
                          [pk, "gate_b"], ["xn%d_%d" % (b, g)])
                xnk = ["xn%d_0" % b, "xn%d_1" % b]
                S.add("pool", lambda e, b=b: e.tensor_tensor(out=xn[b][:], in0=xn[b][:], in1=xt3[b][:], op=ALU.add), xnk + [xk], xnk)
                S.add("act", lambda e, b=b: e.activation(out=junk3[:], in_=xn[b][:], func=AF.Square, accum_out=ss3[:, b:b + 1]),
                      xnk, ["junk3", "ss3_%d" % b])
                S.add("dve", lambda e, b=b: e.tensor_scalar(out=ss3[:, b:b + 1], in0=ss3[:, b:b + 1], scalar1=1.0 / D, scalar2=EPS,
                                                            op0=ALU.mult, op1=ALU.add), ["ss3_%d" % b], ["ss3_%d" % b])
                S.add("pool", lambda e, b=b: e.tensor_tensor(out=rs3[:, b:b + 1], in0=ss3[:, b:b + 1], in1=mh3[:], op=ALU.pow),
                      ["ss3_%d" % b, "mh3"], ["rs3_%d" % b])
                S.add("dve", lambda e, b=b: e.scalar_tensor_tensor(out=ot[b][:], in0=xn[b][:], scalar=rs3[:, b:b + 1], in1=fg_b[:],
                                                                   op0=ALU.mult, op1=ALU.mult), xnk + ["rs3_%d" % b, "fg_b"], ["ot%d" % b])
                dma("sp", out_d[ti * 128:(ti + 1) * 128, :], ot[b][:], reads=["ot%d" % b], writes=["out"])

        S.fence()
        for sm in list(sems.values()) + dma_sems + sw_sems:
            nc.gpsimd.sem_clear(sm)
        nc.all_engine_barrier()
        with nc.Block() as block:
            S.emit(nc, block, sems, dma_sems, sw_sems)
        nc.all_engine_barrier()
    return nc


def _perm_cols():
    offs = np.cumsum([0, 1024, 1024, 1024, 1024, 512, 512, 1024, 1024, 1024, 1024])
    o_hq, o_hf, o_hi, o_hz, o_rq, o_rk, o_rv, o_rz, o_ga, o_gb = offs[:10]
    cols = []
    for h in range(NH):
        c = []
        for o in (o_hq, o_hf, o_hz, o_ga, o_rz, o_gb, o_hi, o_rv):
            c += list(range(o + h * 128, o + (h + 1) * 128))
        c += list(range(o_rq + h * 64, o_rq + (h + 1) * 64))
        c += list(range(o_rk + h * 64, o_rk + (h + 1) * 64))
        cols.append(c)
    return np.array(cols)


def make_in_maps(x, c, norm_g, w_ada, b_ada, w_in, hg_lb_logits, hg_norm_g, ret_norm_g, w_out, final_g):
    f = lambda a: np.ascontiguousarray(np.asarray(a, dtype=np.float32))
    x, c = f(x), f(c)
    cols = _perm_cols()
    w_in0 = f(w_in)[0]
    wp = np.ascontiguousarray(np.stack([w_in0[:, cols[h]] for h in range(NH)], 0))
    lb = f(hg_lb_logits)
    lbl = np.ascontiguousarray(np.concatenate([lb[0].reshape(8, 128).T, lb[1].reshape(8, 128).T], 1))
    shared = {
        "w_ada": f(w_ada)[0], "b_ada": f(b_ada)[0].reshape(1, -1), "norm_g": f(norm_g)[0].reshape(1, -1),
        "final_g": f(final_g).reshape(1, -1), "w_in_p": wp, "lbl": lbl,
        "hg_g": np.ascontiguousarray(f(hg_norm_g)[0].reshape(8, 128).T),
        "ret_g": np.ascontiguousarray(f(ret_norm_g)[0].reshape(8, 128).T),
        "w_out": f(w_out)[0],
        "cos_t": _TAB["cos"].reshape(128, -1), "sin_t": _TAB["sin"].reshape(128, -1),
        "xi_t": _TAB["xi"], "zeta_t": _TAB["zeta"], "mask_r": _TAB["mask_r"], "mask_h": _TAB["mask_h"],
        "ident": _TAB["ident"],
    }
    maps = []
    for b in range(8):
        m = dict(shared)
        m["x"] = x[b]
        m["c_col"] = np.ascontiguousarray(c[b].reshape(8, 128).T)
        maps.append(m)
    return maps


def kernel(x, c, norm_g, w_ada, b_ada, w_in, hg_lb_logits, hg_norm_g, ret_norm_g, w_out, final_g):
    maps = make_in_maps(x, c, norm_g, w_ada, b_ada, w_in, hg_lb_logits, hg_norm_g, ret_norm_g, w_out, final_g)
    nc = build_program()
    res = run_bass_kernel_spmd(nc, maps, core_ids=list(range(8)))
    return np.stack([np.asarray(r["out"], dtype=np.float32) for r in res.results], 0)
```
